# Optimizing a Trainium2 kernel written in Bass

```python
import jax, jax.numpy as jnp
from jax import lax
import numpy as np

D_MODEL = 1024
BATCH = 16
SEQ = 256
DEPTH = 4
DEC_BATCH = 2
DEC_SEQ = 4096
PAST_LEN = 256

GRID_W = 64
N_MIXERS = 2
N_POOL_LAYERS = (DEPTH + 1) // 2
N_RET_LAYERS = DEPTH // 2
POOL_WINDOWS = (2, 4, 8, 16)
POOL_GROUPS = 4
POOL_GC = D_MODEL // POOL_GROUPS
RET_HEADS = 4
RET_DK = D_MODEL // RET_HEADS
RET_DV = 2 * RET_DK
RET_HK = RET_HEADS * RET_DK
RET_HV = RET_HEADS * RET_DV
RET_IN = 2 * RET_HK + 2 * RET_HV
RET_CHUNK = 128
D_FF = 4 * D_MODEL
ROPE_BASE = 10000.0
NORM_EPS = 1e-6
GN_EPS = 1e-5
N_MOD = 6

kernel_name = "hybrid_pool_retention_diffusion_step"


def rmsnorm(x, w):
    xf = x.astype(jnp.float32)
    y = xf * lax.rsqrt(jnp.mean(jnp.square(xf), axis=-1, keepdims=True) + NORM_EPS)
    return (y * w.astype(jnp.float32)).astype(x.dtype)


def pool_mean_1d(x, w, axis):
    L = x.shape[axis]
    cs = jnp.cumsum(x.astype(jnp.float32), axis=axis)
    cs = jnp.pad(cs, [(1, 0) if a == axis else (0, 0) for a in range(x.ndim)])
    t = jnp.arange(L)
    lo = jnp.clip(t - w // 2, 0, L)
    hi = jnp.clip(t - w // 2 + w, 0, L)
    s = jnp.take(cs, hi, axis=axis) - jnp.take(cs, lo, axis=axis)
    cnt_shape = [L if a == axis else 1 for a in range(x.ndim)]
    cnt = (hi - lo).astype(jnp.float32).reshape(cnt_shape)
    return s / cnt


def pool_mixer(h, pw, pb, ps, grid):
    B, L, _ = h.shape
    outs = []
    for g, w in enumerate(POOL_WINDOWS):
        xg = h[..., g * POOL_GC:(g + 1) * POOL_GC]
        if grid:
            rows = L // GRID_W
            xr = xg.reshape(B, rows, GRID_W, POOL_GC)
            m = pool_mean_1d(pool_mean_1d(xr, w, 1), w, 2).reshape(B, L, POOL_GC)
        else:
            m = pool_mean_1d(xg, w, 1)
        d = (m - xg.astype(jnp.float32)).astype(h.dtype)
        outs.append(d @ pw[g] + pb[g])
    return jnp.concatenate(outs, axis=-1) * ps


def rope_2d(x):
    L = x.shape[2]
    t = jnp.arange(L)
    row = (t // GRID_W).astype(jnp.float32)
    col = (t % GRID_W).astype(jnp.float32)
    half = RET_DK // 2
    quarter = half // 2
    freqs = ROPE_BASE ** (-jnp.arange(quarter, dtype=jnp.float32) / quarter)

    def rot(xa, pos):
        ang = pos[:, None] * freqs[None, :]
        cos = jnp.cos(ang).astype(x.dtype)
        sin = jnp.sin(ang).astype(x.dtype)
        x1, x2 = xa[..., :quarter], xa[..., quarter:]
        return jnp.concatenate([x1 * cos - x2 * sin, x2 * cos + x1 * sin], axis=-1)

    return jnp.concatenate([rot(x[..., :half], row), rot(x[..., half:], col)], axis=-1)


def retention_chunked(q, k, v, log_gamma, s0, strict):
    b, h, L, _ = q.shape
    dv = v.shape[-1]
    n = L // RET_CHUNK

    def to_chunks(a):
        return jnp.moveaxis(a.astype(jnp.float32).reshape(b, h, n, RET_CHUNK, a.shape[-1]), 2, 0)

    qc, kc, vc = to_chunks(q), to_chunks(k), to_chunks(v)
    idx = jnp.arange(RET_CHUNK, dtype=jnp.float32)
    diff = idx[:, None] - idx[None, :]
    mask = (diff > 0) if strict else (diff >= 0)
    lg = log_gamma.astype(jnp.float32)
    dmat = jnp.where(mask[None], jnp.exp(lg[:, None, None] * jnp.maximum(diff, 0.0)[None]), 0.0)
    xi = jnp.exp(lg[:, None] * (idx + 1.0)[None])[:, :, None]
    zeta = jnp.exp(lg[:, None] * (RET_CHUNK - 1.0 - idx)[None])[:, :, None]
    g_chunk = jnp.exp(lg * RET_CHUNK)[:, None, None]

    def step(S, inp):
        qb, kb, vb = inp
        scores = jnp.einsum('bhid,bhjd->bhij', qb, kb) * dmat
        o = jnp.einsum('bhij,bhje->bhie', scores, vb) + jnp.einsum('bhid,bhde->bhie', qb * xi, S)
        S = g_chunk * S + jnp.einsum('bhjd,bhje->bhde', kb * zeta, vb)
        return S, o

    s_fin, oc = lax.scan(step, s0.astype(jnp.float32), (qc, kc, vc))
    o = jnp.moveaxis(oc, 0, 2).reshape(b, h, L, dv)
    return o, s_fin


def retention_mixer(h, w_in, decay_exp, gn_w, w_out, s0_f, s0_b, grid):
    B, L, _ = h.shape
    proj = h @ w_in
    q, k, v, g = jnp.split(proj, [RET_HK, 2 * RET_HK, 2 * RET_HK + RET_HV], axis=-1)

    def heads(a, d):
        return a.reshape(B, L, RET_HEADS, d).transpose(0, 2, 1, 3)

    q = heads(q, RET_DK)
    k = heads(k, RET_DK) * (RET_DK ** -0.5)
    v = heads(v, RET_DV)
    if grid:
        q = rope_2d(q)
        k = rope_2d(k)
    lg = jnp.log1p(-jnp.exp2(-decay_exp.astype(jnp.float32)))
    o_f, s_f = retention_chunked(q, k, v, lg[0], s0_f, False)
    o_b, s_b = retention_chunked(q[:, :, ::-1], k[:, :, ::-1], v[:, :, ::-1], lg[1], s0_b, True)
    o = o_f + o_b[:, :, ::-1]
    mu = jnp.mean(o, axis=-1, keepdims=True)
    var = jnp.mean(jnp.square(o - mu), axis=-1, keepdims=True)
    o = (o - mu) * lax.rsqrt(var + GN_EPS) * gn_w.astype(jnp.float32)[:, None, :]
    o = o.transpose(0, 2, 1, 3).reshape(B, L, RET_HV).astype(h.dtype)
    out = (jax.nn.silu(g) * o) @ w_out
    return out, s_f, s_b


def sq_relu_mlp(h, w1, w2):
    return jnp.square(jax.nn.relu(h @ w1)) @ w2


def trunk(x, cond, state_ret, grid, w_ada, b_ada, norm_mix_w, norm_mlp_w, pool_w, pool_b,
          pool_scale, ret_w_in, ret_decay, ret_gn_w, ret_w_out, mlp_w1, mlp_w2, final_norm_w):
    B = x.shape[0]
    new_states = []
    for i in range(DEPTH):
        j = i // N_MIXERS
        mod = (jax.nn.silu(cond) @ w_ada[i] + b_ada[i])[:, None, :]
        sh_a, sc_a, g_a, sh_m, sc_m, g_m = jnp.split(mod, N_MOD, axis=-1)
        h = rmsnorm(x, norm_mix_w[i]) * (1 + sc_a) + sh_a
        if i % N_MIXERS == 0:
            mix = pool_mixer(h, pool_w[j], pool_b[j], pool_scale[j], grid)
        else:
            if state_ret is None:
                s0 = jnp.zeros((B, RET_HEADS, RET_DK, RET_DV), jnp.float32)
                s0_f, s0_b = s0, s0
            else:
                s0_f, s0_b = state_ret[:, j, 0], state_ret[:, j, 1]
            mix, s_f, s_b = retention_mixer(h, ret_w_in[j], ret_decay[j], ret_gn_w[j], ret_w_out[j],
                                            s0_f, s0_b, grid)
            if state_ret is None:
                new_states.append(jnp.stack([s_f, s_b], axis=1))
        x = x + g_a * mix
        h = rmsnorm(x, norm_mlp_w[i]) * (1 + sc_m) + sh_m
        x = x + g_m * sq_relu_mlp(h, mlp_w1[i], mlp_w2[i])
    y = rmsnorm(x, final_norm_w)
    if state_ret is None:
        return y, jnp.stack(new_states, axis=1)
    return y, None


def setup_inputs(seed: int = 0) -> dict:
    key = jax.random.key(seed)
    ks = jax.random.split(key, 20)
    f32 = jnp.float32
    nrm = lambda k, s: jax.random.normal(k, s, f32)
    return {
        "x_prompt": nrm(ks[0], (BATCH, SEQ, D_MODEL)),
        "x_sample": nrm(ks[1], (DEC_BATCH, DEC_SEQ, D_MODEL)),
        "state_ret": 0.5 * nrm(ks[2], (DEC_BATCH, N_RET_LAYERS, 2, RET_HEADS, RET_DK, RET_DV)),
        "c": nrm(ks[3], (DEC_BATCH, D_MODEL)),
        "c_ctx": nrm(ks[4], (D_MODEL,)),
        "w_ada": 0.5 * D_MODEL ** -0.5 * nrm(ks[5], (DEPTH, D_MODEL, N_MOD * D_MODEL)),
        "b_ada": 0.01 * nrm(ks[6], (DEPTH, N_MOD * D_MODEL)),
        "norm_mix_w": 1.0 + 0.05 * nrm(ks[7], (DEPTH, D_MODEL)),
        "norm_mlp_w": 1.0 + 0.05 * nrm(ks[8], (DEPTH, D_MODEL)),
        "pool_w": POOL_GC ** -0.5 * nrm(ks[9], (N_POOL_LAYERS, POOL_GROUPS, POOL_GC, POOL_GC)),
        "pool_b": 0.01 * nrm(ks[10], (N_POOL_LAYERS, POOL_GROUPS, POOL_GC)),
        "pool_scale": 1.0 + 0.1 * nrm(ks[11], (N_POOL_LAYERS, D_MODEL)),
        "ret_w_in": D_MODEL ** -0.5 * nrm(ks[12], (N_RET_LAYERS, D_MODEL, RET_IN)),
        "ret_decay": 5.0 + jnp.arange(RET_HEADS, dtype=f32)[None, None, :]
                     + 0.1 * nrm(ks[13], (N_RET_LAYERS, 2, RET_HEADS)),
        "ret_gn_w": 1.0 + 0.05 * nrm(ks[14], (N_RET_LAYERS, RET_HEADS, RET_DV)),
        "ret_w_out": RET_HV ** -0.5 * nrm(ks[15], (N_RET_LAYERS, RET_HV, D_MODEL)),
        "mlp_w1": D_MODEL ** -0.5 * nrm(ks[16], (DEPTH, D_MODEL, D_FF)),
        "mlp_w2": D_FF ** -0.5 * nrm(ks[17], (DEPTH, D_FF, D_MODEL)),
        "final_norm_w": 1.0 + 0.05 * nrm(ks[18], (D_MODEL,)),
    }


def reference(x_prompt, x_sample, state_ret, c, c_ctx, w_ada, b_ada, norm_mix_w, norm_mlp_w,
              pool_w, pool_b, pool_scale, ret_w_in, ret_decay, ret_gn_w, ret_w_out, mlp_w1,
              mlp_w2, final_norm_w):
    y_prompt, new_state_ret = trunk(x_prompt, c_ctx[None, :], None, False, w_ada, b_ada,
                                    norm_mix_w, norm_mlp_w, pool_w, pool_b, pool_scale, ret_w_in,
                                    ret_decay, ret_gn_w, ret_w_out, mlp_w1, mlp_w2, final_norm_w)
    y_sample, _ = trunk(x_sample, c, state_ret, True, w_ada, b_ada, norm_mix_w, norm_mlp_w,
                        pool_w, pool_b, pool_scale, ret_w_in, ret_decay, ret_gn_w, ret_w_out,
                        mlp_w1, mlp_w2, final_norm_w)
    return (y_prompt, y_sample, new_state_ret)
```

```python
import os
import sys
import numpy as np
from contextlib import ExitStack
import concourse.bass as bass
import concourse.mybir as mybir
from concourse.bass_utils import run_bass_kernel_spmd

F32 = mybir.dt.float32
BF16 = mybir.dt.bfloat16
AF = mybir.ActivationFunctionType
ALU = mybir.AluOpType
AX = mybir.AxisListType

P = 128
TS = 1024
TP = 512
TT = TS + TP
DEPTH = 4
NORM_EPS = 1e-6
GN_EPS = 1e-5
SEM_M = 1024
NSLOT = 5
POOL_W = (2, 4, 8, 16)


def _cst_layout():
    o = {}
    n = 0
    for name, w in [("ident", 128), ("dpos", 128), ("dneg", 128), ("mf", 128), ("mb", 128),
                    ("xif", 128), ("xib", 128), ("colA", 1), ("colB", 1), ("colsF", 8), ("colsB", 8)]:
        o[name] = (n, w)
        n += w
    return o, n


CST_L, CST_N = _cst_layout()


def _make_cst():
    c = np.zeros((P, CST_N), np.float32)
    p = np.arange(P, dtype=np.float32)[:, None]
    i = np.arange(P, dtype=np.float32)[None, :]

    def put(name, v):
        a, w = CST_L[name]
        c[:, a:a + w] = v

    put("ident", (p == i).astype(np.float32))
    put("dpos", np.maximum(i - p, 0.0))
    put("dneg", np.maximum(p - i, 0.0))
    put("mf", (i >= p).astype(np.float32) * 0.0625)
    put("mb", (p > i).astype(np.float32) * 0.0625)
    put("xif", np.broadcast_to(i + 1.0, (P, P)))
    put("xib", np.broadcast_to(128.0 - i, (P, P)))
    put("colA", 127.0 - p)
    put("colB", p)
    cc = np.arange(8, dtype=np.float32)[None, :]
    put("colsF", 1023.0 - 128.0 * cc - p)
    put("colsB", 128.0 * cc + p)
    return c


def _pcv_layout():
    o = {}
    n = 0
    for name, w in [("bada", 4 * 48), ("nmix", 32), ("nmlp", 32), ("poolb", 16), ("pools", 16),
                    ("fnw", 8), ("cond", 16), ("decay", 16), ("xw", 18), ("pm", 8),
                    ("icr", 64), ("icc", 256), ("icp", 1024)]:
        o[name] = (n, w)
        n += w
    return o, n


PCV_L, PCV_N = _pcv_layout()


def _inv_cnt(L, w):
    t = np.arange(L)
    lo = np.clip(t - w // 2, 0, L)
    hi = np.clip(t - w // 2 + w, 0, L)
    return (1.0 / (hi - lo).astype(np.float32)).astype(np.float32)


def _make_pcv(core, inp):
    b, q = core // 4, core % 4
    v = np.zeros((P, PCV_N), np.float32)

    def put(name, arr):
        a, w = PCV_L[name]
        arr = np.asarray(arr, np.float32)
        assert arr.shape[-1] == w, (name, arr.shape, w)
        v[:, a:a + w] = arr

    def fm(x):
        x = np.asarray(x, np.float32).reshape(-1, P)
        return x.T

    put("bada", np.concatenate([fm(inp["b_ada"][i]) for i in range(4)], axis=1))
    put("nmix", np.concatenate([fm(inp["norm_mix_w"][i]) for i in range(4)], axis=1))
    put("nmlp", np.concatenate([fm(inp["norm_mlp_w"][i]) for i in range(4)], axis=1))
    put("poolb", np.concatenate([fm(inp["pool_b"][j].reshape(-1)) for j in range(2)], axis=1))
    put("pools", np.concatenate([fm(inp["pool_scale"][j]) for j in range(2)], axis=1))
    put("fnw", fm(inp["final_norm_w"]))
    cs = fm(inp["c"][b])
    cc = fm(inp["c_ctx"])
    cond = np.zeros((P, 16), np.float32)
    cond[:, 0::2] = cs
    cond[:, 1::2] = cc
    put("cond", cond)
    put("decay", np.broadcast_to(np.asarray(inp["ret_decay"], np.float32).reshape(1, 16), (P, 16)))
    xw = np.zeros(18, np.float32)
    for r in range(4):
        if r < q:
            xw[r] = 1024.0 * (q - 1 - r)
            xw[4 + r] = 1.0
        if r > q:
            xw[8 + r] = 1024.0 * (r - q - 1)
            xw[12 + r] = 1.0
    xw[16] = 1024.0 * q
    xw[17] = 1024.0 * (3 - q)
    put("xw", np.broadcast_to(xw[None, :], (P, 18)))
    pm = np.zeros(8, np.float32)
    if q > 0:
        pm[q - 1] = 1.0
    if q < 3:
        pm[4 + q + 1] = 1.0
    put("pm", np.broadcast_to(pm[None, :], (P, 8)))
    icr = np.concatenate([_inv_cnt(64, w)[q * 16:(q + 1) * 16] for w in POOL_W])
    put("icr", np.broadcast_to(icr[None, :], (P, 64)))
    icc = np.concatenate([_inv_cnt(64, w) for w in POOL_W])
    put("icc", np.broadcast_to(icc[None, :], (P, 256)))
    icp = np.concatenate([_inv_cnt(256, w) for w in POOL_W])
    put("icp", np.broadcast_to(icp[None, :], (P, 1024)))
    return v


def _make_rope(core):
    q = core % 4
    t = np.arange(TS) + q * TS
    row = (t // 64).astype(np.float32)
    col = (t % 64).astype(np.float32)
    quarter = 64
    freqs = (np.float32(10000.0) ** (-np.arange(quarter, dtype=np.float32) / np.float32(quarter))).astype(np.float32)
    ar = (row[:, None] * freqs[None, :]).astype(np.float32)
    ac = (col[:, None] * freqs[None, :]).astype(np.float32)
    cos = np.concatenate([np.cos(ar), np.cos(ac)], axis=1).astype(np.float32)
    sin = np.concatenate([np.sin(ar), np.sin(ac)], axis=1).astype(np.float32)
    tab = np.stack([cos, sin, -sin], axis=1)
    tab = tab.reshape(8, P, 3, 128).transpose(1, 0, 2, 3).reshape(P, 8, 384)
    return np.ascontiguousarray(tab)


class Eng:
    def __init__(self, name):
        self.name = name
        self.count = 0
        self.seen = {}
        self.prog = []
        self.meta = []


class DSem:
    _serial = 0

    def __init__(self, sem):
        self.sem = sem
        self.n = 0
        DSem._serial += 1
        self.uid = "dsem%d" % DSem._serial


class Buf:
    __slots__ = ("name", "w", "r", "excl")

    def __init__(self, name="", excl=False):
        self.name = name
        self.w = None
        self.r = {}
        self.excl = excl

    def add_reader(self, tok):
        k = (tok[0], tok[1].uid if tok[0] == "d" else tok[1].name)
        old = self.r.get(k)
        if old is None or old[2] < tok[2]:
            self.r[k] = tok


class Ctx:
    def __init__(self, nc, es):
        self.nc = nc
        self.es = es
        self.pe = Eng("pe")
        self.act = Eng("act")
        self.dve = Eng("dve")
        self.pool = Eng("pool")
        self.sp = Eng("sp")
        self.engs = [self.pe, self.act, self.dve, self.pool, self.sp]
        self.esems = {}
        self.free_sems = []
        self.pe_open = False
        self.mem_off = 0
        self.live_dsems = {}

    def prealloc_sems(self, n):
        for i in range(n):
            self.free_sems.append(self.es.enter_context(self.nc.semaphore(f"s{i}")))

    def new_sem(self):
        return self.free_sems.pop()

    def esem(self, eng, ep):
        k = (eng.name, ep)
        if k not in self.esems:
            self.esems[k] = self.new_sem()
        return self.esems[k]

    def dsem(self):
        return DSem(self.new_sem())

    def _waits_for(self, eng, tok, out):
        kind, src, n = tok
        if kind == "e":
            if src is eng and eng.name == "pe":
                return
            key = src.name
            if eng.seen.get(key, 0) >= n:
                return
            eng.seen[key] = n
            ep = (n - 1) // SEM_M
            out.append((self.esem(src, ep), n - ep * SEM_M))
        else:
            key = src.uid
            if eng.seen.get(key, 0) >= n:
                return
            eng.seen[key] = n
            out.append((src.sem, n))

    def _collect(self, eng, reads, writes):
        ws = []
        for b in reads:
            if b.w is not None:
                self._waits_for(eng, b.w, ws)
        for b in writes:
            if b.w is not None:
                self._waits_for(eng, b.w, ws)
            for t in b.r.values():
                self._waits_for(eng, t, ws)
        return ws

    def _finish(self, tok, reads, writes):
        for b in reads:
            b.add_reader(tok)
        for b in writes:
            b.w = tok
            b.r = {}

    def op(self, eng, fn, reads=(), writes=(), inc=True):
        if any(b.excl for b in reads):
            writes = tuple(writes) + tuple(b for b in reads if b.excl and b not in writes)
            reads = tuple(b for b in reads if not b.excl)
        if eng.name != "pe":
            assert not self.pe_open, "non-PE op inside open PE group"
        ws = self._collect(eng, reads, writes)
        if inc:
            eng.count += 1
            ep = (eng.count - 1) // SEM_M
            sem = self.esem(eng, ep)
            tok = ("e", eng, eng.count)
            if eng.name == "pe":
                self.pe_open = False
        else:
            sem = None
            tok = ("e", eng, eng.count + 1)
            self.pe_open = True

        def run(h, ws=ws, fn=fn, sem=sem):
            for s, v in ws:
                h.wait_ge(s, v)
            ins = fn(h)
            if sem is not None:
                ins.then_inc(sem, 1)

        eng.prog.append(run)
        eng.meta.append((ws, [(sem, 1)] if sem is not None else [], sys._getframe(1).f_lineno))
        self._finish(tok, reads, writes)
        return tok

    def dma(self, q, out, in_, dsem, reads=(), writes=(), inc=16, kind="dma", **kw):
        assert not self.pe_open
        ws = self._collect(q, reads, writes)
        dsem.n += inc
        tok = ("d", dsem, dsem.n)

        def run(h, ws=ws, out=out, in_=in_, kw=kw, sem=dsem.sem, inc=inc):
            for s, v in ws:
                h.wait_ge(s, v)
            h.dma_start(out=out, in_=in_, **kw).then_inc(sem, inc)

        q.prog.append(run)
        if q is self.sp:
            self.live_dsems[dsem.uid] = dsem
        q.meta.append((ws, [(dsem.sem, inc)], sys._getframe(1).f_lineno))
        self._finish(tok, reads, writes)
        return tok

    def barrier(self):
        assert not self.pe_open
        toks = [("e", e, e.count) for e in self.engs if e.count > 0]
        dtoks = [("d", ds, ds.n) for ds in self.live_dsems.values()]
        self.live_dsems = {}
        for e in (self.pe, self.act, self.dve):
            self.wait_all(e, [t for t in toks if t[1] is not e or e.name != "pe"] + dtoks)
        self.wait_all(self.sp, toks)

    def dry_run(self):
        vals = {}
        pcs = {e.name: 0 for e in self.engs}
        progress = True
        while progress:
            progress = False
            for e in self.engs:
                while pcs[e.name] < len(e.meta):
                    ws, incs, line = e.meta[pcs[e.name]]
                    if all(vals.get(id(s_), 0) >= v for s_, v in ws):
                        for s_, a in incs:
                            vals[id(s_)] = vals.get(id(s_), 0) + a
                        pcs[e.name] += 1
                        progress = True
                    else:
                        break
        stuck = [(e.name, pcs[e.name], len(e.meta)) for e in self.engs if pcs[e.name] < len(e.meta)]
        if stuck:
            msg = []
            for e in self.engs:
                if pcs[e.name] < len(e.meta):
                    ws, incs, line = e.meta[pcs[e.name]]
                    msg.append(f"{e.name} pc={pcs[e.name]}/{len(e.meta)} line={line} waits=" +
                               str([(self._semname(s_), v, vals.get(id(s_), 0)) for s_, v in ws]))
            raise RuntimeError("DEADLOCK in dry run:\n" + "\n".join(msg))
        return {e.name: len(e.meta) for e in self.engs}

    def _semname(self, sem):
        for k, v in self.esems.items():
            if v is sem:
                return str(k)
        return "dma/" + str(id(sem) % 10000)

    def wait_all(self, eng, toks):
        ws = []
        for t in toks:
            self._waits_for(eng, t, ws)

        def run(h, ws=ws):
            for s, v in ws:
                h.wait_ge(s, v)

        eng.prog.append(run)
        eng.meta.append((ws, [], sys._getframe(1).f_lineno))


def build_program(nsub=2 * DEPTH):
    nc = bass.Bass("TRN2", target_bir_lowering=False)
    es = ExitStack()
    K = Ctx(nc, es)
    pe, act, dve, pool, sp = K.pe, K.act, K.dve, K.pool, K.sp

    def dram_in(name, shape):
        return nc.dram_tensor(name, list(shape), F32, kind="ExternalInput").ap()

    def dram_out(name, shape):
        return nc.dram_tensor(name, list(shape), F32, kind="ExternalOutput").ap()

    xs_d = dram_in("xs", [TS, 1024])
    xp_d = dram_in("xp", [TP, 1024])
    s0_d = dram_in("s0", [2, 2, 4, 256, 512])
    cst_d = dram_in("cst", [P, CST_N])
    pcv_d = dram_in("pcv", [P, PCV_N])
    rope_d = dram_in("rope", [P, 8, 384])
    gnw_d = dram_in("gnw", [P, 2, 4, 512])
    wada_d = dram_in("w_ada", [4, 1024, 6144])
    poolw_d = dram_in("pool_w", [2, 4, 256, 256])
    win_d = dram_in("ret_w_in", [2, 1024, 6144])
    wout_d = dram_in("ret_w_out", [2, 2048, 1024])
    w1_d = dram_in("mlp_w1", [4, 1024, 4096])
    w2_d = dram_in("mlp_w2", [4, 4096, 1024])
    ys_d = dram_out("ys", [TS, 1024])
    yp_d = dram_out("yp", [TP, 1024])
    ns_d = dram_out("ns", [2, 2, 2, 4, 256, 512])

    HA = [POOL_W[c // 2] // 2 for c in range(8)]
    HB = [POOL_W[c // 2] // 2 - 1 for c in range(8)]
    A_OFF = [sum(HA[:c]) * 64 for c in range(8)]
    B_OFF = [sum(HB[:c]) * 64 for c in range(8)]
    A_W = sum(HA) * 64
    B_W = sum(HB) * 64
    ex_ea, ex_ga, ex_eb, ex_gb, ex_e, ex_g = {}, {}, {}, {}, {}, {}
    for i in (0, 2):
        ex_ea[i] = nc.dram_tensor(f"exea{i}", [P, A_W], F32)
        ex_ga[i] = nc.dram_tensor(f"exga{i}", [4 * P, A_W], F32)
        ex_eb[i] = nc.dram_tensor(f"exeb{i}", [P, B_W], F32)
        ex_gb[i] = nc.dram_tensor(f"exgb{i}", [4 * P, B_W], F32)
    for i in (1, 3):
        for hh in range(4):
            ex_e[(i, hh)] = nc.dram_tensor(f"exe{i}_{hh}", [P, 2048], F32)
            ex_g[(i, hh)] = nc.dram_tensor(f"exg{i}_{hh}", [4 * P, 2048], F32)

    def all_gather(src_t, dst_t, b_src, b_dst, dsem_):
        ws = K._collect(pool, (b_src,), (b_dst,))
        dsem_.n += 1
        tok = ("d", dsem_, dsem_.n)

        def run_cc(h, ws=ws, sem=dsem_.sem):
            for s_, v in ws:
                h.wait_ge(s_, v)
            h.collective_compute("AllGather", ALU.bypass, replica_groups=[[0, 1, 2, 3], [4, 5, 6, 7]],
                                 ins=[src_t.ap()], outs=[dst_t.ap()]).then_inc(sem, 1)

        pool.prog.append(run_cc)
        pool.meta.append((ws, [(dsem_.sem, 1)], sys._getframe(1).f_lineno))
        K._finish(tok, (b_src,), (b_dst,))

    MEMW = 53200
    mem = es.enter_context(nc.sbuf_tensor("mem", [P, MEMW], F32))
    ps = es.enter_context(nc.psum_tensor("ps", [P, 8, 512], F32))
    K.prealloc_sems(90)
    pbuf = [Buf(f"psum{b}", excl=True) for b in range(8)]

    def alloc(words):
        o = K.mem_off
        K.mem_off += words
        assert K.mem_off <= MEMW, K.mem_off
        return o

    def f32v(off, n):
        return mem[:, off:off + n]

    def bfv(off, nwords):
        return mem[:, off:off + nwords].bitcast(BF16)

    o_xs = alloc(8 * TS)
    o_xp = alloc(8 * TP)
    o_ht = alloc(8 * TT // 2)
    o_ring = alloc(NSLOT * 2048)
    o_cst = alloc(CST_N)
    o_pcv = alloc(PCV_N)
    o_mod = alloc(96 + 6 * 16 + 32)
    o_rstd = alloc(512)
    o_tmp = alloc(2 * 512)
    o_sq = alloc(2 * 256)
    o_ones = alloc(64)
    o_sc = alloc(8)
    o_phase = K.mem_off
    PHASE_W = MEMW - o_phase

    XS = f32v(o_xs, 8 * TS).rearrange("p (c t) -> p c t", c=8)
    XP = f32v(o_xp, 8 * TP).rearrange("p (c t) -> p c t", c=8)
    HT = bfv(o_ht, 8 * TT // 2).rearrange("p (c t) -> p c t", c=8)
    CST = f32v(o_cst, CST_N)
    PCV = f32v(o_pcv, PCV_N)
    MODR = f32v(o_mod, 96).rearrange("p (f c) -> p f c", c=2)
    MODD = f32v(o_mod + 96, 96).rearrange("p (k f c) -> p k f c", k=6, c=2)
    POOLD = f32v(o_mod + 192, 32).rearrange("p (k f c) -> p k f c", k=2, c=2)
    RSTD = f32v(o_rstd, 512)
    TMP = [f32v(o_tmp + i * 512, 512) for i in range(2)]
    SQ = [bfv(o_sq + i * 256, 256) for i in range(2)]
    ONES = bfv(o_ones, 64)
    SC = bfv(o_sc, 8).rearrange("p (k c) -> p k c", c=2)

    b_xs = [[Buf(f"xs{c}_{g}") for g in range(2)] for c in range(8)]
    b_xp = [Buf(f"xp{c}") for c in range(8)]
    b_ht = [[Buf(f"ht{c}_{g}") for g in range(3)] for c in range(8)]
    b_cst, b_pcv, b_modr, b_modd, b_poold = Buf("cst"), Buf("pcv"), Buf("modr"), Buf("modd"), Buf("poold")
    b_rstd = Buf("rstd")
    b_tmp = [Buf("tmp0"), Buf("tmp1")]
    b_sq = [Buf("sq0"), Buf("sq1")]
    b_ones, b_sc = Buf("ones"), Buf("sc")

    def cst(name):
        a, w = CST_L[name]
        return CST[:, a:a + w]

    def pcv(name):
        a, w = PCV_L[name]
        return PCV[:, a:a + w]

    def xview(c, g):
        if g < 2:
            return XS[:, c, g * 512:(g + 1) * 512]
        return XP[:, c, :]

    def xbuf(c, g):
        return b_xs[c][g] if g < 2 else b_xp[c]

    def htview(c, g):
        return HT[:, c, g * 512:(g + 1) * 512]

    rot = {"tmp": 0, "sq": 0, "pb": 0}

    ring_bufs = [Buf(f"ring{i}") for i in range(NSLOT)]
    ring_sems = [K.dsem() for _ in range(NSLOT)]
    ring_state = {"next": 0}

    def ring_load(parts):
        i = ring_state["next"]
        ring_state["next"] = (i + 1) % NSLOT
        slot = bfv(o_ring + i * 2048, 2048)
        first = True
        for (eo, c, n, src) in parts:
            dst = slot[:, eo:eo + c * n].rearrange("p (c n) -> p c n", c=c)
            K.dma(pool, dst, src, ring_sems[i], reads=(), writes=(ring_bufs[i],) if first else ())
            if not first:
                ring_bufs[i].w = ("d", ring_sems[i], ring_sems[i].n)
            first = False
        return slot, ring_bufs[i]

    s_misc = K.dsem()
    K.dma(sp, CST, cst_d, s_misc, writes=(b_cst,))
    s_misc2 = K.dsem()
    K.dma(sp, PCV, pcv_d, s_misc2, writes=(b_pcv,))
    K.op(dve, lambda h: h.memset(ONES, 1.0), writes=(b_ones,))

    K.op(act, lambda h: h.activation(out=SC.rearrange("p k c -> p (k c)"), in_=pcv("cond"), func=AF.Silu),
         reads=(b_pcv,), writes=(b_sc,))

    o_io = o_phase
    IO = [f32v(o_io + i * 1024, 1024) for i in range(2)]
    b_io = [Buf("io0"), Buf("io1")]
    s_io = [K.dsem(), K.dsem()]
    ident = cst("ident")

    def load_x_tile(t):
        i = t % 2
        src = xs_d[t * 128:(t + 1) * 128, :] if t < 8 else xp_d[(t - 8) * 128:(t - 7) * 128, :]
        K.dma(sp, IO[i], src, s_io[i], writes=(b_io[i],))
        for half in range(2):
            bank = 2 * i + half
            for cc in range(4):
                c = half * 4 + cc
                K.op(pe, lambda h, c=c, cc=cc, bank=bank, i=i: h.transpose(
                    out=ps[:, bank, cc * 128:(cc + 1) * 128], in_=IO[i][:, c * 128:(c + 1) * 128], identity=ident),
                    reads=(b_io[i], b_cst), writes=(pbuf[bank],), inc=(cc == 3))
            if t < 8:
                dst = XS[:, half * 4:(half + 1) * 4, t * 128:(t + 1) * 128]
                wb = [b_xs[c][t // 4] for c in range(half * 4, half * 4 + 4)]
            else:
                dst = XP[:, half * 4:(half + 1) * 4, (t - 8) * 128:(t - 7) * 128]
                wb = [b_xp[c] for c in range(half * 4, half * 4 + 4)]
            eng = act if half == 0 else dve
            src_ps = ps[:, bank, :].rearrange("p (c n) -> p c n", c=4)
            if eng is act:
                K.op(act, lambda h, dst=dst, src_ps=src_ps: h.activation(out=dst, in_=src_ps, func=AF.Copy),
                     reads=(pbuf[bank],), writes=wb)
            else:
                K.op(dve, lambda h, dst=dst, src_ps=src_ps: h.tensor_copy(out=dst, in_=src_ps),
                     reads=(pbuf[bank],), writes=wb)

    for t in range(12):
        load_x_tile(t)

    def mod_matmuls(i, blks):
        for blk in blks:
            slot, sb = ring_load([(0, 8, 512, wada_d[i, :, blk * 512:(blk + 1) * 512].rearrange("(c p) n -> p c n", p=P))])
            W = slot.rearrange("p (c n) -> p c n", c=8)
            for fcl in range(4):
                fc = blk * 4 + fcl
                for k in range(8):
                    K.op(pe, lambda h, W=W, k=k, fcl=fcl, fc=fc: h.matmul(
                        ps[:, 7, fc * 2:fc * 2 + 2], W[:, k, fcl * 128:(fcl + 1) * 128], SC[:, k, :],
                        start=(k == 0), stop=(k == 7)),
                        reads=(sb, b_sc), writes=(pbuf[7],), inc=(k == 7 and fcl == 3))

    def compute_mod(i, do_matmuls=True):
        if do_matmuls:
            mod_matmuls(i, range(12))
        bada = pcv("bada")[:, i * 48:(i + 1) * 48]
        K.op(dve, lambda h: h.tensor_tensor(
            out=MODR, in0=ps[:, 7, 0:96].rearrange("p (f c) -> p f c", c=2),
            in1=bada.unsqueeze(2).broadcast_to([P, 48, 2]), op=ALU.add),
            reads=(pbuf[7], b_pcv), writes=(b_modr,))
        nmix = pcv("nmix")[:, i * 8:(i + 1) * 8].unsqueeze(2).broadcast_to([P, 8, 2])
        nmlp = pcv("nmlp")[:, i * 8:(i + 1) * 8].unsqueeze(2).broadcast_to([P, 8, 2])
        K.op(dve, lambda h: h.scalar_tensor_tensor(out=MODD[:, 0], in0=MODR[:, 8:16, :], scalar=1.0, in1=nmix,
                                                   op0=ALU.add, op1=ALU.mult),
             reads=(b_modr, b_pcv), writes=(b_modd,))
        K.op(dve, lambda h: h.tensor_copy(out=MODD[:, 1], in_=MODR[:, 0:8, :]), reads=(b_modr,), writes=(b_modd,))
        K.op(dve, lambda h: h.tensor_copy(out=MODD[:, 2], in_=MODR[:, 16:24, :]), reads=(b_modr,), writes=(b_modd,))
        K.op(dve, lambda h: h.scalar_tensor_tensor(out=MODD[:, 3], in0=MODR[:, 32:40, :], scalar=1.0, in1=nmlp,
                                                   op0=ALU.add, op1=ALU.mult),
             reads=(b_modr, b_pcv), writes=(b_modd,))
        K.op(dve, lambda h: h.tensor_copy(out=MODD[:, 4], in_=MODR[:, 24:32, :]), reads=(b_modr,), writes=(b_modd,))
        K.op(dve, lambda h: h.tensor_copy(out=MODD[:, 5], in_=MODR[:, 40:48, :]), reads=(b_modr,), writes=(b_modd,))
        if i % 2 == 0:
            j = i // 2
            psc = pcv("pools")[:, j * 8:(j + 1) * 8].unsqueeze(2).broadcast_to([P, 8, 2])
            pbb = pcv("poolb")[:, j * 8:(j + 1) * 8].unsqueeze(2).broadcast_to([P, 8, 2])
            K.op(dve, lambda h: h.tensor_tensor(out=POOLD[:, 0], in0=MODR[:, 16:24, :], in1=psc, op=ALU.mult),
                 reads=(b_modr, b_pcv), writes=(b_poold,))
            K.op(dve, lambda h: h.tensor_tensor(out=POOLD[:, 1], in0=POOLD[:, 0], in1=pbb, op=ALU.mult),
                 reads=(b_poold, b_pcv), writes=(b_poold,))

    def mcol(k, c, cond):
        return MODD[:, k, c, cond:cond + 1]

    def norm_group(g, ka, kb, out_fn):
        cond = 0 if g < 2 else 1
        bank = 6
        for c in range(8):
            si = rot["sq"] % 2
            rot["sq"] += 1
            K.op(act, lambda h, c=c, si=si: h.activation(out=SQ[si], in_=xview(c, g), func=AF.Square),
                 reads=(xbuf(c, g),), writes=(b_sq[si],))
            K.op(pe, lambda h, c=c, si=si: h.matmul(ps[:, bank, :], ONES, SQ[si], start=(c == 0), stop=(c == 7)),
                 reads=(b_sq[si], b_ones), writes=(pbuf[bank],), inc=True)
        ti = rot["tmp"] % 2
        rot["tmp"] += 1
        K.op(act, lambda h, ti=ti: h.activation(out=TMP[ti], in_=ps[:, bank, :], func=AF.Sqrt, bias=NORM_EPS, scale=1.0 / 1024.0),
             reads=(pbuf[bank],), writes=(b_tmp[ti],))
        K.op(dve, lambda h, ti=ti: h.reciprocal(out=RSTD, in_=TMP[ti]), reads=(b_tmp[ti],), writes=(b_rstd,))
        for c in range(8):
            ti = rot["tmp"] % 2
            rot["tmp"] += 1
            K.op(dve, lambda h, c=c, ti=ti: h.tensor_tensor(out=TMP[ti], in0=xview(c, g), in1=RSTD, op=ALU.mult),
                 reads=(xbuf(c, g), b_rstd), writes=(b_tmp[ti],))
            dst, wb = out_fn(c)
            K.op(act, lambda h, c=c, ti=ti, dst=dst: h.activation(out=dst, in_=TMP[ti], func=AF.Identity,
                                                                  scale=mcol(ka, c, cond), bias=mcol(kb, c, cond)),
                 reads=(b_tmp[ti], b_modd), writes=wb)

    def norm_to_ht(g, ka, kb):
        norm_group(g, ka, kb, lambda c: (htview(c, g), (b_ht[c][g],)))

    def mlp_layer(i, prefetch_mod=None):
        K.barrier()
        for g in range(3):
            norm_to_ht(g, 3, 4)
        o_h1 = o_phase
        H1 = bfv(o_h1, 8 * TT // 2).rearrange("p (f t) -> p f t", f=8)
        b_h1 = [[Buf(f"h1_{f}_{g}") for g in range(3)] for f in range(8)]
        o_sqf = o_h1 + 8 * TT // 2
        SQF = [f32v(o_sqf + k * 512, 512) for k in range(2)]
        b_sqf = [Buf("sqf0"), Buf("sqf1")]
        for q in range(4):
            w1s = []
            for hb in range(2):
                c0 = q * 1024 + hb * 512
                slot, sb = ring_load([(0, 8, 512, w1_d[i, :, c0:c0 + 512].rearrange("(c p) n -> p c n", p=P))])
                w1s.append((slot.rearrange("p (c n) -> p c n", c=8), sb))
            for fc in range(8):
                W, sb = w1s[fc // 4]
                fl = fc % 4
                for g in range(3):
                    bank = rot["pb"] % 4
                    rot["pb"] += 1
                    for k in range(8):
                        K.op(pe, lambda h, W=W, k=k, fl=fl, g=g, bank=bank: h.matmul(
                            ps[:, bank, :], W[:, k, fl * 128:(fl + 1) * 128], htview(k, g), start=(k == 0), stop=(k == 7)),
                            reads=(sb, b_ht[k][g]), writes=(pbuf[bank],), inc=(k == 7))
                    si = rot["sq"] % 2
                    rot["sq"] += 1
                    K.op(act, lambda h, bank=bank, si=si: h.activation(out=SQF[si], in_=ps[:, bank, :], func=AF.Square),
                         reads=(pbuf[bank],), writes=(b_sqf[si],))
                    K.op(dve, lambda h, bank=bank, si=si, fc=fc, g=g: h.scalar_tensor_tensor(
                        out=H1[:, fc, g * 512:(g + 1) * 512], in0=ps[:, bank, :], scalar=0.0, in1=SQF[si],
                        op0=ALU.is_gt, op1=ALU.mult),
                        reads=(pbuf[bank], b_sqf[si]), writes=(b_h1[fc][g],))
            w2s = []
            for hb in range(2):
                r0 = q * 1024 + hb * 512
                slot, sb = ring_load([(0, 4, 1024, w2_d[i, r0:r0 + 512, :].rearrange("(c p) n -> p c n", p=P))])
                w2s.append((slot.rearrange("p (c n) -> p c n", c=4), sb))
            for dm in range(8):
                for g in range(3):
                    cond = 0 if g < 2 else 1
                    bank = 4 + rot["pb"] % 2
                    rot["pb"] += 1
                    for fc in range(8):
                        W, sb = w2s[fc // 4]
                        K.op(pe, lambda h, W=W, fc=fc, dm=dm, g=g, bank=bank: h.matmul(
                            ps[:, bank, :], W[:, fc % 4, dm * 128:(dm + 1) * 128], H1[:, fc, g * 512:(g + 1) * 512],
                            start=(fc == 0), stop=(fc == 7)),
                            reads=(sb, b_h1[fc][g]), writes=(pbuf[bank],), inc=(fc == 7))
                    K.op(dve, lambda h, dm=dm, g=g, bank=bank, cond=cond: h.scalar_tensor_tensor(
                        out=xview(dm, g), in0=ps[:, bank, :], scalar=mcol(5, dm, cond), in1=xview(dm, g),
                        op0=ALU.mult, op1=ALU.add),
                        reads=(pbuf[bank], b_modd, xbuf(dm, g)), writes=(xbuf(dm, g),))
            if prefetch_mod is not None:
                mod_matmuls(prefetch_mod, range(3 * q, 3 * q + 3))

    def pool_layer(i):
        K.barrier()
        j = i // 2
        o = o_phase
        o_hf = o
        o += 8 * TS
        o_pa = o
        o += 2 * 272
        o_pb = o
        o += 2 * 272
        o_rowa = o
        o += 32 * 64
        o_rowb = o
        o += 32 * 64
        o_cola = o
        o += 8 * 80
        o_colb = o
        o += 8 * 80
        o_hal = o
        o += 4 * 8 * 64
        o_evt = o
        o += 512
        assert o <= MEMW, o
        HFS = f32v(o_hf, 8 * TS).rearrange("p (c r w) -> p c r w", c=8, r=16)
        HFPc = [HT[:, c, 0:1024].bitcast(F32).rearrange("p (s t) -> p s t", s=2) for c in range(8)]
        b_hfs = [Buf(f"hfs{c}") for c in range(8)]
        ROWA = f32v(o_rowa, 2048).rearrange("p (r w) -> p r w", w=64)
        ROWB = f32v(o_rowb, 2048).rearrange("p (r w) -> p r w", w=64)
        COLA = f32v(o_cola, 640).rearrange("p (r w) -> p r w", w=80)
        COLB = f32v(o_colb, 640).rearrange("p (r w) -> p r w", w=80)
        HAL = f32v(o_hal, 2048).rearrange("p (k r w) -> p k r w", k=4, w=64)
        PA = f32v(o_pa, 544).rearrange("p (s t) -> p s t", s=2)
        PB = f32v(o_pb, 544).rearrange("p (s t) -> p s t", s=2)
        EVT = f32v(o_evt, 512)
        b_rowa, b_rowb, b_cola, b_colb, b_hal, b_pa, b_pb, b_evt = (Buf(n) for n in
                                                                    ("rowa", "rowb", "cola", "colb", "hal", "pa", "pb", "evt"))
        b_exe, b_exg, b_exe2, b_exg2 = Buf("exe"), Buf("exg"), Buf("exe2"), Buf("exg2")
        s_ex = K.dsem()
        s_ex2 = K.dsem()
        s_hal = K.dsem()
        s_cc = K.dsem()
        STEPS = [(1, 0), (1, 1), (2, 2), (4, 4)]
        NLEV = {2: 1, 4: 2, 8: 3, 16: 4}
        icp = pcv("icp")
        icr = pcv("icr")
        icc = pcv("icc")
        pm = pcv("pm")

        def dbl_last(cur, cb, outs, w, length):
            lo, hi = 0, length
            for lev in range(NLEV[w]):
                sm, spp = STEPS[lev]
                nlo, nhi = lo + sm, hi - spp
                dst, db = outs[lev % 2]
                K.op(dve, lambda h, dst=dst, cur=cur, nlo=nlo, nhi=nhi, sm=sm, spp=spp: h.tensor_tensor(
                    out=dst[:, :, nlo:nhi], in0=cur[:, :, nlo - sm:nhi - sm], in1=cur[:, :, nlo + spp:nhi + spp], op=ALU.add),
                    reads=(cb,), writes=(db,))
                cur, cb, lo, hi = dst, db, nlo, nhi
            return cur, cb

        def dbl_mid(cur, cb, outs, w, length):
            lo, hi = 0, length
            for lev in range(NLEV[w]):
                sm, spp = STEPS[lev]
                nlo, nhi = lo + sm, hi - spp
                dst, db = outs[lev % 2]
                K.op(dve, lambda h, dst=dst, cur=cur, nlo=nlo, nhi=nhi, sm=sm, spp=spp: h.tensor_tensor(
                    out=dst[:, nlo:nhi, :], in0=cur[:, nlo - sm:nhi - sm, :], in1=cur[:, nlo + spp:nhi + spp, :], op=ALU.add),
                    reads=(cb,), writes=(db,))
                cur, cb, lo, hi = dst, db, nlo, nhi
            return cur, cb

        def prompt_chunk(c):
            grp = c // 2
            w = POOL_W[grp]
            K.op(dve, lambda h: h.memset(PA, 0.0), writes=(b_pa,))
            K.op(dve, lambda h: h.tensor_copy(out=PA[:, :, 8:264], in_=HFPc[c]), reads=(b_ht[c][0], b_ht[c][1]), writes=(b_pa,))
            cur, cb = dbl_last(PA, b_pa, [(PB, b_pb), (PA, b_pa)], w, 272)
            ev = EVT.rearrange("p (s t) -> p s t", s=2)
            K.op(dve, lambda h: h.tensor_tensor(
                out=ev, in0=cur[:, :, 8:264], in1=icp[:, grp * 256:(grp + 1) * 256].unsqueeze(1).broadcast_to([P, 2, 256]),
                op=ALU.mult), reads=(cb, b_pcv), writes=(b_evt,))
            K.op(dve, lambda h: h.tensor_tensor(
                out=HT[:, c, 1024:1536].rearrange("p (s t) -> p s t", s=2), in0=ev, in1=HFPc[c], op=ALU.subtract),
                reads=(b_evt, b_ht[c][0], b_ht[c][1]), writes=(b_ht[c][2],))

        def sample_chunk(c):
            grp = c // 2
            w = POOL_W[grp]
            ha, hb = w // 2, w // 2 - 1
            K.op(act, lambda h: h.activation(out=ROWA[:, 8:24, :], in_=HFS[:, c], func=AF.Copy),
                 reads=(b_hfs[c],), writes=(b_rowa,))
            K.dma(sp, HAL[:, :, 0:ha, :].rearrange("p k r w -> p k (r w)"), GA[:, :, A_OFF[c]:A_OFF[c] + ha * 64], s_hal,
                  reads=(b_exg,), writes=(b_hal,))
            K.op(dve, lambda h: h.tensor_scalar(out=ROWA[:, 8 - ha:8, :], in0=HAL[:, 0, 0:ha, :], scalar1=pm[:, 0:1], scalar2=None,
                                                op0=ALU.mult), reads=(b_hal, b_pcv), writes=(b_rowa,))
            for k in range(1, 4):
                K.op(dve, lambda h, k=k: h.scalar_tensor_tensor(
                    out=ROWA[:, 8 - ha:8, :], in0=HAL[:, k, 0:ha, :], scalar=pm[:, k:k + 1], in1=ROWA[:, 8 - ha:8, :],
                    op0=ALU.mult, op1=ALU.add), reads=(b_hal, b_pcv, b_rowa), writes=(b_rowa,))
            if hb > 0:
                K.dma(sp, HAL[:, :, 0:hb, :].rearrange("p k r w -> p k (r w)"), GB[:, :, B_OFF[c]:B_OFF[c] + hb * 64], s_hal,
                      reads=(b_exg2,), writes=(b_hal,))
                K.op(dve, lambda h: h.tensor_scalar(out=ROWA[:, 24:24 + hb, :], in0=HAL[:, 0, 0:hb, :], scalar1=pm[:, 4:5], scalar2=None,
                                                    op0=ALU.mult), reads=(b_hal, b_pcv), writes=(b_rowa,))
                for k in range(1, 4):
                    K.op(dve, lambda h, k=k: h.scalar_tensor_tensor(
                        out=ROWA[:, 24:24 + hb, :], in0=HAL[:, k, 0:hb, :], scalar=pm[:, 4 + k:5 + k], in1=ROWA[:, 24:24 + hb, :],
                        op0=ALU.mult, op1=ALU.add), reads=(b_hal, b_pcv, b_rowa), writes=(b_rowa,))
            cur, cb = dbl_mid(ROWA, b_rowa, [(ROWB, b_rowb), (ROWA, b_rowa)], w, 32)
            for g in range(2):
                K.op(dve, lambda h: h.memset(COLA, 0.0), writes=(b_cola,))
                K.op(dve, lambda h, g=g, cur=cur: h.tensor_tensor(
                    out=COLA[:, :, 8:72], in0=cur[:, 8 + g * 8:16 + g * 8, :],
                    in1=icr[:, grp * 16 + g * 8:grp * 16 + g * 8 + 8].unsqueeze(2).broadcast_to([P, 8, 64]), op=ALU.mult),
                    reads=(cb, b_pcv), writes=(b_cola,))
                c2, c2b = dbl_last(COLA, b_cola, [(COLB, b_colb), (COLA, b_cola)], w, 80)
                ev = EVT.rearrange("p (r w) -> p r w", w=64)
                K.op(dve, lambda h, c2=c2: h.tensor_tensor(
                    out=ev, in0=c2[:, :, 8:72],
                    in1=icc[:, grp * 64:(grp + 1) * 64].unsqueeze(1).broadcast_to([P, 8, 64]), op=ALU.mult),
                    reads=(c2b, b_pcv), writes=(b_evt,))
                K.op(dve, lambda h, g=g: h.tensor_tensor(
                    out=HT[:, c, g * 512:(g + 1) * 512].rearrange("p (r w) -> p r w", w=64), in0=ev,
                    in1=HFS[:, c, g * 8:(g + 1) * 8, :], op=ALU.subtract),
                    reads=(b_evt, b_hfs[c]), writes=(b_ht[c][g],))

        def linear_group(g):
            cond = 0 if g < 2 else 1
            for fo in range(8):
                grp = fo // 2
                bank = rot["pb"] % 4
                rot["pb"] += 1
                for kk in range(2):
                    fi = grp * 2 + kk
                    K.op(pe, lambda h, fi=fi, fo=fo, bank=bank, kk=kk: h.matmul(
                        ps[:, bank, :], PW[:, fi, (fo % 2) * 128:(fo % 2 + 1) * 128], htview(fi, g), start=(kk == 0), stop=(kk == 1)),
                        reads=(sbw, b_ht[fi][g]), writes=(pbuf[bank],), inc=(kk == 1))
                ti = rot["tmp"] % 2
                rot["tmp"] += 1
                K.op(act, lambda h, bank=bank, ti=ti, fo=fo: h.activation(
                    out=TMP[ti], in_=ps[:, bank, :], func=AF.Identity,
                    scale=POOLD[:, 0, fo, cond:cond + 1], bias=POOLD[:, 1, fo, cond:cond + 1]),
                    reads=(pbuf[bank], b_poold), writes=(b_tmp[ti],))
                K.op(dve, lambda h, ti=ti, fo=fo: h.tensor_tensor(out=xview(fo, g), in0=xview(fo, g), in1=TMP[ti], op=ALU.add),
                     reads=(b_tmp[ti], xbuf(fo, g)), writes=(xbuf(fo, g),))

        slot, sbw = ring_load([(0, 8, 256, poolw_d[j].rearrange("g (c p) n -> p (g c) n", p=P))])
        PW = slot[:, 0:2048].rearrange("p (c n) -> p c n", c=8)

        for g in range(2):
            norm_group(g, 0, 1, lambda c, g=g: (HFS[:, c, g * 8:(g + 1) * 8, :].rearrange("p r w -> p (r w)"), (b_hfs[c],)))
        EA, EB = ex_ea[i].ap(), ex_eb[i].ap()
        for c in range(8):
            ha, hb = HA[c], HB[c]
            K.dma(sp, EA[:, A_OFF[c]:A_OFF[c] + ha * 64], HFS[:, c, 16 - ha:16, :].rearrange("p r w -> p (r w)"), s_ex,
                  reads=(b_hfs[c],), writes=())
        b_exe.w = ("d", s_ex, s_ex.n)
        for c in range(8):
            ha, hb = HA[c], HB[c]
            if hb > 0:
                K.dma(sp, EB[:, B_OFF[c]:B_OFF[c] + hb * 64], HFS[:, c, 0:hb, :].rearrange("p r w -> p (r w)"), s_ex2,
                      reads=(b_hfs[c],), writes=())
        b_exe2.w = ("d", s_ex2, s_ex2.n)
        all_gather(ex_ea[i], ex_ga[i], b_exe, b_exg, s_cc)
        all_gather(ex_eb[i], ex_gb[i], b_exe2, b_exg2, s_cc)
        GA = ex_ga[i].ap().rearrange("(k p) n -> p k n", p=P)
        GB = ex_gb[i].ap().rearrange("(k p) n -> p k n", p=P)
        norm_group(2, 0, 1, lambda c: (HFPc[c].rearrange("p s t -> p (s t)"), (b_ht[c][0], b_ht[c][1])))
        for c in range(8):
            prompt_chunk(c)
        linear_group(2)
        for c in range(8):
            sample_chunk(c)
        for g in range(2):
            linear_group(g)

    def ret_layer(i):
        K.barrier()
        j = i // 2
        for g in range(3):
            norm_to_ht(g, 0, 1)
        oo = [o_phase]

        def al(n):
            r_ = oo[0]
            oo[0] += n
            assert oo[0] <= MEMW, oo[0]
            return r_

        o_rope = al(2 * 384)
        o_dec = al(384)
        o_cols = al(256)
        o_lg = al(16)
        o_idb = al(64)
        o_qt = al(1024)
        o_kt = al(1024)
        o_ktok = al(1024)
        o_v = al(2048)
        o_ra = al(256)
        o_rb = al(256)
        o_qtok = al(2 * 128)
        o_ks = al(2 * 128)
        o_qs = al(2 * 128)
        o_st = al(2 * 64)
        o_sball = al(4096)
        o_sfm = al(1024)
        o_sbm = al(1024)
        o_sfb = al(512)
        o_sin = al(2 * 1024)
        o_oh = al(512)
        o_sg = al(512)
        o_ogt = al(256)
        o_og = al(1024)
        o_gnw = al(512)
        o_stat = al(32)

        ROPE = [f32v(o_rope + r_ * 384, 384) for r_ in range(2)]
        b_rope = [Buf("rope0"), Buf("rope1")]
        s_rope = [K.dsem(), K.dsem()]
        DEC = f32v(o_dec, 384)
        Dm, XIF, XIB = DEC[:, 0:128], DEC[:, 128:256], DEC[:, 256:384]
        COLS4 = f32v(o_cols, 256).rearrange("p (h c) -> p h c", h=4)
        b_cols = Buf("cols")
        LG = f32v(o_lg, 16)
        IDB = bfv(o_idb, 64)
        QT = bfv(o_qt, 1024).rearrange("p (c t) -> p c t", c=2)
        KT = bfv(o_kt, 1024).rearrange("p (c t) -> p c t", c=2)
        KTOK = bfv(o_ktok, 1024).rearrange("p (t d) -> p t d", t=8)
        V = bfv(o_v, 2048).rearrange("p (t e) -> p t e", t=8)
        RA = f32v(o_ra, 256)
        RB = f32v(o_rb, 256)
        QTOK = [bfv(o_qtok + r_ * 128, 128) for r_ in range(2)]
        KS = [bfv(o_ks + r_ * 128, 128) for r_ in range(2)]
        QS = [bfv(o_qs + r_ * 128, 128).rearrange("p (c t) -> p c t", c=2) for r_ in range(2)]
        ST = [bfv(o_st + r_ * 64, 64) for r_ in range(2)]
        SBALL = bfv(o_sball, 4096).rearrange("p (t c e) -> p t c e", t=8, c=2)
        SFM = f32v(o_sfm, 1024).rearrange("p (c e) -> p c e", c=2)
        SBM = f32v(o_sbm, 1024).rearrange("p (c e) -> p c e", c=2)
        SFBS = [bfv(o_sfb, 512).rearrange("p (c e) -> p c e", c=2),
                bfv(o_tmp + 512, 512).rearrange("p (c e) -> p c e", c=2)]
        SIN = [f32v(o_sin + r_ * 1024, 1024).rearrange("p (c e) -> p c e", c=2) for r_ in range(2)]
        TSTG = f32v(o_sin, 2048).rearrange("p (a e) -> p a e", a=4)
        VT = [bfv(o_v + r_ * 256, 256) for r_ in range(2)]
        OH = f32v(o_oh, 512)
        SG = f32v(o_sg, 512)
        OGT = bfv(o_ogt, 256)
        OG = bfv(o_og, 1024).rearrange("p (e t) -> p e t", e=4)
        GNW = f32v(o_gnw, 512)
        STAT = f32v(o_stat, 32)
        TPB = ps[:, 3, :].bitcast(BF16)[:, 0:512].rearrange("p (a n) -> p a n", a=4)

        b_lg, b_dec, b_idb = Buf("lg"), Buf("dec"), Buf("idb")
        b_ra, b_rb = Buf("ra"), Buf("rb")
        b_qtok = [Buf("qtok0"), Buf("qtok1")]
        b_ks = [Buf("ks0"), Buf("ks1")]
        b_qs = [Buf("qs0"), Buf("qs1")]
        b_st = [Buf("st0"), Buf("st1")]
        b_sfm, b_sbm = Buf("sfm"), Buf("sbm")
        b_statc = Buf("statc")
        K.op(dve, lambda h: h.memset(STAT[:, 16:17], -0.5), writes=(b_statc,))
        b_sfbs = [Buf("sfb0"), b_tmp[1]]
        b_sin = [Buf("sin0"), Buf("sin1")]
        b_vt = [Buf("vt0"), Buf("vt1")]
        b_oh, b_sg, b_ogt, b_og, b_gnw, b_stat = Buf("oh"), Buf("sg"), Buf("ogt"), Buf("og"), Buf("gnw"), Buf("stat")
        b_exe = [Buf(f"exe{q_}") for q_ in range(4)]
        b_exg = [Buf(f"exg{q_}") for q_ in range(4)]
        s_sin = [K.dsem(), K.dsem()]
        s_ex, s_cc, s_gnw, s_ns = K.dsem(), K.dsem(), K.dsem(), K.dsem()
        LN16 = -2.772588722239781

        dec = pcv("decay")[:, j * 8:(j + 1) * 8]
        xw = pcv("xw")
        K.op(act, lambda h: h.activation(out=LG[:, 8:16], in_=dec, func=AF.Exp, scale=-0.6931471805599453),
             reads=(b_pcv,), writes=(b_lg,))
        K.op(act, lambda h: h.activation(out=LG[:, 0:8], in_=LG[:, 8:16], func=AF.Ln, scale=-1.0, bias=1.0),
             reads=(b_lg,), writes=(b_lg,))
        K.op(dve, lambda h: h.tensor_copy(out=IDB, in_=ident), reads=(b_cst,), writes=(b_idb,))

        def aexp(out, in_, lg, bias=0.0, rd=(), wr=()):
            K.op(act, lambda h: h.activation(out=out, in_=in_, func=AF.Exp, scale=lg, bias=bias),
                 reads=(b_lg, b_cst, b_pcv) + tuple(rd), writes=tuple(wr))

        def setup_cols(hh):
            COLS = COLS4[:, hh, :]
            lgf = LG[:, hh:hh + 1]
            lgb = LG[:, 4 + hh:5 + hh]
            aexp(COLS[:, 0:1], cst("colA"), lgf, bias=LN16, wr=(b_cols,))
            aexp(COLS[:, 1:2], cst("colB"), lgb, bias=LN16, wr=(b_cols,))
            aexp(COLS[:, 2:3], cst("xib")[:, 0:1], lgf, wr=(b_cols,))
            aexp(COLS[:, 3:4], cst("xib")[:, 0:1], lgb, wr=(b_cols,))
            aexp(COLS[:, 4:12], cst("colsF"), lgf, bias=LN16, wr=(b_cols,))
            aexp(COLS[:, 12:20], cst("colsB"), lgb, bias=LN16, wr=(b_cols,))
            aexp(COLS[:, 32:36], xw[:, 0:4], lgf, wr=(b_cols,))
            aexp(COLS[:, 36:40], xw[:, 8:12], lgb, wr=(b_cols,))
            aexp(COLS[:, 28:29], xw[:, 16:17], lgf, wr=(b_cols,))
            aexp(COLS[:, 29:30], xw[:, 17:18], lgb, wr=(b_cols,))
            K.op(dve, lambda h: h.tensor_tensor(out=COLS[:, 20:24], in0=COLS[:, 32:36], in1=xw[:, 4:8], op=ALU.mult),
                 reads=(b_cols, b_pcv), writes=(b_cols,))
            K.op(dve, lambda h: h.tensor_tensor(out=COLS[:, 24:28], in0=COLS[:, 36:40], in1=xw[:, 12:16], op=ALU.mult),
                 reads=(b_cols, b_pcv), writes=(b_cols,))

        def setup_head(hh):
            lgf = LG[:, hh:hh + 1]
            lgb = LG[:, 4 + hh:5 + hh]
            T1 = TMP[0][:, 0:128]
            T2 = TMP[1][:, 0:128]
            aexp(T1, cst("dpos"), lgf, wr=(b_tmp[0],))
            aexp(T2, cst("dneg"), lgb, wr=(b_tmp[1],))
            K.op(dve, lambda h: h.tensor_tensor(out=T1, in0=T1, in1=cst("mf"), op=ALU.mult), reads=(b_tmp[0], b_cst), writes=(b_tmp[0],))
            K.op(dve, lambda h: h.tensor_tensor(out=T2, in0=T2, in1=cst("mb"), op=ALU.mult), reads=(b_tmp[1], b_cst), writes=(b_tmp[1],))
            K.op(dve, lambda h: h.tensor_tensor(out=Dm, in0=T1, in1=T2, op=ALU.add), reads=(b_tmp[0], b_tmp[1]), writes=(b_dec,))
            aexp(XIF, cst("xif"), lgf, wr=(b_dec,))
            aexp(XIB, cst("xib"), lgb, wr=(b_dec,))

        def rope_apply(src_ps, pb_, r_, out_ap, out_bufs):
            rp = ROPE[r_]
            s4 = src_ps.rearrange("p (h x f) -> p h x f", h=2, x=2)
            cos4 = rp[:, 0:128].rearrange("p (h f) -> p h f", h=2).unsqueeze(2).broadcast_to([P, 2, 2, 64])
            sin3 = rp[:, 128:256].rearrange("p (h f) -> p h f", h=2)
            nsin3 = rp[:, 256:384].rearrange("p (h f) -> p h f", h=2)
            RA4 = RA.rearrange("p (h x f) -> p h x f", h=2, x=2)
            RB4 = RB.rearrange("p (h x f) -> p h x f", h=2, x=2)
            K.op(dve, lambda h: h.tensor_tensor(out=RA4, in0=s4, in1=cos4, op=ALU.mult),
                 reads=(pb_, b_rope[r_]), writes=(b_ra,))
            K.op(dve, lambda h: h.tensor_tensor(out=RB4[:, :, 0, :], in0=s4[:, :, 1, :], in1=nsin3, op=ALU.mult),
                 reads=(pb_, b_rope[r_]), writes=(b_rb,))
            K.op(dve, lambda h: h.tensor_tensor(out=RB4[:, :, 1, :], in0=s4[:, :, 0, :], in1=sin3, op=ALU.mult),
                 reads=(pb_, b_rope[r_]), writes=(b_rb,))
            K.op(dve, lambda h: h.tensor_tensor(out=out_ap, in0=RA, in1=RB, op=ALU.add),
                 reads=(b_ra, b_rb), writes=tuple(out_bufs))

        def proj(bank, ncol, W, sbw_, colsl, htb_, last_inc=True):
            for k in range(8):
                K.op(pe, lambda h, k=k: h.matmul(ps[:, bank, 0:ncol], HT[:, k, colsl], W[:, k, :], start=(k == 0), stop=(k == 7)),
                     reads=(sbw_, htb_[k]), writes=(pbuf[bank],), inc=(k == 7))

        def phase1_loads(hh):
            slotk, sbk = ring_load([(0, 8, 256, win_d[j, :, 1024 + hh * 256:1024 + (hh + 1) * 256].rearrange("(c p) n -> p c n", p=P))])
            WK = slotk[:, 0:2048].rearrange("p (c n) -> p c n", c=8)
            slotv, sbv = ring_load([(0, 8, 512, win_d[j, :, 2048 + hh * 512:2048 + (hh + 1) * 512].rearrange("(c p) n -> p c n", p=P))])
            WV = slotv.rearrange("p (c n) -> p c n", c=8)
            return WK, sbk, WV, sbv

        def phase1_head(hh, wts, dummy=False):
            COLS = COLS4[:, hh, :]
            WK, sbk, WV, sbv = wts

            def tile_proj(t):
                r_ = t % 2
                bk, bv = (0, 1) if t % 2 == 0 else (2, 3)
                colsl = slice(t * 128, (t + 1) * 128)
                htb_ = [b_ht[k][t // 4] for k in range(8)]
                K.dma(sp, ROPE[r_], rope_d[:, t, :], s_rope[r_], writes=(b_rope[r_],))
                proj(bk, 256, WK, sbk, colsl, htb_)
                proj(bv, 512, WV, sbv, colsl, htb_)

            def tile_post(t):
                r_ = t % 2
                bk, bv = (0, 1) if t % 2 == 0 else (2, 3)
                rope_apply(ps[:, bk, 0:256], pbuf[bk], r_, RA, (b_ra,))
                K.op(dve, lambda h: h.tensor_scalar(out=KS[0], in0=RA, scalar1=COLS[:, 4 + t:5 + t], scalar2=None, op0=ALU.mult),
                     reads=(b_ra, b_dec, b_cols), writes=(b_ks[0],))
                K.op(dve, lambda h: h.tensor_scalar(out=KS[1], in0=RA, scalar1=COLS[:, 12 + t:13 + t], scalar2=None, op0=ALU.mult),
                     reads=(b_ra, b_dec, b_cols), writes=(b_ks[1],))
                K.op(act, lambda h: h.activation(out=VT[r_], in_=ps[:, bv, :], func=AF.Copy), reads=(pbuf[bv],), writes=(b_vt[r_],))
                for d in range(2):
                    for dc in range(2):
                        K.op(pe, lambda h, d=d, dc=dc: h.matmul(ps[:, 4 + d * 2 + dc, :], KS[d][:, dc * 128:(dc + 1) * 128], VT[r_],
                                                              start=(t == 0), stop=(t == 7)),
                             reads=(b_ks[d], b_vt[r_]), writes=(pbuf[4 + d * 2 + dc],), inc=(d == 1 and dc == 1))

            tile_proj(0)
            for t in range(8):
                if t + 1 < 8:
                    tile_proj(t + 1)
                tile_post(t)
            for a in range(4):
                if a % 2 == 0:
                    K.op(act, lambda h, a=a: h.activation(out=TSTG[:, a, :], in_=ps[:, 4 + a, :], func=AF.Copy),
                         reads=(pbuf[4 + a],), writes=(b_sin[a // 2],))
                else:
                    K.op(dve, lambda h, a=a: h.tensor_copy(out=TSTG[:, a, :], in_=ps[:, 4 + a, :]),
                         reads=(pbuf[4 + a],), writes=(b_sin[a // 2],))
            K.dma(sp, ex_e[(i, hh)].ap(), TSTG.rearrange("p a e -> p (a e)"), s_ex,
                  reads=(b_sin[0], b_sin[1]), writes=(b_exe[hh],))

        RSTAGE = 99
        for hh in range(4):
            setup_cols(hh)
        wts = phase1_loads(0)
        for hh in range(4):
            phase1_head(hh, wts)
            if hh + 1 < 4:
                wts = phase1_loads(hh + 1)
                all_gather(ex_e[(i, hh)], ex_g[(i, hh)], b_exe[hh], b_exg[hh], s_cc)
        def ret_full(sample, hh, WQK, sbqk, WV, sbv, WG, sbg, WO, sbo):
            COLS = COLS4[:, hh, :]
            nt = 8 if sample else 4
            tok0 = 0 if sample else 1024
            cond = 0 if sample else 1
            seqs = [list(range(8))] if sample else [[0, 1], [2, 3]]
            b_qt = [Buf(f"qt{t}") for t in range(nt)]
            b_kt = [Buf(f"kt{t}") for t in range(nt)]
            b_ktok = [Buf(f"ktok{t}") for t in range(nt)]
            b_v = [Buf(f"v{t}") for t in range(nt)]
            b_sball = [Buf(f"sball{t}") for t in range(nt)]

            def htbs(t):
                return [b_ht[k][(tok0 + t * 128) // 512] for k in range(8)]

            def stepA_proj(t):
                r_ = t % 2
                bq, bv = (0, 1) if t % 2 == 0 else (6, 7)
                colsl = slice(tok0 + t * 128, tok0 + (t + 1) * 128)
                if sample:
                    K.dma(sp, ROPE[r_], rope_d[:, t, :], s_rope[r_], writes=(b_rope[r_],))
                proj(bq, 512, WQK, sbqk, colsl, htbs(t))
                proj(bv, 512, WV, sbv, colsl, htbs(t))

            def stepA_post(t):
                r_ = t % 2
                bq, bv = (0, 1) if t % 2 == 0 else (6, 7)
                tc = slice(t * 128, (t + 1) * 128)
                if sample:
                    rope_apply(ps[:, bq, 0:256], pbuf[bq], r_, QTOK[r_], (b_qtok[r_],))
                    rope_apply(ps[:, bq, 256:512], pbuf[bq], r_, KTOK[:, t, :], (b_ktok[t],))
                else:
                    K.op(act, lambda h: h.activation(out=QTOK[r_], in_=ps[:, bq, 0:256], func=AF.Copy),
                         reads=(pbuf[bq],), writes=(b_qtok[r_],))
                    K.op(dve, lambda h: h.tensor_copy(out=KTOK[:, t, :], in_=ps[:, bq, 256:512]),
                         reads=(pbuf[bq],), writes=(b_ktok[t],))
                K.op(act, lambda h: h.activation(out=V[:, t, :], in_=ps[:, bv, :], func=AF.Copy), reads=(pbuf[bv],), writes=(b_v[t],))
                for dc in range(2):
                    K.op(pe, lambda h, dc=dc: h.transpose(out=TPB[:, dc, :], in_=QTOK[r_][:, dc * 128:(dc + 1) * 128], identity=IDB),
                         reads=(b_qtok[r_], b_idb), writes=(pbuf[3],), inc=False)
                for dc in range(2):
                    K.op(pe, lambda h, dc=dc: h.transpose(out=TPB[:, 2 + dc, :], in_=KTOK[:, t, dc * 128:(dc + 1) * 128], identity=IDB),
                         reads=(b_ktok[t], b_idb), writes=(pbuf[3],), inc=(dc == 1))
                K.op(act, lambda h: h.activation(out=QT[:, :, tc], in_=TPB[:, 0:2, :], func=AF.Copy), reads=(pbuf[3],), writes=(b_qt[t],))
                K.op(dve, lambda h: h.tensor_copy(out=KT[:, :, tc], in_=TPB[:, 2:4, :]), reads=(pbuf[3],), writes=(b_kt[t],))

            def s_in(d, SM, b_sm):
                K.dma(sp, SIN[0], s0_d[j, d, hh].rearrange("(c p) e -> p c e", p=P), s_sin[0], writes=(b_sin[0],))
                K.op(dve, lambda h: h.tensor_scalar(out=SM, in0=SIN[0], scalar1=COLS[:, 28 + d:29 + d], scalar2=None, op0=ALU.mult),
                     reads=(b_sin[0], b_dec, b_cols), writes=(b_sm,))
                yield
                for r4 in (range(0, 3) if d == 0 else range(1, 4)):
                    bi = (r4 + 1) % 2
                    srcg = ex_g[(i, hh)].ap()[r4 * 128:(r4 + 1) * 128, d * 1024:(d + 1) * 1024].rearrange(
                        "p (c e) -> p c e", c=2)
                    K.dma(sp, SIN[bi], srcg, s_sin[bi], reads=(b_exg[hh],), writes=(b_sin[bi],))
                    if os.environ.get("DEBUG_SB") and hh == 0 and d == 1:
                        K.dma(sp, ns_d[0, 1, 0, r4].rearrange("(c p) e -> p c e", p=P), SIN[bi], s_ns, reads=(b_sin[bi],))
                    K.op(dve, lambda h, bi=bi, r4=r4: h.scalar_tensor_tensor(
                        out=SM, in0=SIN[bi], scalar=COLS[:, 20 + d * 4 + r4:21 + d * 4 + r4], in1=SM, op0=ALU.mult, op1=ALU.add),
                        reads=(b_sin[bi], b_dec, b_cols, b_sm), writes=(b_sm,))
                    yield

            def wout(g):
                for dm in range(8):
                    bank = 7 if dm % 2 == 0 else 0
                    for e in range(4):
                        K.op(pe, lambda h, dm=dm, e=e, bank=bank: h.matmul(ps[:, bank, :], WO[:, e, dm * 128:(dm + 1) * 128], OG[:, e, :],
                                                                            start=(e == 0), stop=(e == 3)),
                             reads=(sbo, b_og), writes=(pbuf[bank],), inc=(e == 3))
                    K.op(dve, lambda h, dm=dm, bank=bank: h.scalar_tensor_tensor(
                        out=xview(dm, g), in0=ps[:, bank, :], scalar=mcol(2, dm, cond), in1=xview(dm, g), op0=ALU.mult, op1=ALU.add),
                        reads=(pbuf[bank], b_modd, xbuf(dm, g)), writes=(xbuf(dm, g),))

            def bwd(seq, si):
                if not sample:
                    K.op(dve, lambda h: h.memset(SBM, 0.0), writes=(b_sbm,))
                order = list(reversed(seq))

                def ksu(c, ub):
                    K.op(dve, lambda h: h.tensor_scalar(out=KS[1], in0=KTOK[:, c, :], scalar1=COLS[:, 1:2], scalar2=None, op0=ALU.mult),
                         reads=(b_ktok[c], b_dec, b_cols), writes=(b_ks[1],))
                    for dc in range(2):
                        K.op(pe, lambda h, dc=dc: h.matmul(ps[:, ub + dc, :], KS[1][:, dc * 128:(dc + 1) * 128], V[:, c, :],
                                                           start=True, stop=True),
                             reads=(b_ks[1], b_v[c]), writes=(pbuf[ub + dc],), inc=(dc == 1))

                def upd(c, ub):
                    K.op(act, lambda h: h.activation(out=SBALL[:, c], in_=SBM, func=AF.Copy), reads=(b_sbm,), writes=(b_sball[c],))
                    K.op(dve, lambda h: h.scalar_tensor_tensor(out=SBM, in0=SBM, scalar=COLS[:, 3:4], in1=ps[:, ub:ub + 2, :],
                                                               op0=ALU.mult, op1=ALU.add),
                         reads=(b_sbm, b_dec, b_cols, pbuf[ub], pbuf[ub + 1]), writes=(b_sbm,))

                ksu(order[0], 4)
                for k_, c in enumerate(order):
                    if k_ + 1 < len(order):
                        ksu(order[k_ + 1], 4 if (k_ + 1) % 2 == 0 else 6)
                    upd(c, 4 if k_ % 2 == 0 else 6)
                if sample and os.environ.get("DEBUG_SB"):
                    K.dma(sp, ns_d[1, j, 1, hh].rearrange("(c p) e -> p c e", p=P), SBM, s_ns, reads=(b_sbm,))
                if not sample:
                    K.dma(sp, ns_d[si, j, 1, hh].rearrange("(c p) e -> p c e", p=P), SBM, s_ns, reads=(b_sbm,))

            def fwd(seq, si):
                if not sample:
                    K.op(dve, lambda h: h.memset(SFM, 0.0), writes=(b_sfm,))
                K.op(act, lambda h: h.activation(out=SFBS[0], in_=SFM, func=AF.Copy), reads=(b_sfm,), writes=(b_sfbs[0],))
                prev = None
                for k_, c in enumerate(seq):
                    fwd_head(c, k_)
                    if prev is not None:
                        fwd_tail(*prev)
                    prev = (c, k_)
                fwd_tail(*prev)
                if not sample:
                    K.dma(sp, ns_d[si, j, 0, hh].rearrange("(c p) e -> p c e", p=P), SFM, s_ns, reads=(b_sfm,))

            def fwd_head(c, k_):
                r_ = c % 2
                ob = 1 if k_ % 2 == 0 else 6
                sfb_r, sfb_w = SFBS[k_ % 2], SFBS[(k_ + 1) % 2]
                bs_r, bs_w = b_sfbs[k_ % 2], b_sfbs[(k_ + 1) % 2]
                tc = slice(c * 128, (c + 1) * 128)
                K.op(dve, lambda h: h.tensor_scalar(out=KS[0], in0=KTOK[:, c, :], scalar1=COLS[:, 0:1], scalar2=None, op0=ALU.mult),
                     reads=(b_ktok[c], b_dec, b_cols), writes=(b_ks[0],))
                for dc in range(2):
                    K.op(pe, lambda h, dc=dc: h.matmul(ps[:, 2, 0:128], KT[:, dc, tc], QT[:, dc, tc], start=(dc == 0), stop=(dc == 1)),
                         reads=(b_kt[c], b_qt[c]), writes=(pbuf[2],), inc=(dc == 1))
                for dc in range(2):
                    K.op(pe, lambda h, dc=dc: h.matmul(ps[:, 4 + dc, :], KS[0][:, dc * 128:(dc + 1) * 128], V[:, c, :], start=True, stop=True),
                         reads=(b_ks[0], b_v[c]), writes=(pbuf[4 + dc],), inc=(dc == 1))
                K.op(dve, lambda h: h.tensor_tensor(out=QS[0], in0=QT[:, :, tc], in1=XIF.unsqueeze(1).broadcast_to([P, 2, 128]), op=ALU.mult),
                     reads=(b_qt[c], b_dec), writes=(b_qs[0],))
                K.op(dve, lambda h: h.tensor_tensor(out=QS[1], in0=QT[:, :, tc], in1=XIB.unsqueeze(1).broadcast_to([P, 2, 128]), op=ALU.mult),
                     reads=(b_qt[c], b_dec), writes=(b_qs[1],))
                K.op(dve, lambda h: h.tensor_tensor(out=ST[r_], in0=ps[:, 2, 0:128], in1=Dm, op=ALU.mult),
                     reads=(pbuf[2], b_dec), writes=(b_st[r_],))
                K.op(pe, lambda h: h.matmul(ps[:, ob, :], ST[r_], V[:, c, :], start=True, stop=False),
                     reads=(b_st[r_], b_v[c]), writes=(pbuf[ob],), inc=False)
                for dc in range(2):
                    K.op(pe, lambda h, dc=dc: h.matmul(ps[:, ob, :], QS[1][:, dc, :], SBALL[:, c, dc, :], start=False, stop=False),
                         reads=(b_qs[1], b_sball[c]), writes=(pbuf[ob],), inc=False)
                for dc in range(2):
                    K.op(pe, lambda h, dc=dc: h.matmul(ps[:, ob, :], QS[0][:, dc, :], sfb_r[:, dc, :], start=False, stop=(dc == 1)),
                         reads=(b_qs[0], bs_r), writes=(pbuf[ob],), inc=(dc == 1))
                K.op(dve, lambda h: h.scalar_tensor_tensor(out=SFM, in0=SFM, scalar=COLS[:, 2:3], in1=ps[:, 4:6, :],
                                                           op0=ALU.mult, op1=ALU.add),
                     reads=(b_sfm, b_dec, b_cols, pbuf[4], pbuf[5]), writes=(b_sfm,))
                K.op(act, lambda h: h.activation(out=sfb_w, in_=SFM, func=AF.Copy), reads=(b_sfm,), writes=(bs_w,))

            def fwd_tail(c, k_):
                ob = 1 if k_ % 2 == 0 else 6
                colsl = slice(tok0 + c * 128, tok0 + (c + 1) * 128)
                proj(0, 512, WG, sbg, colsl, htbs(c))
                K.op(act, lambda h: h.activation(out=SG, in_=ps[:, 0, :], func=AF.Silu), reads=(pbuf[0],), writes=(b_sg,))
                K.op(dve, lambda h: h.bn_stats(out=STAT[:, 0:6], in_=ps[:, ob, :]), reads=(pbuf[ob],), writes=(b_stat,))
                K.op(dve, lambda h: h.bn_aggr(out=STAT[:, 8:10], in_=STAT[:, 0:6]), reads=(b_stat,), writes=(b_stat,))
                K.op(dve, lambda h: h.tensor_scalar(out=STAT[:, 10:11], in0=STAT[:, 9:10], scalar1=GN_EPS, scalar2=None, op0=ALU.add),
                     reads=(b_stat,), writes=(b_stat,))
                K.op(pool, lambda h: h.tensor_tensor(out=STAT[:, 11:12], in0=STAT[:, 10:11], in1=STAT[:, 16:17], op=ALU.pow),
                     reads=(b_stat, b_statc), writes=(b_stat,))
                K.op(dve, lambda h: h.tensor_tensor(out=SG, in0=SG, in1=GNW, op=ALU.mult), reads=(b_sg, b_gnw), writes=(b_sg,))
                K.op(dve, lambda h: h.tensor_scalar(out=STAT[:, 12:13], in0=STAT[:, 8:9], scalar1=STAT[:, 11:12], scalar2=-1.0,
                                                    op0=ALU.mult, op1=ALU.mult), reads=(b_stat,), writes=(b_stat,))
                K.op(act, lambda h: h.activation(out=OH, in_=ps[:, ob, :], func=AF.Identity, scale=STAT[:, 11:12], bias=STAT[:, 12:13]),
                     reads=(pbuf[ob], b_stat), writes=(b_oh,))
                K.op(dve, lambda h: h.tensor_tensor(out=OGT, in0=OH, in1=SG, op=ALU.mult), reads=(b_oh, b_sg), writes=(b_ogt,))
                for e in range(4):
                    K.op(pe, lambda h, e=e: h.transpose(out=TPB[:, e, :], in_=OGT[:, e * 128:(e + 1) * 128], identity=IDB),
                         reads=(b_ogt, b_idb), writes=(pbuf[3],), inc=(e == 3))
                oc = slice((c % 4) * 128, (c % 4 + 1) * 128)
                K.op(act, lambda h: h.activation(out=OG[:, :, oc], in_=TPB, func=AF.Copy), reads=(pbuf[3],), writes=(b_og,))
                if c % 4 == 3:
                    wout((c // 4) if sample else 2)

            import itertools
            sin_steps = itertools.chain(s_in(1, SBM, b_sbm), s_in(0, SFM, b_sfm)) if sample else iter(())
            stepA_proj(0)
            for t in range(nt):
                if t + 1 < nt:
                    stepA_proj(t + 1)
                stepA_post(t)
                next(sin_steps, None)
            for _ in sin_steps:
                pass
            for si, seq in enumerate(seqs):
                bwd(seq, si)
            for si, seq in enumerate(seqs):
                fwd(seq, si)

        def p2_load_qk(hh):
            i_slot = ring_state["next"]
            slotqk = bfv(o_ring + i_slot * 2048, 2048)
            WQK = slotqk.rearrange("p (c n) -> p c n", c=8)
            ring_state["next"] = (i_slot + 1) % NSLOT
            K.dma(pool, WQK[:, :, 0:256], win_d[j, :, hh * 256:(hh + 1) * 256].rearrange("(c p) n -> p c n", p=P),
                  ring_sems[i_slot], writes=(ring_bufs[i_slot],))
            K.dma(pool, WQK[:, :, 256:512], win_d[j, :, 1024 + hh * 256:1024 + (hh + 1) * 256].rearrange("(c p) n -> p c n", p=P),
                  ring_sems[i_slot], writes=())
            ring_bufs[i_slot].w = ("d", ring_sems[i_slot], ring_sems[i_slot].n)
            return WQK, ring_bufs[i_slot]

        def p2_load_v(hh):
            slotv, sbv = ring_load([(0, 8, 512, win_d[j, :, 2048 + hh * 512:2048 + (hh + 1) * 512].rearrange("(c p) n -> p c n", p=P))])
            return slotv.rearrange("p (c n) -> p c n", c=8), sbv

        def p2_load_g(hh):
            slotg, sbg = ring_load([(0, 8, 512, win_d[j, :, 4096 + hh * 512:4096 + (hh + 1) * 512].rearrange("(c p) n -> p c n", p=P))])
            return slotg.rearrange("p (c n) -> p c n", c=8), sbg

        def p2_load_o(hh):
            sloto, sbo = ring_load([(0, 4, 1024, wout_d[j, hh * 512:(hh + 1) * 512, :].rearrange("(c p) n -> p c n", p=P))])
            return sloto.rearrange("p (c n) -> p c n", c=4), sbo

        WQK, sbqk = p2_load_qk(0)
        WV, sbv = p2_load_v(0)
        WG, sbg = p2_load_g(0)
        all_gather(ex_e[(i, 3)], ex_g[(i, 3)], b_exe[3], b_exg[3], s_cc)
        WO, sbo = p2_load_o(0)
        for hh in range(4):
            if hh > 0:
                WQK, sbqk = p2_load_qk(hh)
                WV, sbv = p2_load_v(hh)
                WG, sbg = p2_load_g(hh)
                WO, sbo = p2_load_o(hh)
            setup_head(hh)
            K.dma(sp, GNW, gnw_d[:, j, hh, :], s_gnw, writes=(b_gnw,))
            ret_full(False, hh, WQK, sbqk, WV, sbv, WG, sbg, WO, sbo)
            ret_full(True, hh, WQK, sbqk, WV, sbv, WG, sbg, WO, sbo)

    sub = 0
    mod_prefetched = False
    for i in range(DEPTH):
        if sub < nsub or sub + 1 < nsub:
            compute_mod(i, do_matmuls=not mod_prefetched)
        mod_prefetched = False
        if sub < nsub:
            if i % 2 == 0:
                pool_layer(i)
            else:
                ret_layer(i)
        sub += 1
        if sub < nsub:
            nxt = i + 1 if (i + 1 < DEPTH and sub + 1 < nsub) else None
            mlp_layer(i, prefetch_mod=nxt)
            mod_prefetched = nxt is not None
        sub += 1

    K.barrier()
    fnw = pcv("fnw")
    o_yt = o_phase + 2048
    s_out = [K.dsem(), K.dsem()]
    out_toks = []
    YT = f32v(o_yt, 8 * 512).rearrange("p (c t) -> p c t", c=8)
    b_yt = [Buf(f"yt{c}") for c in range(8)]
    def final_group(g):
        bank = 6
        for c in range(8):
            si = rot["sq"] % 2
            rot["sq"] += 1
            K.op(act, lambda h, c=c, si=si: h.activation(out=SQ[si], in_=xview(c, g), func=AF.Square),
                 reads=(xbuf(c, g),), writes=(b_sq[si],))
            K.op(pe, lambda h, c=c, si=si: h.matmul(ps[:, bank, :], ONES, SQ[si], start=(c == 0), stop=(c == 7)),
                 reads=(b_sq[si], b_ones), writes=(pbuf[bank],), inc=True)
        ti = rot["tmp"] % 2
        rot["tmp"] += 1
        K.op(act, lambda h, ti=ti: h.activation(out=TMP[ti], in_=ps[:, bank, :], func=AF.Sqrt, bias=NORM_EPS, scale=1.0 / 1024.0),
             reads=(pbuf[bank],), writes=(b_tmp[ti],))
        K.op(dve, lambda h, ti=ti: h.reciprocal(out=RSTD, in_=TMP[ti]), reads=(b_tmp[ti],), writes=(b_rstd,))
        for c in range(8):
            K.op(dve, lambda h, c=c: h.scalar_tensor_tensor(out=YT[:, c, :], in0=xview(c, g), scalar=fnw[:, c:c + 1], in1=RSTD,
                                                             op0=ALU.mult, op1=ALU.mult),
                 reads=(xbuf(c, g), b_rstd, b_pcv), writes=(b_yt[c],))
        for tt in range(4):
            t = g * 4 + tt
            i2 = t % 2
            for half in range(2):
                bank2 = 2 * i2 + half
                for cc in range(4):
                    c = half * 4 + cc
                    K.op(pe, lambda h, c=c, cc=cc, bank2=bank2, tt=tt: h.transpose(
                        out=ps[:, bank2, cc * 128:(cc + 1) * 128], in_=YT[:, c, tt * 128:(tt + 1) * 128], identity=ident),
                        reads=(b_yt[c], b_cst), writes=(pbuf[bank2],), inc=(cc == 3))
                if half == 0:
                    K.op(act, lambda h, i2=i2, bank2=bank2: h.activation(out=IO[i2][:, 0:512], in_=ps[:, bank2, :], func=AF.Copy),
                         reads=(pbuf[bank2],), writes=(b_io[i2],))
                else:
                    K.op(dve, lambda h, i2=i2, bank2=bank2: h.tensor_copy(out=IO[i2][:, 512:1024], in_=ps[:, bank2, :]),
                         reads=(pbuf[bank2],), writes=(b_io[i2],))
            dst = ys_d[t * 128:(t + 1) * 128, :] if t < 8 else yp_d[(t - 8) * 128:(t - 7) * 128, :]
            out_toks.append(K.dma(sp, dst, IO[i2], s_out[i2], reads=(b_io[i2],)))
    for g in range(3):
        final_group(g)
    K.wait_all(sp, out_toks + [("d", ds, ds.n) for ds in K.live_dsems.values()])

    print("dry run:", K.dry_run(), file=sys.stderr)
    with nc.Block() as block:
        @block.tensor
        def _(h):
            for f in pe.prog:
                f(h)

        @block.scalar
        def _(h):
            for f in act.prog:
                f(h)

        @block.vector
        def _(h):
            for f in dve.prog:
                f(h)

        @block.gpsimd
        def _(h):
            for f in pool.prog:
                f(h)

        @block.sync
        def _(h):
            for f in sp.prog:
                f(h)
    es.close()
    return nc


_NC_CACHE = {}


def kernel(nsub=None, **inputs):
    if nsub is None:
        nsub = int(os.environ.get("KNSUB", str(2 * DEPTH)))
    inp = {k: np.asarray(v) for k, v in inputs.items()}
    if nsub not in _NC_CACHE:
        _NC_CACHE[nsub] = build_program(nsub)
    nc = _NC_CACHE[nsub]
    cst = _make_cst()
    gnw = np.ascontiguousarray(np.broadcast_to(np.asarray(inp["ret_gn_w"], np.float32)[None], (P, 2, 4, 512)))
    in_maps = []
    for core in range(8):
        b, q = core // 4, core % 4
        m = {
            "xs": np.ascontiguousarray(inp["x_sample"][b, q * TS:(q + 1) * TS]),
            "xp": np.ascontiguousarray(inp["x_prompt"][2 * core:2 * core + 2].reshape(TP, 1024)),
            "s0": np.ascontiguousarray(inp["state_ret"][b]),
            "cst": cst,
            "pcv": _make_pcv(core, inp),
            "rope": _make_rope(core),
            "gnw": gnw,
            "w_ada": inp["w_ada"], "pool_w": inp["pool_w"], "ret_w_in": inp["ret_w_in"],
            "ret_w_out": inp["ret_w_out"], "mlp_w1": inp["mlp_w1"], "mlp_w2": inp["mlp_w2"],
        }
        in_maps.append(m)
    res = run_bass_kernel_spmd(nc, in_maps, core_ids=list(range(8)))
    y_prompt = np.zeros((16, 256, 1024), np.float32)
    y_sample = np.zeros((2, 4096, 1024), np.float32)
    new_state = np.zeros((16, 2, 2, 4, 256, 512), np.float32)
    for core in range(8):
        b, q = core // 4, core % 4
        r = res.results[core]
        y_sample[b, q * TS:(q + 1) * TS] = r["ys"]
        y_prompt[2 * core:2 * core + 2] = np.asarray(r["yp"]).reshape(2, 256, 1024)
        new_state[2 * core:2 * core + 2] = r["ns"]
    return (y_prompt, y_sample, new_state)
```

```python
import os
import sys
import numpy as np
from contextlib import ExitStack
import concourse.bass as bass
import concourse.mybir as mybir
from concourse.bass_utils import run_bass_kernel_spmd

F32 = mybir.dt.float32
BF16 = mybir.dt.bfloat16
AF = mybir.ActivationFunctionType
ALU = mybir.AluOpType
AX = mybir.AxisListType

P = 128
TS = 1024
TP = 512
TT = TS + TP
DEPTH = 4
NORM_EPS = 1e-6
GN_EPS = 1e-5
SEM_M = 1024
NSLOT = 5
POOL_W = (2, 4, 8, 16)


def _cst_layout():
    o = {}
    n = 0
    for name, w in [("ident", 128), ("dpos", 128), ("dneg", 128), ("mf", 128), ("mb", 128),
                    ("xif", 128), ("xib", 128), ("colA", 1), ("colB", 1), ("colsF", 8), ("colsB", 8)]:
        o[name] = (n, w)
        n += w
    return o, n


CST_L, CST_N = _cst_layout()


def _make_cst():
    c = np.zeros((P, CST_N), np.float32)
    p = np.arange(P, dtype=np.float32)[:, None]
    i = np.arange(P, dtype=np.float32)[None, :]

    def put(name, v):
        a, w = CST_L[name]
        c[:, a:a + w] = v

    put("ident", (p == i).astype(np.float32))
    put("dpos", np.maximum(i - p, 0.0))
    put("dneg", np.maximum(p - i, 0.0))
    put("mf", (i >= p).astype(np.float32) * 0.0625)
    put("mb", (p > i).astype(np.float32) * 0.0625)
    put("xif", np.broadcast_to(i + 1.0, (P, P)))
    put("xib", np.broadcast_to(128.0 - i, (P, P)))
    put("colA", 127.0 - p)
    put("colB", p)
    cc = np.arange(8, dtype=np.float32)[None, :]
    put("colsF", 1023.0 - 128.0 * cc - p)
    put("colsB", 128.0 * cc + p)
    return c


def _pcv_layout():
    o = {}
    n = 0
    for name, w in [("bada", 4 * 48), ("nmix", 32), ("nmlp", 32), ("poolb", 16), ("pools", 16),
                    ("fnw", 8), ("cond", 16), ("decay", 16), ("xw", 18), ("pm", 8),
                    ("icr", 64), ("icc", 256), ("icp", 1024)]:
        o[name] = (n, w)
        n += w
    return o, n


PCV_L, PCV_N = _pcv_layout()


def _inv_cnt(L, w):
    t = np.arange(L)
    lo = np.clip(t - w // 2, 0, L)
    hi = np.clip(t - w // 2 + w, 0, L)
    return (1.0 / (hi - lo).astype(np.float32)).astype(np.float32)


def _make_pcv(core, inp):
    b, q = core // 4, core % 4
    v = np.zeros((P, PCV_N), np.float32)

    def put(name, arr):
        a, w = PCV_L[name]
        arr = np.asarray(arr, np.float32)
        assert arr.shape[-1] == w, (name, arr.shape, w)
        v[:, a:a + w] = arr

    def fm(x):
        x = np.asarray(x, np.float32).reshape(-1, P)
        return x.T

    put("bada", np.concatenate([fm(inp["b_ada"][i]) for i in range(4)], axis=1))
    put("nmix", np.concatenate([fm(inp["norm_mix_w"][i]) for i in range(4)], axis=1))
    put("nmlp", np.concatenate([fm(inp["norm_mlp_w"][i]) for i in range(4)], axis=1))
    put("poolb", np.concatenate([fm(inp["pool_b"][j].reshape(-1)) for j in range(2)], axis=1))
    put("pools", np.concatenate([fm(inp["pool_scale"][j]) for j in range(2)], axis=1))
    put("fnw", fm(inp["final_norm_w"]))
    cs = fm(inp["c"][b])
    cc = fm(inp["c_ctx"])
    cond = np.zeros((P, 16), np.float32)
    cond[:, 0::2] = cs
    cond[:, 1::2] = cc
    put("cond", cond)
    put("decay", np.broadcast_to(np.asarray(inp["ret_decay"], np.float32).reshape(1, 16), (P, 16)))
    xw = np.zeros(18, np.float32)
    for r in range(4):
        if r < q:
            xw[r] = 1024.0 * (q - 1 - r)
            xw[4 + r] = 1.0
        if r > q:
            xw[8 + r] = 1024.0 * (r - q - 1)
            xw[12 + r] = 1.0
    xw[16] = 1024.0 * q
    xw[17] = 1024.0 * (3 - q)
    put("xw", np.broadcast_to(xw[None, :], (P, 18)))
    pm = np.zeros(8, np.float32)
    if q > 0:
        pm[q - 1] = 1.0
    if q < 3:
        pm[4 + q + 1] = 1.0
    put("pm", np.broadcast_to(pm[None, :], (P, 8)))
    icr = np.concatenate([_inv_cnt(64, w)[q * 16:(q + 1) * 16] for w in POOL_W])
    put("icr", np.broadcast_to(icr[None, :], (P, 64)))
    icc = np.concatenate([_inv_cnt(64, w) for w in POOL_W])
    put("icc", np.broadcast_to(icc[None, :], (P, 256)))
    icp = np.concatenate([_inv_cnt(256, w) for w in POOL_W])
    put("icp", np.broadcast_to(icp[None, :], (P, 1024)))
    return v


def _make_rope(core):
    q = core % 4
    t = np.arange(TS) + q * TS
    row = (t // 64).astype(np.float32)
    col = (t % 64).astype(np.float32)
    quarter = 64
    freqs = (np.float32(10000.0) ** (-np.arange(quarter, dtype=np.float32) / np.float32(quarter))).astype(np.float32)
    ar = (row[:, None] * freqs[None, :]).astype(np.float32)
    ac = (col[:, None] * freqs[None, :]).astype(np.float32)
    cos = np.concatenate([np.cos(ar), np.cos(ac)], axis=1).astype(np.float32)
    sin = np.concatenate([np.sin(ar), np.sin(ac)], axis=1).astype(np.float32)
    tab = np.stack([cos, sin, -sin], axis=1)
    tab = tab.reshape(8, P, 3, 128).transpose(1, 0, 2, 3).reshape(P, 8, 384)
    return np.ascontiguousarray(tab)


class Eng:
    def __init__(self, name):
        self.name = name
        self.count = 0
        self.seen = {}
        self.prog = []
        self.meta = []


class DSem:
    _serial = 0

    def __init__(self, sem):
        self.sem = sem
        self.n = 0
        DSem._serial += 1
        self.uid = "dsem%d" % DSem._serial


class Buf:
    __slots__ = ("name", "w", "r", "excl")

    def __init__(self, name="", excl=False):
        self.name = name
        self.w = None
        self.r = {}
        self.excl = excl

    def add_reader(self, tok):
        k = (tok[0], tok[1].uid if tok[0] == "d" else tok[1].name)
        old = self.r.get(k)
        if old is None or old[2] < tok[2]:
            self.r[k] = tok


class Ctx:
    def __init__(self, nc, es):
        self.nc = nc
        self.es = es
        self.pe = Eng("pe")
        self.act = Eng("act")
        self.dve = Eng("dve")
        self.pool = Eng("pool")
        self.sp = Eng("sp")
        self.engs = [self.pe, self.act, self.dve, self.pool, self.sp]
        self.esems = {}
        self.free_sems = []
        self.pe_open = False
        self.mem_off = 0
        self.live_dsems = {}

    def prealloc_sems(self, n):
        for i in range(n):
            self.free_sems.append(self.es.enter_context(self.nc.semaphore(f"s{i}")))

    def new_sem(self):
        return self.free_sems.pop()

    def esem(self, eng, ep):
        k = (eng.name, ep)
        if k not in self.esems:
            self.esems[k] = self.new_sem()
        return self.esems[k]

    def dsem(self):
        return DSem(self.new_sem())

    def _waits_for(self, eng, tok, out):
        kind, src, n = tok
        if kind == "e":
            if src is eng and eng.name == "pe":
                return
            key = src.name
            if eng.seen.get(key, 0) >= n:
                return
            eng.seen[key] = n
            ep = (n - 1) // SEM_M
            out.append((self.esem(src, ep), n - ep * SEM_M))
        else:
            key = src.uid
            if eng.seen.get(key, 0) >= n:
                return
            eng.seen[key] = n
            out.append((src.sem, n))

    def _collect(self, eng, reads, writes):
        ws = []
        for b in reads:
            if b.w is not None:
                self._waits_for(eng, b.w, ws)
        for b in writes:
            if b.w is not None:
                self._waits_for(eng, b.w, ws)
            for t in b.r.values():
                self._waits_for(eng, t, ws)
        return ws

    def _finish(self, tok, reads, writes):
        for b in reads:
            b.add_reader(tok)
        for b in writes:
            b.w = tok
            b.r = {}

    def op(self, eng, fn, reads=(), writes=(), inc=True):
        if any(b.excl for b in reads):
            writes = tuple(writes) + tuple(b for b in reads if b.excl and b not in writes)
            reads = tuple(b for b in reads if not b.excl)
        if eng.name != "pe":
            assert not self.pe_open, "non-PE op inside open PE group"
        ws = self._collect(eng, reads, writes)
        if inc:
            eng.count += 1
            ep = (eng.count - 1) // SEM_M
            sem = self.esem(eng, ep)
            tok = ("e", eng, eng.count)
            if eng.name == "pe":
                self.pe_open = False
        else:
            sem = None
            tok = ("e", eng, eng.count + 1)
            self.pe_open = True

        def run(h, ws=ws, fn=fn, sem=sem):
            for s, v in ws:
                h.wait_ge(s, v)
            ins = fn(h)
            if sem is not None:
                ins.then_inc(sem, 1)

        eng.prog.append(run)
        eng.meta.append((ws, [(sem, 1)] if sem is not None else [], sys._getframe(1).f_lineno))
        self._finish(tok, reads, writes)
        return tok

    def dma(self, q, out, in_, dsem, reads=(), writes=(), inc=16, kind="dma", **kw):
        assert not self.pe_open
        ws = self._collect(q, reads, writes)
        dsem.n += inc
        tok = ("d", dsem, dsem.n)

        def run(h, ws=ws, out=out, in_=in_, kw=kw, sem=dsem.sem, inc=inc):
            for s, v in ws:
                h.wait_ge(s, v)
            h.dma_start(out=out, in_=in_, **kw).then_inc(sem, inc)

        q.prog.append(run)
        if q is self.sp:
            self.live_dsems[dsem.uid] = dsem
        q.meta.append((ws, [(dsem.sem, inc)], sys._getframe(1).f_lineno))
        self._finish(tok, reads, writes)
        return tok

    def barrier(self):
        assert not self.pe_open
        toks = [("e", e, e.count) for e in self.engs if e.count > 0]
        dtoks = [("d", ds, ds.n) for ds in self.live_dsems.values()]
        self.live_dsems = {}
        for e in (self.pe, self.act, self.dve):
            self.wait_all(e, [t for t in toks if t[1] is not e or e.name != "pe"] + dtoks)
        self.wait_all(self.sp, toks)

    def dry_run(self):
        vals = {}
        pcs = {e.name: 0 for e in self.engs}
        progress = True
        while progress:
            progress = False
            for e in self.engs:
                while pcs[e.name] < len(e.meta):
                    ws, incs, line = e.meta[pcs[e.name]]
                    if all(vals.get(id(s_), 0) >= v for s_, v in ws):
                        for s_, a in incs:
                            vals[id(s_)] = vals.get(id(s_), 0) + a
                        pcs[e.name] += 1
                        progress = True
                    else:
                        break
        stuck = [(e.name, pcs[e.name], len(e.meta)) for e in self.engs if pcs[e.name] < len(e.meta)]
        if stuck:
            msg = []
            for e in self.engs:
                if pcs[e.name] < len(e.meta):
                    ws, incs, line = e.meta[pcs[e.name]]
                    msg.append(f"{e.name} pc={pcs[e.name]}/{len(e.meta)} line={line} waits=" +
                               str([(self._semname(s_), v, vals.get(id(s_), 0)) for s_, v in ws]))
            raise RuntimeError("DEADLOCK in dry run:\n" + "\n".join(msg))
        return {e.name: len(e.meta) for e in self.engs}

    def _semname(self, sem):
        for k, v in self.esems.items():
            if v is sem:
                return str(k)
        return "dma/" + str(id(sem) % 10000)

    def wait_all(self, eng, toks):
        ws = []
        for t in toks:
            self._waits_for(eng, t, ws)

        def run(h, ws=ws):
            for s, v in ws:
                h.wait_ge(s, v)

        eng.prog.append(run)
        eng.meta.append((ws, [], sys._getframe(1).f_lineno))


def build_program(nsub=2 * DEPTH):
    nc = bass.Bass("TRN2", target_bir_lowering=False)
    es = ExitStack()
    K = Ctx(nc, es)
    pe, act, dve, pool, sp = K.pe, K.act, K.dve, K.pool, K.sp

    def dram_in(name, shape):
        return nc.dram_tensor(name, list(shape), F32, kind="ExternalInput").ap()

    def dram_out(name, shape):
        return nc.dram_tensor(name, list(shape), F32, kind="ExternalOutput").ap()

    xs_d = dram_in("xs", [TS, 1024])
    xp_d = dram_in("xp", [TP, 1024])
    s0_d = dram_in("s0", [2, 2, 4, 256, 512])
    cst_d = dram_in("cst", [P, CST_N])
    pcv_d = dram_in("pcv", [P, PCV_N])
    rope_d = dram_in("rope", [P, 8, 384])
    gnw_d = dram_in("gnw", [P, 2, 4, 512])
    wada_d = dram_in("w_ada", [4, 1024, 6144])
    poolw_d = dram_in("pool_w", [2, 4, 256, 256])
    win_d = dram_in("ret_w_in", [2, 1024, 6144])
    wout_d = dram_in("ret_w_out", [2, 2048, 1024])
    w1_d = dram_in("mlp_w1", [4, 1024, 4096])
    w2_d = dram_in("mlp_w2", [4, 4096, 1024])
    ys_d = dram_out("ys", [TS, 1024])
    yp_d = dram_out("yp", [TP, 1024])
    ns_d = dram_out("ns", [2, 2, 2, 4, 256, 512])

    HA = [POOL_W[c // 2] // 2 for c in range(8)]
    HB = [POOL_W[c // 2] // 2 - 1 for c in range(8)]
    A_OFF = [sum(HA[:c]) * 64 for c in range(8)]
    B_OFF = [sum(HB[:c]) * 64 for c in range(8)]
    A_W = sum(HA) * 64
    B_W = sum(HB) * 64
    ex_ea, ex_ga, ex_eb, ex_gb, ex_e, ex_g = {}, {}, {}, {}, {}, {}
    for i in (0, 2):
        ex_ea[i] = nc.dram_tensor(f"exea{i}", [P, A_W], F32)
        ex_ga[i] = nc.dram_tensor(f"exga{i}", [4 * P, A_W], F32)
        ex_eb[i] = nc.dram_tensor(f"exeb{i}", [P, B_W], F32)
        ex_gb[i] = nc.dram_tensor(f"exgb{i}", [4 * P, B_W], F32)
    for i in (1, 3):
        for hh in range(4):
            ex_e[(i, hh)] = nc.dram_tensor(f"exe{i}_{hh}", [P, 2048], F32)
            ex_g[(i, hh)] = nc.dram_tensor(f"exg{i}_{hh}", [4 * P, 2048], F32)

    def all_gather(src_t, dst_t, b_src, b_dst, dsem_):
        ws = K._collect(pool, (b_src,), (b_dst,))
        dsem_.n += 1
        tok = ("d", dsem_, dsem_.n)

        def run_cc(h, ws=ws, sem=dsem_.sem):
            for s_, v in ws:
                h.wait_ge(s_, v)
            h.collective_compute("AllGather", ALU.bypass, replica_groups=[[0, 1, 2, 3], [4, 5, 6, 7]],
                                 ins=[src_t.ap()], outs=[dst_t.ap()]).then_inc(sem, 1)

        pool.prog.append(run_cc)
        pool.meta.append((ws, [(dsem_.sem, 1)], sys._getframe(1).f_lineno))
        K._finish(tok, (b_src,), (b_dst,))

    MEMW = 53200
    mem = es.enter_context(nc.sbuf_tensor("mem", [P, MEMW], F32))
    ps = es.enter_context(nc.psum_tensor("ps", [P, 8, 512], F32))
    K.prealloc_sems(90)
    pbuf = [Buf(f"psum{b}", excl=True) for b in range(8)]

    def alloc(words):
        o = K.mem_off
        K.mem_off += words
        assert K.mem_off <= MEMW, K.mem_off
        return o

    def f32v(off, n):
        return mem[:, off:off + n]

    def bfv(off, nwords):
        return mem[:, off:off + nwords].bitcast(BF16)

    o_xs = alloc(8 * TS)
    o_xp = alloc(8 * TP)
    o_ht = alloc(8 * TT // 2)
    o_ring = alloc(NSLOT * 2048)
    o_cst = alloc(CST_N)
    o_pcv = alloc(PCV_N)
    o_mod = alloc(96 + 6 * 16 + 32)
    o_rstd = alloc(512)
    o_tmp = alloc(2 * 512)
    o_sq = alloc(2 * 256)
    o_ones = alloc(64)
    o_sc = alloc(8)
    o_phase = K.mem_off
    PHASE_W = MEMW - o_phase

    XS = f32v(o_xs, 8 * TS).rearrange("p (c t) -> p c t", c=8)
    XP = f32v(o_xp, 8 * TP).rearrange("p (c t) -> p c t", c=8)
    HT = bfv(o_ht, 8 * TT // 2).rearrange("p (c t) -> p c t", c=8)
    CST = f32v(o_cst, CST_N)
    PCV = f32v(o_pcv, PCV_N)
    MODR = f32v(o_mod, 96).rearrange("p (f c) -> p f c", c=2)
    MODD = f32v(o_mod + 96, 96).rearrange("p (k f c) -> p k f c", k=6, c=2)
    POOLD = f32v(o_mod + 192, 32).rearrange("p (k f c) -> p k f c", k=2, c=2)
    RSTD = f32v(o_rstd, 512)
    TMP = [f32v(o_tmp + i * 512, 512) for i in range(2)]
    SQ = [bfv(o_sq + i * 256, 256) for i in range(2)]
    ONES = bfv(o_ones, 64)
    SC = bfv(o_sc, 8).rearrange("p (k c) -> p k c", c=2)

    b_xs = [[Buf(f"xs{c}_{g}") for g in range(2)] for c in range(8)]
    b_xp = [Buf(f"xp{c}") for c in range(8)]
    b_ht = [[Buf(f"ht{c}_{g}") for g in range(3)] for c in range(8)]
    b_cst, b_pcv, b_modr, b_modd, b_poold = Buf("cst"), Buf("pcv"), Buf("modr"), Buf("modd"), Buf("poold")
    b_rstd = Buf("rstd")
    b_tmp = [Buf("tmp0"), Buf("tmp1")]
    b_sq = [Buf("sq0"), Buf("sq1")]
    b_ones, b_sc = Buf("ones"), Buf("sc")

    def cst(name):
        a, w = CST_L[name]
        return CST[:, a:a + w]

    def pcv(name):
        a, w = PCV_L[name]
        return PCV[:, a:a + w]

    def xview(c, g):
        if g < 2:
            return XS[:, c, g * 512:(g + 1) * 512]
        return XP[:, c, :]

    def xbuf(c, g):
        return b_xs[c][g] if g < 2 else b_xp[c]

    def htview(c, g):
        return HT[:, c, g * 512:(g + 1) * 512]

    rot = {"tmp": 0, "sq": 0, "pb": 0}

    ring_bufs = [Buf(f"ring{i}") for i in range(NSLOT)]
    ring_sems = [K.dsem() for _ in range(NSLOT)]
    ring_state = {"next": 0}

    def ring_load(parts):
        i = ring_state["next"]
        ring_state["next"] = (i + 1) % NSLOT
        slot = bfv(o_ring + i * 2048, 2048)
        first = True
        for (eo, c, n, src) in parts:
            dst = slot[:, eo:eo + c * n].rearrange("p (c n) -> p c n", c=c)
            K.dma(pool, dst, src, ring_sems[i], reads=(), writes=(ring_bufs[i],) if first else ())
            if not first:
                ring_bufs[i].w = ("d", ring_sems[i], ring_sems[i].n)
            first = False
        return slot, ring_bufs[i]

    s_misc = K.dsem()
    K.dma(sp, CST, cst_d, s_misc, writes=(b_cst,))
    s_misc2 = K.dsem()
    K.dma(sp, PCV, pcv_d, s_misc2, writes=(b_pcv,))
    K.op(dve, lambda h: h.memset(ONES, 1.0), writes=(b_ones,))

    K.op(act, lambda h: h.activation(out=SC.rearrange("p k c -> p (k c)"), in_=pcv("cond"), func=AF.Silu),
         reads=(b_pcv,), writes=(b_sc,))

    o_io = o_phase
    IO = [f32v(o_io + i * 1024, 1024) for i in range(2)]
    b_io = [Buf("io0"), Buf("io1")]
    s_io = [K.dsem(), K.dsem()]
    ident = cst("ident")

    def load_x_tile(t):
        i = t % 2
        src = xs_d[t * 128:(t + 1) * 128, :] if t < 8 else xp_d[(t - 8) * 128:(t - 7) * 128, :]
        K.dma(sp, IO[i], src, s_io[i], writes=(b_io[i],))
        for half in range(2):
            bank = 2 * i + half
            for cc in range(4):
                c = half * 4 + cc
                K.op(pe, lambda h, c=c, cc=cc, bank=bank, i=i: h.transpose(
                    out=ps[:, bank, cc * 128:(cc + 1) * 128], in_=IO[i][:, c * 128:(c + 1) * 128], identity=ident),
                    reads=(b_io[i], b_cst), writes=(pbuf[bank],), inc=(cc == 3))
            if t < 8:
                dst = XS[:, half * 4:(half + 1) * 4, t * 128:(t + 1) * 128]
                wb = [b_xs[c][t // 4] for c in range(half * 4, half * 4 + 4)]
            else:
                dst = XP[:, half * 4:(half + 1) * 4, (t - 8) * 128:(t - 7) * 128]
                wb = [b_xp[c] for c in range(half * 4, half * 4 + 4)]
            eng = act if half == 0 else dve
            src_ps = ps[:, bank, :].rearrange("p (c n) -> p c n", c=4)
            if eng is act:
                K.op(act, lambda h, dst=dst, src_ps=src_ps: h.activation(out=dst, in_=src_ps, func=AF.Copy),
                     reads=(pbuf[bank],), writes=wb)
            else:
                K.op(dve, lambda h, dst=dst, src_ps=src_ps: h.tensor_copy(out=dst, in_=src_ps),
                     reads=(pbuf[bank],), writes=wb)

    for t in range(12):
        load_x_tile(t)

    def mod_matmuls(i, blks):
        for blk in blks:
            slot, sb = ring_load([(0, 8, 512, wada_d[i, :, blk * 512:(blk + 1) * 512].rearrange("(c p) n -> p c n", p=P))])
            W = slot.rearrange("p (c n) -> p c n", c=8)
            for fcl in range(4):
                fc = blk * 4 + fcl
                for k in range(8):
                    K.op(pe, lambda h, W=W, k=k, fcl=fcl, fc=fc: h.matmul(
                        ps[:, 7, fc * 2:fc * 2 + 2], W[:, k, fcl * 128:(fcl + 1) * 128], SC[:, k, :],
                        start=(k == 0), stop=(k == 7)),
                        reads=(sb, b_sc), writes=(pbuf[7],), inc=(k == 7 and fcl == 3))

    def compute_mod(i, do_matmuls=True):
        if do_matmuls:
            mod_matmuls(i, range(12))
        bada = pcv("bada")[:, i * 48:(i + 1) * 48]
        K.op(dve, lambda h: h.tensor_tensor(
            out=MODR, in0=ps[:, 7, 0:96].rearrange("p (f c) -> p f c", c=2),
            in1=bada.unsqueeze(2).broadcast_to([P, 48, 2]), op=ALU.add),
            reads=(pbuf[7], b_pcv), writes=(b_modr,))
        nmix = pcv("nmix")[:, i * 8:(i + 1) * 8].unsqueeze(2).broadcast_to([P, 8, 2])
        nmlp = pcv("nmlp")[:, i * 8:(i + 1) * 8].unsqueeze(2).broadcast_to([P, 8, 2])
        K.op(dve, lambda h: h.scalar_tensor_tensor(out=MODD[:, 0], in0=MODR[:, 8:16, :], scalar=1.0, in1=nmix,
                                                   op0=ALU.add, op1=ALU.mult),
             reads=(b_modr, b_pcv), writes=(b_modd,))
        K.op(dve, lambda h: h.tensor_copy(out=MODD[:, 1], in_=MODR[:, 0:8, :]), reads=(b_modr,), writes=(b_modd,))
        K.op(dve, lambda h: h.tensor_copy(out=MODD[:, 2], in_=MODR[:, 16:24, :]), reads=(b_modr,), writes=(b_modd,))
        K.op(dve, lambda h: h.scalar_tensor_tensor(out=MODD[:, 3], in0=MODR[:, 32:40, :], scalar=1.0, in1=nmlp,
                                                   op0=ALU.add, op1=ALU.mult),
             reads=(b_modr, b_pcv), writes=(b_modd,))
        K.op(dve, lambda h: h.tensor_copy(out=MODD[:, 4], in_=MODR[:, 24:32, :]), reads=(b_modr,), writes=(b_modd,))
        K.op(dve, lambda h: h.tensor_copy(out=MODD[:, 5], in_=MODR[:, 40:48, :]), reads=(b_modr,), writes=(b_modd,))
        if i % 2 == 0:
            j = i // 2
            psc = pcv("pools")[:, j * 8:(j + 1) * 8].unsqueeze(2).broadcast_to([P, 8, 2])
            pbb = pcv("poolb")[:, j * 8:(j + 1) * 8].unsqueeze(2).broadcast_to([P, 8, 2])
            K.op(dve, lambda h: h.tensor_tensor(out=POOLD[:, 0], in0=MODR[:, 16:24, :], in1=psc, op=ALU.mult),
                 reads=(b_modr, b_pcv), writes=(b_poold,))
            K.op(dve, lambda h: h.tensor_tensor(out=POOLD[:, 1], in0=POOLD[:, 0], in1=pbb, op=ALU.mult),
                 reads=(b_poold, b_pcv), writes=(b_poold,))

    def mcol(k, c, cond):
        return MODD[:, k, c, cond:cond + 1]

    def norm_group(g, ka, kb, out_fn):
        cond = 0 if g < 2 else 1
        bank = 6
        for c in range(8):
            si = rot["sq"] % 2
            rot["sq"] += 1
            K.op(act, lambda h, c=c, si=si: h.activation(out=SQ[si], in_=xview(c, g), func=AF.Square),
                 reads=(xbuf(c, g),), writes=(b_sq[si],))
            K.op(pe, lambda h, c=c, si=si: h.matmul(ps[:, bank, :], ONES, SQ[si], start=(c == 0), stop=(c == 7)),
                 reads=(b_sq[si], b_ones), writes=(pbuf[bank],), inc=True)
        ti = rot["tmp"] % 2
        rot["tmp"] += 1
        K.op(act, lambda h, ti=ti: h.activation(out=TMP[ti], in_=ps[:, bank, :], func=AF.Sqrt, bias=NORM_EPS, scale=1.0 / 1024.0),
             reads=(pbuf[bank],), writes=(b_tmp[ti],))
        K.op(dve, lambda h, ti=ti: h.reciprocal(out=RSTD, in_=TMP[ti]), reads=(b_tmp[ti],), writes=(b_rstd,))
        for c in range(8):
            ti = rot["tmp"] % 2
            rot["tmp"] += 1
            K.op(dve, lambda h, c=c, ti=ti: h.tensor_tensor(out=TMP[ti], in0=xview(c, g), in1=RSTD, op=ALU.mult),
                 reads=(xbuf(c, g), b_rstd), writes=(b_tmp[ti],))
            dst, wb = out_fn(c)
            K.op(act, lambda h, c=c, ti=ti, dst=dst: h.activation(out=dst, in_=TMP[ti], func=AF.Identity,
                                                                  scale=mcol(ka, c, cond), bias=mcol(kb, c, cond)),
                 reads=(b_tmp[ti], b_modd), writes=wb)

    def norm_to_ht(g, ka, kb):
        norm_group(g, ka, kb, lambda c: (htview(c, g), (b_ht[c][g],)))

    def mlp_layer(i, prefetch_mod=None):
        K.barrier()
        for g in range(3):
            norm_to_ht(g, 3, 4)
        o_h1 = o_phase
        H1 = bfv(o_h1, 8 * TT // 2).rearrange("p (f t) -> p f t", f=8)
        b_h1 = [[Buf(f"h1_{f}_{g}") for g in range(3)] for f in range(8)]
        o_sqf = o_h1 + 8 * TT // 2
        SQF = [f32v(o_sqf + k * 512, 512) for k in range(2)]
        b_sqf = [Buf("sqf0"), Buf("sqf1")]
        for q in range(4):
            w1s = []
            for hb in range(2):
                c0 = q * 1024 + hb * 512
                slot, sb = ring_load([(0, 8, 512, w1_d[i, :, c0:c0 + 512].rearrange("(c p) n -> p c n", p=P))])
                w1s.append((slot.rearrange("p (c n) -> p c n", c=8), sb))
            for fc in range(8):
                W, sb = w1s[fc // 4]
                fl = fc % 4
                for g in range(3):
                    bank = rot["pb"] % 4
                    rot["pb"] += 1
                    for k in range(8):
                        K.op(pe, lambda h, W=W, k=k, fl=fl, g=g, bank=bank: h.matmul(
                            ps[:, bank, :], W[:, k, fl * 128:(fl + 1) * 128], htview(k, g), start=(k == 0), stop=(k == 7)),
                            reads=(sb, b_ht[k][g]), writes=(pbuf[bank],), inc=(k == 7))
                    si = rot["sq"] % 2
                    rot["sq"] += 1
                    K.op(act, lambda h, bank=bank, si=si: h.activation(out=SQF[si], in_=ps[:, bank, :], func=AF.Square),
                         reads=(pbuf[bank],), writes=(b_sqf[si],))
                    K.op(dve, lambda h, bank=bank, si=si, fc=fc, g=g: h.scalar_tensor_tensor(
                        out=H1[:, fc, g * 512:(g + 1) * 512], in0=ps[:, bank, :], scalar=0.0, in1=SQF[si],
                        op0=ALU.is_gt, op1=ALU.mult),
                        reads=(pbuf[bank], b_sqf[si]), writes=(b_h1[fc][g],))
            w2s = []
            for hb in range(2):
                r0 = q * 1024 + hb * 512
                slot, sb = ring_load([(0, 4, 1024, w2_d[i, r0:r0 + 512, :].rearrange("(c p) n -> p c n", p=P))])
                w2s.append((slot.rearrange("p (c n) -> p c n", c=4), sb))
            for dm in range(8):
                for g in range(3):
                    cond = 0 if g < 2 else 1
                    bank = 4 + rot["pb"] % 2
                    rot["pb"] += 1
                    for fc in range(8):
                        W, sb = w2s[fc // 4]
                        K.op(pe, lambda h, W=W, fc=fc, dm=dm, g=g, bank=bank: h.matmul(
                            ps[:, bank, :], W[:, fc % 4, dm * 128:(dm + 1) * 128], H1[:, fc, g * 512:(g + 1) * 512],
                            start=(fc == 0), stop=(fc == 7)),
                            reads=(sb, b_h1[fc][g]), writes=(pbuf[bank],), inc=(fc == 7))
                    K.op(dve, lambda h, dm=dm, g=g, bank=bank, cond=cond: h.scalar_tensor_tensor(
                        out=xview(dm, g), in0=ps[:, bank, :], scalar=mcol(5, dm, cond), in1=xview(dm, g),
                        op0=ALU.mult, op1=ALU.add),
                        reads=(pbuf[bank], b_modd, xbuf(dm, g)), writes=(xbuf(dm, g),))
            if prefetch_mod is not None:
                mod_matmuls(prefetch_mod, range(3 * q, 3 * q + 3))

    def pool_layer(i):
        K.barrier()
        j = i // 2
        o = o_phase
        o_hf = o
        o += 8 * TS
        o_pa = o
        o += 2 * 272
        o_pb = o
        o += 2 * 272
        o_rowa = o
        o += 32 * 64
        o_rowb = o
        o += 32 * 64
        o_cola = o
        o += 8 * 80
        o_colb = o
        o += 8 * 80
        o_hal = o
        o += 4 * 8 * 64
        o_evt = o
        o += 512
        assert o <= MEMW, o
        HFS = f32v(o_hf, 8 * TS).rearrange("p (c r w) -> p c r w", c=8, r=16)
        HFPc = [HT[:, c, 0:1024].bitcast(F32).rearrange("p (s t) -> p s t", s=2) for c in range(8)]
        b_hfs = [Buf(f"hfs{c}") for c in range(8)]
        ROWA = f32v(o_rowa, 2048).rearrange("p (r w) -> p r w", w=64)
        ROWB = f32v(o_rowb, 2048).rearrange("p (r w) -> p r w", w=64)
        COLA = f32v(o_cola, 640).rearrange("p (r w) -> p r w", w=80)
        COLB = f32v(o_colb, 640).rearrange("p (r w) -> p r w", w=80)
        HAL = f32v(o_hal, 2048).rearrange("p (k r w) -> p k r w", k=4, w=64)
        PA = f32v(o_pa, 544).rearrange("p (s t) -> p s t", s=2)
        PB = f32v(o_pb, 544).rearrange("p (s t) -> p s t", s=2)
        EVT = f32v(o_evt, 512)
        b_rowa, b_rowb, b_cola, b_colb, b_hal, b_pa, b_pb, b_evt = (Buf(n) for n in
                                                                    ("rowa", "rowb", "cola", "colb", "hal", "pa", "pb", "evt"))
        b_exe, b_exg, b_exe2, b_exg2 = Buf("exe"), Buf("exg"), Buf("exe2"), Buf("exg2")
        s_ex = K.dsem()
        s_ex2 = K.dsem()
        s_hal = K.dsem()
        s_cc = K.dsem()
        STEPS = [(1, 0), (1, 1), (2, 2), (4, 4)]
        NLEV = {2: 1, 4: 2, 8: 3, 16: 4}
        icp = pcv("icp")
        icr = pcv("icr")
        icc = pcv("icc")
        pm = pcv("pm")

        def dbl_last(cur, cb, outs, w, length):
            lo, hi = 0, length
            for lev in range(NLEV[w]):
                sm, spp = STEPS[lev]
                nlo, nhi = lo + sm, hi - spp
                dst, db = outs[lev % 2]
                K.op(dve, lambda h, dst=dst, cur=cur, nlo=nlo, nhi=nhi, sm=sm, spp=spp: h.tensor_tensor(
                    out=dst[:, :, nlo:nhi], in0=cur[:, :, nlo - sm:nhi - sm], in1=cur[:, :, nlo + spp:nhi + spp], op=ALU.add),
                    reads=(cb,), writes=(db,))
                cur, cb, lo, hi = dst, db, nlo, nhi
            return cur, cb

        def dbl_mid(cur, cb, outs, w, length):
            lo, hi = 0, length
            for lev in range(NLEV[w]):
                sm, spp = STEPS[lev]
                nlo, nhi = lo + sm, hi - spp
                dst, db = outs[lev % 2]
                K.op(dve, lambda h, dst=dst, cur=cur, nlo=nlo, nhi=nhi, sm=sm, spp=spp: h.tensor_tensor(
                    out=dst[:, nlo:nhi, :], in0=cur[:, nlo - sm:nhi - sm, :], in1=cur[:, nlo + spp:nhi + spp, :], op=ALU.add),
                    reads=(cb,), writes=(db,))
                cur, cb, lo, hi = dst, db, nlo, nhi
            return cur, cb

        def prompt_chunk(c):
            grp = c // 2
            w = POOL_W[grp]
            K.op(dve, lambda h: h.memset(PA, 0.0), writes=(b_pa,))
            K.op(dve, lambda h: h.tensor_copy(out=PA[:, :, 8:264], in_=HFPc[c]), reads=(b_ht[c][0], b_ht[c][1]), writes=(b_pa,))
            cur, cb = dbl_last(PA, b_pa, [(PB, b_pb), (PA, b_pa)], w, 272)
            ev = EVT.rearrange("p (s t) -> p s t", s=2)
            K.op(dve, lambda h: h.tensor_tensor(
                out=ev, in0=cur[:, :, 8:264], in1=icp[:, grp * 256:(grp + 1) * 256].unsqueeze(1).broadcast_to([P, 2, 256]),
                op=ALU.mult), reads=(cb, b_pcv), writes=(b_evt,))
            K.op(dve, lambda h: h.tensor_tensor(
                out=HT[:, c, 1024:1536].rearrange("p (s t) -> p s t", s=2), in0=ev, in1=HFPc[c], op=ALU.subtract),
                reads=(b_evt, b_ht[c][0], b_ht[c][1]), writes=(b_ht[c][2],))

        def sample_chunk(c):
            grp = c // 2
            w = POOL_W[grp]
            ha, hb = w // 2, w // 2 - 1
            K.op(act, lambda h: h.activation(out=ROWA[:, 8:24, :], in_=HFS[:, c], func=AF.Copy),
                 reads=(b_hfs[c],), writes=(b_rowa,))
            K.dma(sp, HAL[:, :, 0:ha, :].rearrange("p k r w -> p k (r w)"), GA[:, :, A_OFF[c]:A_OFF[c] + ha * 64], s_hal,
                  reads=(b_exg,), writes=(b_hal,))
            K.op(dve, lambda h: h.tensor_scalar(out=ROWA[:, 8 - ha:8, :], in0=HAL[:, 0, 0:ha, :], scalar1=pm[:, 0:1], scalar2=None,
                                                op0=ALU.mult), reads=(b_hal, b_pcv), writes=(b_rowa,))
            for k in range(1, 4):
                K.op(dve, lambda h, k=k: h.scalar_tensor_tensor(
                    out=ROWA[:, 8 - ha:8, :], in0=HAL[:, k, 0:ha, :], scalar=pm[:, k:k + 1], in1=ROWA[:, 8 - ha:8, :],
                    op0=ALU.mult, op1=ALU.add), reads=(b_hal, b_pcv, b_rowa), writes=(b_rowa,))
            if hb > 0:
                K.dma(sp, HAL[:, :, 0:hb, :].rearrange("p k r w -> p k (r w)"), GB[:, :, B_OFF[c]:B_OFF[c] + hb * 64], s_hal,
                      reads=(b_exg2,), writes=(b_hal,))
                K.op(dve, lambda h: h.tensor_scalar(out=ROWA[:, 24:24 + hb, :], in0=HAL[:, 0, 0:hb, :], scalar1=pm[:, 4:5], scalar2=None,
                                                    op0=ALU.mult), reads=(b_hal, b_pcv), writes=(b_rowa,))
                for k in range(1, 4):
                    K.op(dve, lambda h, k=k: h.scalar_tensor_tensor(
                        out=ROWA[:, 24:24 + hb, :], in0=HAL[:, k, 0:hb, :], scalar=pm[:, 4 + k:5 + k], in1=ROWA[:, 24:24 + hb, :],
                        op0=ALU.mult, op1=ALU.add), reads=(b_hal, b_pcv, b_rowa), writes=(b_rowa,))
            cur, cb = dbl_mid(ROWA, b_rowa, [(ROWB, b_rowb), (ROWA, b_rowa)], w, 32)
            for g in range(2):
                K.op(dve, lambda h: h.memset(COLA, 0.0), writes=(b_cola,))
                K.op(dve, lambda h, g=g, cur=cur: h.tensor_tensor(
                    out=COLA[:, :, 8:72], in0=cur[:, 8 + g * 8:16 + g * 8, :],
                    in1=icr[:, grp * 16 + g * 8:grp * 16 + g * 8 + 8].unsqueeze(2).broadcast_to([P, 8, 64]), op=ALU.mult),
                    reads=(cb, b_pcv), writes=(b_cola,))
                c2, c2b = dbl_last(COLA, b_cola, [(COLB, b_colb), (COLA, b_cola)], w, 80)
                ev = EVT.rearrange("p (r w) -> p r w", w=64)
                K.op(dve, lambda h, c2=c2: h.tensor_tensor(
                    out=ev, in0=c2[:, :, 8:72],
                    in1=icc[:, grp * 64:(grp + 1) * 64].unsqueeze(1).broadcast_to([P, 8, 64]), op=ALU.mult),
                    reads=(c2b, b_pcv), writes=(b_evt,))
                K.op(dve, lambda h, g=g: h.tensor_tensor(
                    out=HT[:, c, g * 512:(g + 1) * 512].rearrange("p (r w) -> p r w", w=64), in0=ev,
                    in1=HFS[:, c, g * 8:(g + 1) * 8, :], op=ALU.subtract),
                    reads=(b_evt, b_hfs[c]), writes=(b_ht[c][g],))

        def linear_group(g):
            cond = 0 if g < 2 else 1
            for fo in range(8):
                grp = fo // 2
                bank = rot["pb"] % 4
                rot["pb"] += 1
                for kk in range(2):
                    fi = grp * 2 + kk
                    K.op(pe, lambda h, fi=fi, fo=fo, bank=bank, kk=kk: h.matmul(
                        ps[:, bank, :], PW[:, fi, (fo % 2) * 128:(fo % 2 + 1) * 128], htview(fi, g), start=(kk == 0), stop=(kk == 1)),
                        reads=(sbw, b_ht[fi][g]), writes=(pbuf[bank],), inc=(kk == 1))
                ti = rot["tmp"] % 2
                rot["tmp"] += 1
                K.op(act, lambda h, bank=bank, ti=ti, fo=fo: h.activation(
                    out=TMP[ti], in_=ps[:, bank, :], func=AF.Identity,
                    scale=POOLD[:, 0, fo, cond:cond + 1], bias=POOLD[:, 1, fo, cond:cond + 1]),
                    reads=(pbuf[bank], b_poold), writes=(b_tmp[ti],))
                K.op(dve, lambda h, ti=ti, fo=fo: h.tensor_tensor(out=xview(fo, g), in0=xview(fo, g), in1=TMP[ti], op=ALU.add),
                     reads=(b_tmp[ti], xbuf(fo, g)), writes=(xbuf(fo, g),))

        slot, sbw = ring_load([(0, 8, 256, poolw_d[j].rearrange("g (c p) n -> p (g c) n", p=P))])
        PW = slot[:, 0:2048].rearrange("p (c n) -> p c n", c=8)

        for g in range(2):
            norm_group(g, 0, 1, lambda c, g=g: (HFS[:, c, g * 8:(g + 1) * 8, :].rearrange("p r w -> p (r w)"), (b_hfs[c],)))
        EA, EB = ex_ea[i].ap(), ex_eb[i].ap()
        for c in range(8):
            ha, hb = HA[c], HB[c]
            K.dma(sp, EA[:, A_OFF[c]:A_OFF[c] + ha * 64], HFS[:, c, 16 - ha:16, :].rearrange("p r w -> p (r w)"), s_ex,
                  reads=(b_hfs[c],), writes=())
        b_exe.w = ("d", s_ex, s_ex.n)
        for c in range(8):
            ha, hb = HA[c], HB[c]
            if hb > 0:
                K.dma(sp, EB[:, B_OFF[c]:B_OFF[c] + hb * 64], HFS[:, c, 0:hb, :].rearrange("p r w -> p (r w)"), s_ex2,
                      reads=(b_hfs[c],), writes=())
        b_exe2.w = ("d", s_ex2, s_ex2.n)
        all_gather(ex_ea[i], ex_ga[i], b_exe, b_exg, s_cc)
        all_gather(ex_eb[i], ex_gb[i], b_exe2, b_exg2, s_cc)
        GA = ex_ga[i].ap().rearrange("(k p) n -> p k n", p=P)
        GB = ex_gb[i].ap().rearrange("(k p) n -> p k n", p=P)
        norm_group(2, 0, 1, lambda c: (HFPc[c].rearrange("p s t -> p (s t)"), (b_ht[c][0], b_ht[c][1])))
        for c in range(8):
            prompt_chunk(c)
        linear_group(2)
        K.op(dve, lambda h: h.memset(ROWA, 0.0), writes=(b_rowa,))
        K.op(dve, lambda h: h.memset(ROWB, 0.0), writes=(b_rowb,))
        for c in range(8):
            sample_chunk(c)
        for g in range(2):
            linear_group(g)

    def ret_layer(i):
        K.barrier()
        j = i // 2
        for g in range(3):
            norm_to_ht(g, 0, 1)
        oo = [o_phase]

        def al(n):
            r_ = oo[0]
            oo[0] += n
            assert oo[0] <= MEMW, oo[0]
            return r_

        o_rope = al(2 * 384)
        o_dec = al(384)
        o_cols = al(256)
        o_lg = al(16)
        o_idb = al(64)
        o_qt = al(1024)
        o_kt = al(1024)
        o_ktok = al(1024)
        o_v = al(2048)
        o_ra = al(256)
        o_rb = al(256)
        o_qtok = al(2 * 128)
        o_ks = al(2 * 128)
        o_qs = al(2 * 128)
        o_st = al(2 * 64)
        o_sball = al(4096)
        o_sfm = al(1024)
        o_sbm = al(1024)
        o_sfb = al(512)
        o_sin = al(2 * 1024)
        o_oh = al(512)
        o_sg = al(512)
        o_ogt = al(256)
        o_og = al(1024)
        o_gnw = al(512)
        o_stat = al(32)

        ROPE = [f32v(o_rope + r_ * 384, 384) for r_ in range(2)]
        b_rope = [Buf("rope0"), Buf("rope1")]
        s_rope = [K.dsem(), K.dsem()]
        DEC = f32v(o_dec, 384)
        Dm, XIF, XIB = DEC[:, 0:128], DEC[:, 128:256], DEC[:, 256:384]
        COLS4 = f32v(o_cols, 256).rearrange("p (h c) -> p h c", h=4)
        b_cols = Buf("cols")
        LG = f32v(o_lg, 16)
        IDB = bfv(o_idb, 64)
        QT = bfv(o_qt, 1024).rearrange("p (c t) -> p c t", c=2)
        KT = bfv(o_kt, 1024).rearrange("p (c t) -> p c t", c=2)
        KTOK = bfv(o_ktok, 1024).rearrange("p (t d) -> p t d", t=8)
        V = bfv(o_v, 2048).rearrange("p (t e) -> p t e", t=8)
        RA = f32v(o_ra, 256)
        RB = f32v(o_rb, 256)
        QTOK = [bfv(o_qtok + r_ * 128, 128) for r_ in range(2)]
        KS = [bfv(o_ks + r_ * 128, 128) for r_ in range(2)]
        QS = [bfv(o_qs + r_ * 128, 128).rearrange("p (c t) -> p c t", c=2) for r_ in range(2)]
        ST = [bfv(o_st + r_ * 64, 64) for r_ in range(2)]
        SBALL = bfv(o_sball, 4096).rearrange("p (t c e) -> p t c e", t=8, c=2)
        SFM = f32v(o_sfm, 1024).rearrange("p (c e) -> p c e", c=2)
        SBM = f32v(o_sbm, 1024).rearrange("p (c e) -> p c e", c=2)
        SFBS = [bfv(o_sfb, 512).rearrange("p (c e) -> p c e", c=2),
                bfv(o_tmp + 512, 512).rearrange("p (c e) -> p c e", c=2)]
        SIN = [f32v(o_sin + r_ * 1024, 1024).rearrange("p (c e) -> p c e", c=2) for r_ in range(2)]
        TSTG = f32v(o_sin, 2048).rearrange("p (a e) -> p a e", a=4)
        VT = [bfv(o_v + r_ * 256, 256) for r_ in range(2)]
        OH = f32v(o_oh, 512)
        SG = f32v(o_sg, 512)
        OGT = bfv(o_ogt, 256)
        OG = bfv(o_og, 1024).rearrange("p (e t) -> p e t", e=4)
        GNW = f32v(o_gnw, 512)
        STAT = f32v(o_stat, 32)
        TPB = ps[:, 3, :].bitcast(BF16)[:, 0:512].rearrange("p (a n) -> p a n", a=4)

        b_lg, b_dec, b_idb = Buf("lg"), Buf("dec"), Buf("idb")
        b_ra, b_rb = Buf("ra"), Buf("rb")
        b_qtok = [Buf("qtok0"), Buf("qtok1")]
        b_ks = [Buf("ks0"), Buf("ks1")]
        b_qs = [Buf("qs0"), Buf("qs1")]
        b_st = [Buf("st0"), Buf("st1")]
        b_sfm, b_sbm = Buf("sfm"), Buf("sbm")
        b_statc = Buf("statc")
        K.op(dve, lambda h: h.memset(STAT[:, 16:17], -0.5), writes=(b_statc,))
        b_sfbs = [Buf("sfb0"), b_tmp[1]]
        b_sin = [Buf("sin0"), Buf("sin1")]
        b_vt = [Buf("vt0"), Buf("vt1")]
        b_oh, b_sg, b_ogt, b_og, b_gnw, b_stat = Buf("oh"), Buf("sg"), Buf("ogt"), Buf("og"), Buf("gnw"), Buf("stat")
        b_exe = [Buf(f"exe{q_}") for q_ in range(4)]
        b_exg = [Buf(f"exg{q_}") for q_ in range(4)]
        s_sin = [K.dsem(), K.dsem()]
        s_ex, s_cc, s_gnw, s_ns = K.dsem(), K.dsem(), K.dsem(), K.dsem()
        LN16 = -2.772588722239781

        dec = pcv("decay")[:, j * 8:(j + 1) * 8]
        xw = pcv("xw")
        K.op(act, lambda h: h.activation(out=LG[:, 8:16], in_=dec, func=AF.Exp, scale=-0.6931471805599453),
             reads=(b_pcv,), writes=(b_lg,))
        K.op(act, lambda h: h.activation(out=LG[:, 0:8], in_=LG[:, 8:16], func=AF.Ln, scale=-1.0, bias=1.0),
             reads=(b_lg,), writes=(b_lg,))
        K.op(dve, lambda h: h.tensor_copy(out=IDB, in_=ident), reads=(b_cst,), writes=(b_idb,))

        def aexp(out, in_, lg, bias=0.0, rd=(), wr=()):
            K.op(act, lambda h: h.activation(out=out, in_=in_, func=AF.Exp, scale=lg, bias=bias),
                 reads=(b_lg, b_cst, b_pcv) + tuple(rd), writes=tuple(wr))

        def setup_cols(hh):
            COLS = COLS4[:, hh, :]
            lgf = LG[:, hh:hh + 1]
            lgb = LG[:, 4 + hh:5 + hh]
            aexp(COLS[:, 0:1], cst("colA"), lgf, bias=LN16, wr=(b_cols,))
            aexp(COLS[:, 1:2], cst("colB"), lgb, bias=LN16, wr=(b_cols,))
            aexp(COLS[:, 2:3], cst("xib")[:, 0:1], lgf, wr=(b_cols,))
            aexp(COLS[:, 3:4], cst("xib")[:, 0:1], lgb, wr=(b_cols,))
            aexp(COLS[:, 4:12], cst("colsF"), lgf, bias=LN16, wr=(b_cols,))
            aexp(COLS[:, 12:20], cst("colsB"), lgb, bias=LN16, wr=(b_cols,))
            aexp(COLS[:, 32:36], xw[:, 0:4], lgf, wr=(b_cols,))
            aexp(COLS[:, 36:40], xw[:, 8:12], lgb, wr=(b_cols,))
            aexp(COLS[:, 28:29], xw[:, 16:17], lgf, wr=(b_cols,))
            aexp(COLS[:, 29:30], xw[:, 17:18], lgb, wr=(b_cols,))
            K.op(dve, lambda h: h.tensor_tensor(out=COLS[:, 20:24], in0=COLS[:, 32:36], in1=xw[:, 4:8], op=ALU.mult),
                 reads=(b_cols, b_pcv), writes=(b_cols,))
            K.op(dve, lambda h: h.tensor_tensor(out=COLS[:, 24:28], in0=COLS[:, 36:40], in1=xw[:, 12:16], op=ALU.mult),
                 reads=(b_cols, b_pcv), writes=(b_cols,))

        def setup_head(hh):
            lgf = LG[:, hh:hh + 1]
            lgb = LG[:, 4 + hh:5 + hh]
            T1 = TMP[0][:, 0:128]
            T2 = TMP[1][:, 0:128]
            aexp(T1, cst("dpos"), lgf, wr=(b_tmp[0],))
            aexp(T2, cst("dneg"), lgb, wr=(b_tmp[1],))
            K.op(dve, lambda h: h.tensor_tensor(out=T1, in0=T1, in1=cst("mf"), op=ALU.mult), reads=(b_tmp[0], b_cst), writes=(b_tmp[0],))
            K.op(dve, lambda h: h.tensor_tensor(out=T2, in0=T2, in1=cst("mb"), op=ALU.mult), reads=(b_tmp[1], b_cst), writes=(b_tmp[1],))
            K.op(dve, lambda h: h.tensor_tensor(out=Dm, in0=T1, in1=T2, op=ALU.add), reads=(b_tmp[0], b_tmp[1]), writes=(b_dec,))
            aexp(XIF, cst("xif"), lgf, wr=(b_dec,))
            aexp(XIB, cst("xib"), lgb, wr=(b_dec,))

        def rope_apply(src_ps, pb_, r_, out_ap, out_bufs):
            rp = ROPE[r_]
            s4 = src_ps.rearrange("p (h x f) -> p h x f", h=2, x=2)
            cos4 = rp[:, 0:128].rearrange("p (h f) -> p h f", h=2).unsqueeze(2).broadcast_to([P, 2, 2, 64])
            sin3 = rp[:, 128:256].rearrange("p (h f) -> p h f", h=2)
            nsin3 = rp[:, 256:384].rearrange("p (h f) -> p h f", h=2)
            RA4 = RA.rearrange("p (h x f) -> p h x f", h=2, x=2)
            RB4 = RB.rearrange("p (h x f) -> p h x f", h=2, x=2)
            K.op(dve, lambda h: h.tensor_tensor(out=RA4, in0=s4, in1=cos4, op=ALU.mult),
                 reads=(pb_, b_rope[r_]), writes=(b_ra,))
            K.op(dve, lambda h: h.tensor_tensor(out=RB4[:, :, 0, :], in0=s4[:, :, 1, :], in1=nsin3, op=ALU.mult),
                 reads=(pb_, b_rope[r_]), writes=(b_rb,))
            K.op(dve, lambda h: h.tensor_tensor(out=RB4[:, :, 1, :], in0=s4[:, :, 0, :], in1=sin3, op=ALU.mult),
                 reads=(pb_, b_rope[r_]), writes=(b_rb,))
            K.op(dve, lambda h: h.tensor_tensor(out=out_ap, in0=RA, in1=RB, op=ALU.add),
                 reads=(b_ra, b_rb), writes=tuple(out_bufs))

        def proj(bank, ncol, W, sbw_, colsl, htb_, last_inc=True):
            for k in range(8):
                K.op(pe, lambda h, k=k: h.matmul(ps[:, bank, 0:ncol], HT[:, k, colsl], W[:, k, :], start=(k == 0), stop=(k == 7)),
                     reads=(sbw_, htb_[k]), writes=(pbuf[bank],), inc=(k == 7))

        def phase1_loads(hh):
            slotk, sbk = ring_load([(0, 8, 256, win_d[j, :, 1024 + hh * 256:1024 + (hh + 1) * 256].rearrange("(c p) n -> p c n", p=P))])
            WK = slotk[:, 0:2048].rearrange("p (c n) -> p c n", c=8)
            slotv, sbv = ring_load([(0, 8, 512, win_d[j, :, 2048 + hh * 512:2048 + (hh + 1) * 512].rearrange("(c p) n -> p c n", p=P))])
            WV = slotv.rearrange("p (c n) -> p c n", c=8)
            return WK, sbk, WV, sbv

        def phase1_head(hh, wts, dummy=False):
            COLS = COLS4[:, hh, :]
            WK, sbk, WV, sbv = wts

            def tile_proj(t):
                r_ = t % 2
                bk, bv = (0, 1) if t % 2 == 0 else (2, 3)
                colsl = slice(t * 128, (t + 1) * 128)
                htb_ = [b_ht[k][t // 4] for k in range(8)]
                K.dma(sp, ROPE[r_], rope_d[:, t, :], s_rope[r_], writes=(b_rope[r_],))
                proj(bk, 256, WK, sbk, colsl, htb_)
                proj(bv, 512, WV, sbv, colsl, htb_)

            def tile_post(t):
                r_ = t % 2
                bk, bv = (0, 1) if t % 2 == 0 else (2, 3)
                rope_apply(ps[:, bk, 0:256], pbuf[bk], r_, RA, (b_ra,))
                K.op(dve, lambda h: h.tensor_scalar(out=KS[0], in0=RA, scalar1=COLS[:, 4 + t:5 + t], scalar2=None, op0=ALU.mult),
                     reads=(b_ra, b_dec, b_cols), writes=(b_ks[0],))
                K.op(dve, lambda h: h.tensor_scalar(out=KS[1], in0=RA, scalar1=COLS[:, 12 + t:13 + t], scalar2=None, op0=ALU.mult),
                     reads=(b_ra, b_dec, b_cols), writes=(b_ks[1],))
                K.op(act, lambda h: h.activation(out=VT[r_], in_=ps[:, bv, :], func=AF.Copy), reads=(pbuf[bv],), writes=(b_vt[r_],))
                for d in range(2):
                    for dc in range(2):
                        K.op(pe, lambda h, d=d, dc=dc: h.matmul(ps[:, 4 + d * 2 + dc, :], KS[d][:, dc * 128:(dc + 1) * 128], VT[r_],
                                                              start=(t == 0), stop=(t == 7)),
                             reads=(b_ks[d], b_vt[r_]), writes=(pbuf[4 + d * 2 + dc],), inc=(d == 1 and dc == 1))

            tile_proj(0)
            for t in range(8):
                if t + 1 < 8:
                    tile_proj(t + 1)
                tile_post(t)
            for a in range(4):
                if a % 2 == 0:
                    K.op(act, lambda h, a=a: h.activation(out=TSTG[:, a, :], in_=ps[:, 4 + a, :], func=AF.Copy),
                         reads=(pbuf[4 + a],), writes=(b_sin[a // 2],))
                else:
                    K.op(dve, lambda h, a=a: h.tensor_copy(out=TSTG[:, a, :], in_=ps[:, 4 + a, :]),
                         reads=(pbuf[4 + a],), writes=(b_sin[a // 2],))
            K.dma(sp, ex_e[(i, hh)].ap(), TSTG.rearrange("p a e -> p (a e)"), s_ex,
                  reads=(b_sin[0], b_sin[1]), writes=(b_exe[hh],))

        RSTAGE = 99
        for hh in range(4):
            setup_cols(hh)
        wts = phase1_loads(0)
        for hh in range(4):
            phase1_head(hh, wts)
            if hh + 1 < 4:
                wts = phase1_loads(hh + 1)
                all_gather(ex_e[(i, hh)], ex_g[(i, hh)], b_exe[hh], b_exg[hh], s_cc)
        def ret_full(sample, hh, WQK, sbqk, WV, sbv, WG, sbg, WO, sbo):
            COLS = COLS4[:, hh, :]
            nt = 8 if sample else 4
            tok0 = 0 if sample else 1024
            cond = 0 if sample else 1
            seqs = [list(range(8))] if sample else [[0, 1], [2, 3]]
            b_qt = [Buf(f"qt{t}") for t in range(nt)]
            b_kt = [Buf(f"kt{t}") for t in range(nt)]
            b_ktok = [Buf(f"ktok{t}") for t in range(nt)]
            b_v = [Buf(f"v{t}") for t in range(nt)]
            b_sball = [Buf(f"sball{t}") for t in range(nt)]

            def htbs(t):
                return [b_ht[k][(tok0 + t * 128) // 512] for k in range(8)]

            def stepA_proj(t):
                r_ = t % 2
                bq, bv = (0, 1) if t % 2 == 0 else (6, 7)
                colsl = slice(tok0 + t * 128, tok0 + (t + 1) * 128)
                if sample:
                    K.dma(sp, ROPE[r_], rope_d[:, t, :], s_rope[r_], writes=(b_rope[r_],))
                proj(bq, 512, WQK, sbqk, colsl, htbs(t))
                proj(bv, 512, WV, sbv, colsl, htbs(t))

            def stepA_post(t):
                r_ = t % 2
                bq, bv = (0, 1) if t % 2 == 0 else (6, 7)
                tc = slice(t * 128, (t + 1) * 128)
                if sample:
                    rope_apply(ps[:, bq, 0:256], pbuf[bq], r_, QTOK[r_], (b_qtok[r_],))
                    rope_apply(ps[:, bq, 256:512], pbuf[bq], r_, KTOK[:, t, :], (b_ktok[t],))
                else:
                    K.op(act, lambda h: h.activation(out=QTOK[r_], in_=ps[:, bq, 0:256], func=AF.Copy),
                         reads=(pbuf[bq],), writes=(b_qtok[r_],))
                    K.op(dve, lambda h: h.tensor_copy(out=KTOK[:, t, :], in_=ps[:, bq, 256:512]),
                         reads=(pbuf[bq],), writes=(b_ktok[t],))
                K.op(act, lambda h: h.activation(out=V[:, t, :], in_=ps[:, bv, :], func=AF.Copy), reads=(pbuf[bv],), writes=(b_v[t],))
                for dc in range(2):
                    K.op(pe, lambda h, dc=dc: h.transpose(out=TPB[:, dc, :], in_=QTOK[r_][:, dc * 128:(dc + 1) * 128], identity=IDB),
                         reads=(b_qtok[r_], b_idb), writes=(pbuf[3],), inc=False)
                for dc in range(2):
                    K.op(pe, lambda h, dc=dc: h.transpose(out=TPB[:, 2 + dc, :], in_=KTOK[:, t, dc * 128:(dc + 1) * 128], identity=IDB),
                         reads=(b_ktok[t], b_idb), writes=(pbuf[3],), inc=(dc == 1))
                K.op(act, lambda h: h.activation(out=QT[:, :, tc], in_=TPB[:, 0:2, :], func=AF.Copy), reads=(pbuf[3],), writes=(b_qt[t],))
                K.op(dve, lambda h: h.tensor_copy(out=KT[:, :, tc], in_=TPB[:, 2:4, :]), reads=(pbuf[3],), writes=(b_kt[t],))

            def s_in(d, SM, b_sm):
                K.dma(sp, SIN[0], s0_d[j, d, hh].rearrange("(c p) e -> p c e", p=P), s_sin[0], writes=(b_sin[0],))
                K.op(dve, lambda h: h.tensor_scalar(out=SM, in0=SIN[0], scalar1=COLS[:, 28 + d:29 + d], scalar2=None, op0=ALU.mult),
                     reads=(b_sin[0], b_dec, b_cols), writes=(b_sm,))
                yield
                for r4 in (range(0, 3) if d == 0 else range(1, 4)):
                    bi = (r4 + 1) % 2
                    srcg = ex_g[(i, hh)].ap()[r4 * 128:(r4 + 1) * 128, d * 1024:(d + 1) * 1024].rearrange(
                        "p (c e) -> p c e", c=2)
                    K.dma(sp, SIN[bi], srcg, s_sin[bi], reads=(b_exg[hh],), writes=(b_sin[bi],))
                    if os.environ.get("DEBUG_SB") and hh == 0 and d == 1:
                        K.dma(sp, ns_d[0, 1, 0, r4].rearrange("(c p) e -> p c e", p=P), SIN[bi], s_ns, reads=(b_sin[bi],))
                    K.op(dve, lambda h, bi=bi, r4=r4: h.scalar_tensor_tensor(
                        out=SM, in0=SIN[bi], scalar=COLS[:, 20 + d * 4 + r4:21 + d * 4 + r4], in1=SM, op0=ALU.mult, op1=ALU.add),
                        reads=(b_sin[bi], b_dec, b_cols, b_sm), writes=(b_sm,))
                    yield

            def wout(g):
                for dm in range(8):
                    bank = 7 if dm % 2 == 0 else 0
                    for e in range(4):
                        K.op(pe, lambda h, dm=dm, e=e, bank=bank: h.matmul(ps[:, bank, :], WO[:, e, dm * 128:(dm + 1) * 128], OG[:, e, :],
                                                                            start=(e == 0), stop=(e == 3)),
                             reads=(sbo, b_og), writes=(pbuf[bank],), inc=(e == 3))
                    K.op(dve, lambda h, dm=dm, bank=bank: h.scalar_tensor_tensor(
                        out=xview(dm, g), in0=ps[:, bank, :], scalar=mcol(2, dm, cond), in1=xview(dm, g), op0=ALU.mult, op1=ALU.add),
                        reads=(pbuf[bank], b_modd, xbuf(dm, g)), writes=(xbuf(dm, g),))

            def bwd(seq, si):
                if not sample:
                    K.op(dve, lambda h: h.memset(SBM, 0.0), writes=(b_sbm,))
                order = list(reversed(seq))

                def ksu(c, ub):
                    K.op(dve, lambda h: h.tensor_scalar(out=KS[1], in0=KTOK[:, c, :], scalar1=COLS[:, 1:2], scalar2=None, op0=ALU.mult),
                         reads=(b_ktok[c], b_dec, b_cols), writes=(b_ks[1],))
                    for dc in range(2):
                        K.op(pe, lambda h, dc=dc: h.matmul(ps[:, ub + dc, :], KS[1][:, dc * 128:(dc + 1) * 128], V[:, c, :],
                                                           start=True, stop=True),
                             reads=(b_ks[1], b_v[c]), writes=(pbuf[ub + dc],), inc=(dc == 1))

                def upd(c, ub):
                    K.op(act, lambda h: h.activation(out=SBALL[:, c], in_=SBM, func=AF.Copy), reads=(b_sbm,), writes=(b_sball[c],))
                    K.op(dve, lambda h: h.scalar_tensor_tensor(out=SBM, in0=SBM, scalar=COLS[:, 3:4], in1=ps[:, ub:ub + 2, :],
                                                               op0=ALU.mult, op1=ALU.add),
                         reads=(b_sbm, b_dec, b_cols, pbuf[ub], pbuf[ub + 1]), writes=(b_sbm,))

                ksu(order[0], 4)
                for k_, c in enumerate(order):
                    if k_ + 1 < len(order):
                        ksu(order[k_ + 1], 4 if (k_ + 1) % 2 == 0 else 6)
                    upd(c, 4 if k_ % 2 == 0 else 6)
                if sample and os.environ.get("DEBUG_SB"):
                    K.dma(sp, ns_d[1, j, 1, hh].rearrange("(c p) e -> p c e", p=P), SBM, s_ns, reads=(b_sbm,))
                if not sample:
                    K.dma(sp, ns_d[si, j, 1, hh].rearrange("(c p) e -> p c e", p=P), SBM, s_ns, reads=(b_sbm,))

            def fwd(seq, si):
                if not sample:
                    K.op(dve, lambda h: h.memset(SFM, 0.0), writes=(b_sfm,))
                K.op(act, lambda h: h.activation(out=SFBS[0], in_=SFM, func=AF.Copy), reads=(b_sfm,), writes=(b_sfbs[0],))
                prev = None
                for k_, c in enumerate(seq):
                    fwd_head(c, k_)
                    if prev is not None:
                        fwd_tail(*prev)
                    prev = (c, k_)
                fwd_tail(*prev)
                if not sample:
                    K.dma(sp, ns_d[si, j, 0, hh].rearrange("(c p) e -> p c e", p=P), SFM, s_ns, reads=(b_sfm,))

            def fwd_head(c, k_):
                r_ = c % 2
                ob = 1 if k_ % 2 == 0 else 6
                sfb_r, sfb_w = SFBS[k_ % 2], SFBS[(k_ + 1) % 2]
                bs_r, bs_w = b_sfbs[k_ % 2], b_sfbs[(k_ + 1) % 2]
                tc = slice(c * 128, (c + 1) * 128)
                K.op(dve, lambda h: h.tensor_scalar(out=KS[0], in0=KTOK[:, c, :], scalar1=COLS[:, 0:1], scalar2=None, op0=ALU.mult),
                     reads=(b_ktok[c], b_dec, b_cols), writes=(b_ks[0],))
                for dc in range(2):
                    K.op(pe, lambda h, dc=dc: h.matmul(ps[:, 2, 0:128], KT[:, dc, tc], QT[:, dc, tc], start=(dc == 0), stop=(dc == 1)),
                         reads=(b_kt[c], b_qt[c]), writes=(pbuf[2],), inc=(dc == 1))
                for dc in range(2):
                    K.op(pe, lambda h, dc=dc: h.matmul(ps[:, 4 + dc, :], KS[0][:, dc * 128:(dc + 1) * 128], V[:, c, :], start=True, stop=True),
                         reads=(b_ks[0], b_v[c]), writes=(pbuf[4 + dc],), inc=(dc == 1))
                K.op(dve, lambda h: h.tensor_tensor(out=QS[0], in0=QT[:, :, tc], in1=XIF.unsqueeze(1).broadcast_to([P, 2, 128]), op=ALU.mult),
                     reads=(b_qt[c], b_dec), writes=(b_qs[0],))
                K.op(dve, lambda h: h.tensor_tensor(out=QS[1], in0=QT[:, :, tc], in1=XIB.unsqueeze(1).broadcast_to([P, 2, 128]), op=ALU.mult),
                     reads=(b_qt[c], b_dec), writes=(b_qs[1],))
                K.op(dve, lambda h: h.tensor_tensor(out=ST[r_], in0=ps[:, 2, 0:128], in1=Dm, op=ALU.mult),
                     reads=(pbuf[2], b_dec), writes=(b_st[r_],))
                K.op(pe, lambda h: h.matmul(ps[:, ob, :], ST[r_], V[:, c, :], start=True, stop=False),
                     reads=(b_st[r_], b_v[c]), writes=(pbuf[ob],), inc=False)
                for dc in range(2):
                    K.op(pe, lambda h, dc=dc: h.matmul(ps[:, ob, :], QS[1][:, dc, :], SBALL[:, c, dc, :], start=False, stop=False),
                         reads=(b_qs[1], b_sball[c]), writes=(pbuf[ob],), inc=False)
                for dc in range(2):
                    K.op(pe, lambda h, dc=dc: h.matmul(ps[:, ob, :], QS[0][:, dc, :], sfb_r[:, dc, :], start=False, stop=(dc == 1)),
                         reads=(b_qs[0], bs_r), writes=(pbuf[ob],), inc=(dc == 1))
                K.op(dve, lambda h: h.scalar_tensor_tensor(out=SFM, in0=SFM, scalar=COLS[:, 2:3], in1=ps[:, 4:6, :],
                                                           op0=ALU.mult, op1=ALU.add),
                     reads=(b_sfm, b_dec, b_cols, pbuf[4], pbuf[5]), writes=(b_sfm,))
                K.op(act, lambda h: h.activation(out=sfb_w, in_=SFM, func=AF.Copy), reads=(b_sfm,), writes=(bs_w,))

            def fwd_tail(c, k_):
                ob = 1 if k_ % 2 == 0 else 6
                colsl = slice(tok0 + c * 128, tok0 + (c + 1) * 128)
                K.op(dve, lambda h: h.bn_stats(out=STAT[:, 0:6], in_=ps[:, ob, :]), reads=(pbuf[ob],), writes=(b_stat,))
                K.op(dve, lambda h: h.bn_aggr(out=STAT[:, 8:10], in_=STAT[:, 0:6]), reads=(b_stat,), writes=(b_stat,))
                K.op(dve, lambda h: h.tensor_scalar(out=STAT[:, 10:11], in0=STAT[:, 9:10], scalar1=GN_EPS, scalar2=None, op0=ALU.add),
                     reads=(b_stat,), writes=(b_stat,))
                K.op(pool, lambda h: h.tensor_tensor(out=STAT[:, 11:12], in0=STAT[:, 10:11], in1=STAT[:, 16:17], op=ALU.pow),
                     reads=(b_stat, b_statc), writes=(b_stat,))
                K.op(dve, lambda h: h.tensor_scalar(out=STAT[:, 12:13], in0=STAT[:, 8:9], scalar1=STAT[:, 11:12], scalar2=-1.0,
                                                    op0=ALU.mult, op1=ALU.mult), reads=(b_stat,), writes=(b_stat,))
                K.op(act, lambda h: h.activation(out=OH, in_=ps[:, ob, :], func=AF.Identity, scale=STAT[:, 11:12], bias=STAT[:, 12:13]),
                     reads=(pbuf[ob], b_stat), writes=(b_oh,))
                proj(0, 512, WG, sbg, colsl, htbs(c))
                K.op(act, lambda h: h.activation(out=SG, in_=ps[:, 0, :], func=AF.Silu), reads=(pbuf[0],), writes=(b_sg,))
                K.op(dve, lambda h: h.tensor_tensor(out=SG, in0=SG, in1=GNW, op=ALU.mult), reads=(b_sg, b_gnw), writes=(b_sg,))
                K.op(dve, lambda h: h.tensor_tensor(out=OGT, in0=OH, in1=SG, op=ALU.mult), reads=(b_oh, b_sg), writes=(b_ogt,))
                for e in range(4):
                    K.op(pe, lambda h, e=e: h.transpose(out=TPB[:, e, :], in_=OGT[:, e * 128:(e + 1) * 128], identity=IDB),
                         reads=(b_ogt, b_idb), writes=(pbuf[3],), inc=(e == 3))
                oc = slice((c % 4) * 128, (c % 4 + 1) * 128)
                K.op(act, lambda h: h.activation(out=OG[:, :, oc], in_=TPB, func=AF.Copy), reads=(pbuf[3],), writes=(b_og,))
                if c % 4 == 3:
                    wout((c // 4) if sample else 2)

            import itertools
            sin_steps = itertools.chain(s_in(1, SBM, b_sbm), s_in(0, SFM, b_sfm)) if sample else iter(())
            stepA_proj(0)
            for t in range(nt):
                if t + 1 < nt:
                    stepA_proj(t + 1)
                stepA_post(t)
                next(sin_steps, None)
            for _ in sin_steps:
                pass
            for si, seq in enumerate(seqs):
                bwd(seq, si)
            for si, seq in enumerate(seqs):
                fwd(seq, si)

        def p2_load_qk(hh):
            i_slot = ring_state["next"]
            slotqk = bfv(o_ring + i_slot * 2048, 2048)
            WQK = slotqk.rearrange("p (c n) -> p c n", c=8)
            ring_state["next"] = (i_slot + 1) % NSLOT
            K.dma(pool, WQK[:, :, 0:256], win_d[j, :, hh * 256:(hh + 1) * 256].rearrange("(c p) n -> p c n", p=P),
                  ring_sems[i_slot], writes=(ring_bufs[i_slot],))
            K.dma(pool, WQK[:, :, 256:512], win_d[j, :, 1024 + hh * 256:1024 + (hh + 1) * 256].rearrange("(c p) n -> p c n", p=P),
                  ring_sems[i_slot], writes=())
            ring_bufs[i_slot].w = ("d", ring_sems[i_slot], ring_sems[i_slot].n)
            return WQK, ring_bufs[i_slot]

        def p2_load_v(hh):
            slotv, sbv = ring_load([(0, 8, 512, win_d[j, :, 2048 + hh * 512:2048 + (hh + 1) * 512].rearrange("(c p) n -> p c n", p=P))])
            return slotv.rearrange("p (c n) -> p c n", c=8), sbv

        def p2_load_g(hh):
            slotg, sbg = ring_load([(0, 8, 512, win_d[j, :, 4096 + hh * 512:4096 + (hh + 1) * 512].rearrange("(c p) n -> p c n", p=P))])
            return slotg.rearrange("p (c n) -> p c n", c=8), sbg

        def p2_load_o(hh):
            sloto, sbo = ring_load([(0, 4, 1024, wout_d[j, hh * 512:(hh + 1) * 512, :].rearrange("(c p) n -> p c n", p=P))])
            return sloto.rearrange("p (c n) -> p c n", c=4), sbo

        WQK, sbqk = p2_load_qk(0)
        WV, sbv = p2_load_v(0)
        WG, sbg = p2_load_g(0)
        all_gather(ex_e[(i, 3)], ex_g[(i, 3)], b_exe[3], b_exg[3], s_cc)
        WO, sbo = p2_load_o(0)
        for hh in range(4):
            if hh > 0:
                WQK, sbqk = p2_load_qk(hh)
                WV, sbv = p2_load_v(hh)
                WG, sbg = p2_load_g(hh)
                WO, sbo = p2_load_o(hh)
            setup_head(hh)
            K.dma(sp, GNW, gnw_d[:, j, hh, :], s_gnw, writes=(b_gnw,))
            ret_full(False, hh, WQK, sbqk, WV, sbv, WG, sbg, WO, sbo)
            ret_full(True, hh, WQK, sbqk, WV, sbv, WG, sbg, WO, sbo)

    sub = 0
    mod_prefetched = False
    for i in range(DEPTH):
        if sub < nsub or sub + 1 < nsub:
            compute_mod(i, do_matmuls=not mod_prefetched)
        mod_prefetched = False
        if sub < nsub:
            if i % 2 == 0:
                pool_layer(i)
            else:
                ret_layer(i)
        sub += 1
        if sub < nsub:
            nxt = i + 1 if (i + 1 < DEPTH and sub + 1 < nsub) else None
            mlp_layer(i, prefetch_mod=nxt)
            mod_prefetched = nxt is not None
        sub += 1

    K.barrier()
    fnw = pcv("fnw")
    o_yt = o_phase + 2048
    s_out = [K.dsem(), K.dsem()]
    out_toks = []
    YT = f32v(o_yt, 8 * 512).rearrange("p (c t) -> p c t", c=8)
    b_yt = [Buf(f"yt{c}") for c in range(8)]
    def final_group(g):
        bank = 6
        for c in range(8):
            si = rot["sq"] % 2
            rot["sq"] += 1
            K.op(act, lambda h, c=c, si=si: h.activation(out=SQ[si], in_=xview(c, g), func=AF.Square),
                 reads=(xbuf(c, g),), writes=(b_sq[si],))
            K.op(pe, lambda h, c=c, si=si: h.matmul(ps[:, bank, :], ONES, SQ[si], start=(c == 0), stop=(c == 7)),
                 reads=(b_sq[si], b_ones), writes=(pbuf[bank],), inc=True)
        ti = rot["tmp"] % 2
        rot["tmp"] += 1
        K.op(act, lambda h, ti=ti: h.activation(out=TMP[ti], in_=ps[:, bank, :], func=AF.Sqrt, bias=NORM_EPS, scale=1.0 / 1024.0),
             reads=(pbuf[bank],), writes=(b_tmp[ti],))
        K.op(dve, lambda h, ti=ti: h.reciprocal(out=RSTD, in_=TMP[ti]), reads=(b_tmp[ti],), writes=(b_rstd,))
        for c in range(8):
            K.op(dve, lambda h, c=c: h.scalar_tensor_tensor(out=YT[:, c, :], in0=xview(c, g), scalar=fnw[:, c:c + 1], in1=RSTD,
                                                             op0=ALU.mult, op1=ALU.mult),
                 reads=(xbuf(c, g), b_rstd, b_pcv), writes=(b_yt[c],))
        for tt in range(4):
            t = g * 4 + tt
            i2 = t % 2
            for half in range(2):
                bank2 = 2 * i2 + half
                for cc in range(4):
                    c = half * 4 + cc
                    K.op(pe, lambda h, c=c, cc=cc, bank2=bank2, tt=tt: h.transpose(
                        out=ps[:, bank2, cc * 128:(cc + 1) * 128], in_=YT[:, c, tt * 128:(tt + 1) * 128], identity=ident),
                        reads=(b_yt[c], b_cst), writes=(pbuf[bank2],), inc=(cc == 3))
                if half == 0:
                    K.op(act, lambda h, i2=i2, bank2=bank2: h.activation(out=IO[i2][:, 0:512], in_=ps[:, bank2, :], func=AF.Copy),
                         reads=(pbuf[bank2],), writes=(b_io[i2],))
                else:
                    K.op(dve, lambda h, i2=i2, bank2=bank2: h.tensor_copy(out=IO[i2][:, 512:1024], in_=ps[:, bank2, :]),
                         reads=(pbuf[bank2],), writes=(b_io[i2],))
            dst = ys_d[t * 128:(t + 1) * 128, :] if t < 8 else yp_d[(t - 8) * 128:(t - 7) * 128, :]
            out_toks.append(K.dma(sp, dst, IO[i2], s_out[i2], reads=(b_io[i2],)))
    for g in range(3):
        final_group(g)
    K.wait_all(sp, out_toks + [("d", ds, ds.n) for ds in K.live_dsems.values()])

    print("dry run:", K.dry_run(), file=sys.stderr)
    with nc.Block() as block:
        @block.tensor
        def _(h):
            for f in pe.prog:
                f(h)

        @block.scalar
        def _(h):
            for f in act.prog:
                f(h)

        @block.vector
        def _(h):
            for f in dve.prog:
                f(h)

        @block.gpsimd
        def _(h):
            for f in pool.prog:
                f(h)

        @block.sync
        def _(h):
            for f in sp.prog:
                f(h)
    es.close()
    return nc


_NC_CACHE = {}


def kernel(nsub=None, **inputs):
    if nsub is None:
        nsub = int(os.environ.get("KNSUB", str(2 * DEPTH)))
    inp = {k: np.asarray(v) for k, v in inputs.items()}
    if nsub not in _NC_CACHE:
        _NC_CACHE[nsub] = build_program(nsub)
    nc = _NC_CACHE[nsub]
    cst = _make_cst()
    gnw = np.ascontiguousarray(np.broadcast_to(np.asarray(inp["ret_gn_w"], np.float32)[None], (P, 2, 4, 512)))
    in_maps = []
    for core in range(8):
        b, q = core // 4, core % 4
        m = {
            "xs": np.ascontiguousarray(inp["x_sample"][b, q * TS:(q + 1) * TS]),
            "xp": np.ascontiguousarray(inp["x_prompt"][2 * core:2 * core + 2].reshape(TP, 1024)),
            "s0": np.ascontiguousarray(inp["state_ret"][b]),
            "cst": cst,
            "pcv": _make_pcv(core, inp),
            "rope": _make_rope(core),
            "gnw": gnw,
            "w_ada": inp["w_ada"], "pool_w": inp["pool_w"], "ret_w_in": inp["ret_w_in"],
            "ret_w_out": inp["ret_w_out"], "mlp_w1": inp["mlp_w1"], "mlp_w2": inp["mlp_w2"],
        }
        in_maps.append(m)
    res = run_bass_kernel_spmd(nc, in_maps, core_ids=list(range(8)))
    y_prompt = np.zeros((16, 256, 1024), np.float32)
    y_sample = np.zeros((2, 4096, 1024), np.float32)
    new_state = np.zeros((16, 2, 2, 4, 256, 512), np.float32)
    for core in range(8):
        b, q = core // 4, core % 4
        r = res.results[core]
        y_sample[b, q * TS:(q + 1) * TS] = r["ys"]
        y_prompt[2 * core:2 * core + 2] = np.asarray(r["yp"]).reshape(2, 256, 1024)
        new_state[2 * core:2 * core + 2] = r["ns"]
    return (y_prompt, y_sample, new_state)
```

```python
import os
import sys
import numpy as np
from contextlib import ExitStack
import concourse.bass as bass
import concourse.mybir as mybir
from concourse.bass_utils import run_bass_kernel_spmd

F32 = mybir.dt.float32
BF16 = mybir.dt.bfloat16
AF = mybir.ActivationFunctionType
ALU = mybir.AluOpType
AX = mybir.AxisListType

P = 128
TS = 1024
TP = 512
TT = TS + TP
DEPTH = 4
NORM_EPS = 1e-6
GN_EPS = 1e-5
SEM_M = 1024
NSLOT = 5
POOL_W = (2, 4, 8, 16)


def _cst_layout():
    o = {}
    n = 0
    for name, w in [("ident", 128), ("dpos", 128), ("dneg", 128), ("mf", 128), ("mb", 128),
                    ("xif", 128), ("xib", 128), ("colA", 1), ("colB", 1), ("colsF", 8), ("colsB", 8)]:
        o[name] = (n, w)
        n += w
    return o, n


CST_L, CST_N = _cst_layout()


def _make_cst():
    c = np.zeros((P, CST_N), np.float32)
    p = np.arange(P, dtype=np.float32)[:, None]
    i = np.arange(P, dtype=np.float32)[None, :]

    def put(name, v):
        a, w = CST_L[name]
        c[:, a:a + w] = v

    put("ident", (p == i).astype(np.float32))
    put("dpos", np.maximum(i - p, 0.0))
    put("dneg", np.maximum(p - i, 0.0))
    put("mf", (i >= p).astype(np.float32) * 0.0625)
    put("mb", (p > i).astype(np.float32) * 0.0625)
    put("xif", np.broadcast_to(i + 1.0, (P, P)))
    put("xib", np.broadcast_to(128.0 - i, (P, P)))
    put("colA", 127.0 - p)
    put("colB", p)
    cc = np.arange(8, dtype=np.float32)[None, :]
    put("colsF", 1023.0 - 128.0 * cc - p)
    put("colsB", 128.0 * cc + p)
    return c


def _pcv_layout():
    o = {}
    n = 0
    for name, w in [("bada", 4 * 48), ("nmix", 32), ("nmlp", 32), ("poolb", 16), ("pools", 16),
                    ("fnw", 8), ("cond", 16), ("decay", 16), ("xw", 18), ("pm", 8),
                    ("icr", 64), ("icc", 256), ("icp", 1024)]:
        o[name] = (n, w)
        n += w
    return o, n


PCV_L, PCV_N = _pcv_layout()


def _inv_cnt(L, w):
    t = np.arange(L)
    lo = np.clip(t - w // 2, 0, L)
    hi = np.clip(t - w // 2 + w, 0, L)
    return (1.0 / (hi - lo).astype(np.float32)).astype(np.float32)


def _make_pcv(core, inp):
    b, q = core // 4, core % 4
    v = np.zeros((P, PCV_N), np.float32)

    def put(name, arr):
        a, w = PCV_L[name]
        arr = np.asarray(arr, np.float32)
        assert arr.shape[-1] == w, (name, arr.shape, w)
        v[:, a:a + w] = arr

    def fm(x):
        x = np.asarray(x, np.float32).reshape(-1, P)
        return x.T

    put("bada", np.concatenate([fm(inp["b_ada"][i]) for i in range(4)], axis=1))
    put("nmix", np.concatenate([fm(inp["norm_mix_w"][i]) for i in range(4)], axis=1))
    put("nmlp", np.concatenate([fm(inp["norm_mlp_w"][i]) for i in range(4)], axis=1))
    put("poolb", np.concatenate([fm(inp["pool_b"][j].reshape(-1)) for j in range(2)], axis=1))
    put("pools", np.concatenate([fm(inp["pool_scale"][j]) for j in range(2)], axis=1))
    put("fnw", fm(inp["final_norm_w"]))
    cs = fm(inp["c"][b])
    cc = fm(inp["c_ctx"])
    cond = np.zeros((P, 16), np.float32)
    cond[:, 0::2] = cs
    cond[:, 1::2] = cc
    put("cond", cond)
    put("decay", np.broadcast_to(np.asarray(inp["ret_decay"], np.float32).reshape(1, 16), (P, 16)))
    xw = np.zeros(18, np.float32)
    for r in range(4):
        if r < q:
            xw[r] = 1024.0 * (q - 1 - r)
            xw[4 + r] = 1.0
        if r > q:
            xw[8 + r] = 1024.0 * (r - q - 1)
            xw[12 + r] = 1.0
    xw[16] = 1024.0 * q
    xw[17] = 1024.0 * (3 - q)
    put("xw", np.broadcast_to(xw[None, :], (P, 18)))
    pm = np.zeros(8, np.float32)
    if q > 0:
        pm[q - 1] = 1.0
    if q < 3:
        pm[4 + q + 1] = 1.0
    put("pm", np.broadcast_to(pm[None, :], (P, 8)))
    icr = np.concatenate([_inv_cnt(64, w)[q * 16:(q + 1) * 16] for w in POOL_W])
    put("icr", np.broadcast_to(icr[None, :], (P, 64)))
    icc = np.concatenate([_inv_cnt(64, w) for w in POOL_W])
    put("icc", np.broadcast_to(icc[None, :], (P, 256)))
    icp = np.concatenate([_inv_cnt(256, w) for w in POOL_W])
    put("icp", np.broadcast_to(icp[None, :], (P, 1024)))
    return v


def _make_rope(core):
    q = core % 4
    t = np.arange(TS) + q * TS
    row = (t // 64).astype(np.float32)
    col = (t % 64).astype(np.float32)
    quarter = 64
    freqs = (np.float32(10000.0) ** (-np.arange(quarter, dtype=np.float32) / np.float32(quarter))).astype(np.float32)
    ar = (row[:, None] * freqs[None, :]).astype(np.float32)
    ac = (col[:, None] * freqs[None, :]).astype(np.float32)
    cos = np.concatenate([np.cos(ar), np.cos(ac)], axis=1).astype(np.float32)
    sin = np.concatenate([np.sin(ar), np.sin(ac)], axis=1).astype(np.float32)
    tab = np.stack([cos, sin, -sin], axis=1)
    tab = tab.reshape(8, P, 3, 128).transpose(1, 0, 2, 3).reshape(P, 8, 384)
    return np.ascontiguousarray(tab)


class Eng:
    def __init__(self, name):
        self.name = name
        self.count = 0
        self.seen = {}
        self.prog = []
        self.meta = []


class DSem:
    _serial = 0

    def __init__(self, sem):
        self.sem = sem
        self.n = 0
        DSem._serial += 1
        self.uid = "dsem%d" % DSem._serial


class Buf:
    __slots__ = ("name", "w", "r", "excl")

    def __init__(self, name="", excl=False):
        self.name = name
        self.w = None
        self.r = {}
        self.excl = excl

    def add_reader(self, tok):
        k = (tok[0], tok[1].uid if tok[0] == "d" else tok[1].name)
        old = self.r.get(k)
        if old is None or old[2] < tok[2]:
            self.r[k] = tok


class Ctx:
    def __init__(self, nc, es):
        self.nc = nc
        self.es = es
        self.pe = Eng("pe")
        self.act = Eng("act")
        self.dve = Eng("dve")
        self.pool = Eng("pool")
        self.sp = Eng("sp")
        self.engs = [self.pe, self.act, self.dve, self.pool, self.sp]
        self.esems = {}
        self.free_sems = []
        self.pe_open = False
        self.mem_off = 0
        self.live_dsems = {}

    def prealloc_sems(self, n):
        for i in range(n):
            self.free_sems.append(self.es.enter_context(self.nc.semaphore(f"s{i}")))

    def new_sem(self):
        return self.free_sems.pop()

    def esem(self, eng, ep):
        k = (eng.name, ep)
        if k not in self.esems:
            self.esems[k] = self.new_sem()
        return self.esems[k]

    def dsem(self):
        return DSem(self.new_sem())

    def _waits_for(self, eng, tok, out):
        kind, src, n = tok
        if kind == "e":
            if src is eng and eng.name == "pe":
                return
            key = src.name
            if eng.seen.get(key, 0) >= n:
                return
            eng.seen[key] = n
            ep = (n - 1) // SEM_M
            out.append((self.esem(src, ep), n - ep * SEM_M))
        else:
            key = src.uid
            if eng.seen.get(key, 0) >= n:
                return
            eng.seen[key] = n
            out.append((src.sem, n))

    def _collect(self, eng, reads, writes):
        ws = []
        for b in reads:
            if b.w is not None:
                self._waits_for(eng, b.w, ws)
        for b in writes:
            if b.w is not None:
                self._waits_for(eng, b.w, ws)
            for t in b.r.values():
                self._waits_for(eng, t, ws)
        return ws

    def _finish(self, tok, reads, writes):
        for b in reads:
            b.add_reader(tok)
        for b in writes:
            b.w = tok
            b.r = {}

    def op(self, eng, fn, reads=(), writes=(), inc=True):
        if any(b.excl for b in reads):
            writes = tuple(writes) + tuple(b for b in reads if b.excl and b not in writes)
            reads = tuple(b for b in reads if not b.excl)
        if eng.name != "pe":
            assert not self.pe_open, "non-PE op inside open PE group"
        ws = self._collect(eng, reads, writes)
        if inc:
            eng.count += 1
            ep = (eng.count - 1) // SEM_M
            sem = self.esem(eng, ep)
            tok = ("e", eng, eng.count)
            if eng.name == "pe":
                self.pe_open = False
        else:
            sem = None
            tok = ("e", eng, eng.count + 1)
            self.pe_open = True

        def run(h, ws=ws, fn=fn, sem=sem):
            for s, v in ws:
                h.wait_ge(s, v)
            ins = fn(h)
            if sem is not None:
                ins.then_inc(sem, 1)

        eng.prog.append(run)
        eng.meta.append((ws, [(sem, 1)] if sem is not None else [], sys._getframe(1).f_lineno))
        self._finish(tok, reads, writes)
        return tok

    def dma(self, q, out, in_, dsem, reads=(), writes=(), inc=16, kind="dma", **kw):
        assert not self.pe_open
        ws = self._collect(q, reads, writes)
        dsem.n += inc
        tok = ("d", dsem, dsem.n)

        def run(h, ws=ws, out=out, in_=in_, kw=kw, sem=dsem.sem, inc=inc):
            for s, v in ws:
                h.wait_ge(s, v)
            h.dma_start(out=out, in_=in_, **kw).then_inc(sem, inc)

        q.prog.append(run)
        if q is self.sp:
            self.live_dsems[dsem.uid] = dsem
        q.meta.append((ws, [(dsem.sem, inc)], sys._getframe(1).f_lineno))
        self._finish(tok, reads, writes)
        return tok

    def barrier(self):
        assert not self.pe_open
        toks = [("e", e, e.count) for e in self.engs if e.count > 0]
        dtoks = [("d", ds, ds.n) for ds in self.live_dsems.values()]
        self.live_dsems = {}
        for e in (self.pe, self.act, self.dve):
            self.wait_all(e, [t for t in toks if t[1] is not e or e.name != "pe"] + dtoks)
        self.wait_all(self.sp, toks)

    def dry_run(self):
        vals = {}
        pcs = {e.name: 0 for e in self.engs}
        progress = True
        while progress:
            progress = False
            for e in self.engs:
                while pcs[e.name] < len(e.meta):
                    ws, incs, line = e.meta[pcs[e.name]]
                    if all(vals.get(id(s_), 0) >= v for s_, v in ws):
                        for s_, a in incs:
                            vals[id(s_)] = vals.get(id(s_), 0) + a
                        pcs[e.name] += 1
                        progress = True
                    else:
                        break
        stuck = [(e.name, pcs[e.name], len(e.meta)) for e in self.engs if pcs[e.name] < len(e.meta)]
        if stuck:
            msg = []
            for e in self.engs:
                if pcs[e.name] < len(e.meta):
                    ws, incs, line = e.meta[pcs[e.name]]
                    msg.append(f"{e.name} pc={pcs[e.name]}/{len(e.meta)} line={line} waits=" +
                               str([(self._semname(s_), v, vals.get(id(s_), 0)) for s_, v in ws]))
            raise RuntimeError("DEADLOCK in dry run:\n" + "\n".join(msg))
        return {e.name: len(e.meta) for e in self.engs}

    def _semname(self, sem):
        for k, v in self.esems.items():
            if v is sem:
                return str(k)
        return "dma/" + str(id(sem) % 10000)

    def wait_all(self, eng, toks):
        ws = []
        for t in toks:
            self._waits_for(eng, t, ws)

        def run(h, ws=ws):
            for s, v in ws:
                h.wait_ge(s, v)

        eng.prog.append(run)
        eng.meta.append((ws, [], sys._getframe(1).f_lineno))


def build_program(nsub=2 * DEPTH):
    nc = bass.Bass("TRN2", target_bir_lowering=False)
    es = ExitStack()
    K = Ctx(nc, es)
    pe, act, dve, pool, sp = K.pe, K.act, K.dve, K.pool, K.sp

    def dram_in(name, shape):
        return nc.dram_tensor(name, list(shape), F32, kind="ExternalInput").ap()

    def dram_out(name, shape):
        return nc.dram_tensor(name, list(shape), F32, kind="ExternalOutput").ap()

    xs_d = dram_in("xs", [TS, 1024])
    xp_d = dram_in("xp", [TP, 1024])
    s0_d = dram_in("s0", [2, 2, 4, 256, 512])
    cst_d = dram_in("cst", [P, CST_N])
    pcv_d = dram_in("pcv", [P, PCV_N])
    rope_d = dram_in("rope", [P, 8, 384])
    gnw_d = dram_in("gnw", [P, 2, 4, 512])
    wada_d = dram_in("w_ada", [4, 1024, 6144])
    poolw_d = dram_in("pool_w", [2, 4, 256, 256])
    win_d = dram_in("ret_w_in", [2, 1024, 6144])
    wout_d = dram_in("ret_w_out", [2, 2048, 1024])
    w1_d = dram_in("mlp_w1", [4, 1024, 4096])
    w2_d = dram_in("mlp_w2", [4, 4096, 1024])
    ys_d = dram_out("ys", [TS, 1024])
    yp_d = dram_out("yp", [TP, 1024])
    ns_d = dram_out("ns", [2, 2, 2, 4, 256, 512])

    HA = [POOL_W[c // 2] // 2 for c in range(8)]
    HB = [POOL_W[c // 2] // 2 - 1 for c in range(8)]
    A_OFF = [sum(HA[:c]) * 64 for c in range(8)]
    B_OFF = [sum(HB[:c]) * 64 for c in range(8)]
    A_W = sum(HA) * 64
    B_W = sum(HB) * 64
    ex_ea, ex_ga, ex_eb, ex_gb, ex_e, ex_g = {}, {}, {}, {}, {}, {}
    for i in (0, 2):
        ex_ea[i] = nc.dram_tensor(f"exea{i}", [P, A_W], F32)
        ex_ga[i] = nc.dram_tensor(f"exga{i}", [4 * P, A_W], F32)
        ex_eb[i] = nc.dram_tensor(f"exeb{i}", [P, B_W], F32)
        ex_gb[i] = nc.dram_tensor(f"exgb{i}", [4 * P, B_W], F32)
    for i in (1, 3):
        for hh in range(4):
            ex_e[(i, hh)] = nc.dram_tensor(f"exe{i}_{hh}", [P, 2048], F32)
            ex_g[(i, hh)] = nc.dram_tensor(f"exg{i}_{hh}", [4 * P, 2048], F32)

    def all_gather(src_t, dst_t, b_src, b_dst, dsem_):
        ws = K._collect(pool, (b_src,), (b_dst,))
        dsem_.n += 1
        tok = ("d", dsem_, dsem_.n)

        def run_cc(h, ws=ws, sem=dsem_.sem):
            for s_, v in ws:
                h.wait_ge(s_, v)
            h.collective_compute("AllGather", ALU.bypass, replica_groups=[[0, 1, 2, 3], [4, 5, 6, 7]],
                                 ins=[src_t.ap()], outs=[dst_t.ap()]).then_inc(sem, 1)

        pool.prog.append(run_cc)
        pool.meta.append((ws, [(dsem_.sem, 1)], sys._getframe(1).f_lineno))
        K._finish(tok, (b_src,), (b_dst,))

    MEMW = 53200
    mem = es.enter_context(nc.sbuf_tensor("mem", [P, MEMW], F32))
    ps = es.enter_context(nc.psum_tensor("ps", [P, 8, 512], F32))
    K.prealloc_sems(90)
    pbuf = [Buf(f"psum{b}", excl=True) for b in range(8)]

    def alloc(words):
        o = K.mem_off
        K.mem_off += words
        assert K.mem_off <= MEMW, K.mem_off
        return o

    def f32v(off, n):
        return mem[:, off:off + n]

    def bfv(off, nwords):
        return mem[:, off:off + nwords].bitcast(BF16)

    o_xs = alloc(8 * TS)
    o_xp = alloc(8 * TP)
    o_ht = alloc(8 * TT // 2)
    o_ring = alloc(NSLOT * 2048)
    o_cst = alloc(CST_N)
    o_pcv = alloc(PCV_N)
    o_mod = alloc(96 + 6 * 16 + 32)
    o_rstd = alloc(512)
    o_tmp = alloc(2 * 512)
    o_sq = alloc(2 * 256)
    o_ones = alloc(64)
    o_sc = alloc(8)
    o_phase = K.mem_off
    PHASE_W = MEMW - o_phase

    XS = f32v(o_xs, 8 * TS).rearrange("p (c t) -> p c t", c=8)
    XP = f32v(o_xp, 8 * TP).rearrange("p (c t) -> p c t", c=8)
    HT = bfv(o_ht, 8 * TT // 2).rearrange("p (c t) -> p c t", c=8)
    CST = f32v(o_cst, CST_N)
    PCV = f32v(o_pcv, PCV_N)
    MODR = f32v(o_mod, 96).rearrange("p (f c) -> p f c", c=2)
    MODD = f32v(o_mod + 96, 96).rearrange("p (k f c) -> p k f c", k=6, c=2)
    POOLD = f32v(o_mod + 192, 32).rearrange("p (k f c) -> p k f c", k=2, c=2)
    RSTD = f32v(o_rstd, 512)
    TMP = [f32v(o_tmp + i * 512, 512) for i in range(2)]
    SQ = [bfv(o_sq + i * 256, 256) for i in range(2)]
    ONES = bfv(o_ones, 64)
    SC = bfv(o_sc, 8).rearrange("p (k c) -> p k c", c=2)

    b_xs = [[Buf(f"xs{c}_{g}") for g in range(2)] for c in range(8)]
    b_xp = [Buf(f"xp{c}") for c in range(8)]
    b_ht = [[Buf(f"ht{c}_{g}") for g in range(3)] for c in range(8)]
    b_cst, b_pcv, b_modr, b_modd, b_poold = Buf("cst"), Buf("pcv"), Buf("modr"), Buf("modd"), Buf("poold")
    b_rstd = Buf("rstd")
    b_tmp = [Buf("tmp0"), Buf("tmp1")]
    b_sq = [Buf("sq0"), Buf("sq1")]
    b_ones, b_sc = Buf("ones"), Buf("sc")

    def cst(name):
        a, w = CST_L[name]
        return CST[:, a:a + w]

    def pcv(name):
        a, w = PCV_L[name]
        return PCV[:, a:a + w]

    def xview(c, g):
        if g < 2:
            return XS[:, c, g * 512:(g + 1) * 512]
        return XP[:, c, :]

    def xbuf(c, g):
        return b_xs[c][g] if g < 2 else b_xp[c]

    def htview(c, g):
        return HT[:, c, g * 512:(g + 1) * 512]

    rot = {"tmp": 0, "sq": 0, "pb": 0}

    ring_bufs = [Buf(f"ring{i}") for i in range(NSLOT)]
    ring_sems = [K.dsem() for _ in range(NSLOT)]
    ring_state = {"next": 0}

    def ring_load(parts):
        i = ring_state["next"]
        ring_state["next"] = (i + 1) % NSLOT
        slot = bfv(o_ring + i * 2048, 2048)
        first = True
        for (eo, c, n, src) in parts:
            dst = slot[:, eo:eo + c * n].rearrange("p (c n) -> p c n", c=c)
            K.dma(pool, dst, src, ring_sems[i], reads=(), writes=(ring_bufs[i],) if first else ())
            if not first:
                ring_bufs[i].w = ("d", ring_sems[i], ring_sems[i].n)
            first = False
        return slot, ring_bufs[i]

    s_misc = K.dsem()
    K.dma(sp, CST, cst_d, s_misc, writes=(b_cst,))
    s_misc2 = K.dsem()
    K.dma(sp, PCV, pcv_d, s_misc2, writes=(b_pcv,))
    K.op(dve, lambda h: h.memset(ONES, 1.0), writes=(b_ones,))

    K.op(act, lambda h: h.activation(out=SC.rearrange("p k c -> p (k c)"), in_=pcv("cond"), func=AF.Silu),
         reads=(b_pcv,), writes=(b_sc,))

    o_io = o_phase
    IO = [f32v(o_io + i * 1024, 1024) for i in range(2)]
    b_io = [Buf("io0"), Buf("io1")]
    s_io = [K.dsem(), K.dsem()]
    ident = cst("ident")

    def load_x_tile(t):
        i = t % 2
        src = xs_d[t * 128:(t + 1) * 128, :] if t < 8 else xp_d[(t - 8) * 128:(t - 7) * 128, :]
        K.dma(sp, IO[i], src, s_io[i], writes=(b_io[i],))
        for half in range(2):
            bank = 2 * i + half
            for cc in range(4):
                c = half * 4 + cc
                K.op(pe, lambda h, c=c, cc=cc, bank=bank, i=i: h.transpose(
                    out=ps[:, bank, cc * 128:(cc + 1) * 128], in_=IO[i][:, c * 128:(c + 1) * 128], identity=ident),
                    reads=(b_io[i], b_cst), writes=(pbuf[bank],), inc=(cc == 3))
            if t < 8:
                dst = XS[:, half * 4:(half + 1) * 4, t * 128:(t + 1) * 128]
                wb = [b_xs[c][t // 4] for c in range(half * 4, half * 4 + 4)]
            else:
                dst = XP[:, half * 4:(half + 1) * 4, (t - 8) * 128:(t - 7) * 128]
                wb = [b_xp[c] for c in range(half * 4, half * 4 + 4)]
            eng = act if half == 0 else dve
            src_ps = ps[:, bank, :].rearrange("p (c n) -> p c n", c=4)
            if eng is act:
                K.op(act, lambda h, dst=dst, src_ps=src_ps: h.activation(out=dst, in_=src_ps, func=AF.Copy),
                     reads=(pbuf[bank],), writes=wb)
            else:
                K.op(dve, lambda h, dst=dst, src_ps=src_ps: h.tensor_copy(out=dst, in_=src_ps),
                     reads=(pbuf[bank],), writes=wb)

    for t in range(12):
        load_x_tile(t)

    def mod_matmuls(i, blks):
        for blk in blks:
            slot, sb = ring_load([(0, 8, 512, wada_d[i, :, blk * 512:(blk + 1) * 512].rearrange("(c p) n -> p c n", p=P))])
            W = slot.rearrange("p (c n) -> p c n", c=8)
            for fcl in range(4):
                fc = blk * 4 + fcl
                for k in range(8):
                    K.op(pe, lambda h, W=W, k=k, fcl=fcl, fc=fc: h.matmul(
                        ps[:, 7, fc * 2:fc * 2 + 2], W[:, k, fcl * 128:(fcl + 1) * 128], SC[:, k, :],
                        start=(k == 0), stop=(k == 7)),
                        reads=(sb, b_sc), writes=(pbuf[7],), inc=(k == 7 and fcl == 3))

    def compute_mod(i, do_matmuls=True):
        if do_matmuls:
            mod_matmuls(i, range(12))
        bada = pcv("bada")[:, i * 48:(i + 1) * 48]
        K.op(dve, lambda h: h.tensor_tensor(
            out=MODR, in0=ps[:, 7, 0:96].rearrange("p (f c) -> p f c", c=2),
            in1=bada.unsqueeze(2).broadcast_to([P, 48, 2]), op=ALU.add),
            reads=(pbuf[7], b_pcv), writes=(b_modr,))
        nmix = pcv("nmix")[:, i * 8:(i + 1) * 8].unsqueeze(2).broadcast_to([P, 8, 2])
        nmlp = pcv("nmlp")[:, i * 8:(i + 1) * 8].unsqueeze(2).broadcast_to([P, 8, 2])
        K.op(dve, lambda h: h.scalar_tensor_tensor(out=MODD[:, 0], in0=MODR[:, 8:16, :], scalar=1.0, in1=nmix,
                                                   op0=ALU.add, op1=ALU.mult),
             reads=(b_modr, b_pcv), writes=(b_modd,))
        K.op(dve, lambda h: h.tensor_copy(out=MODD[:, 1], in_=MODR[:, 0:8, :]), reads=(b_modr,), writes=(b_modd,))
        K.op(dve, lambda h: h.tensor_copy(out=MODD[:, 2], in_=MODR[:, 16:24, :]), reads=(b_modr,), writes=(b_modd,))
        K.op(dve, lambda h: h.scalar_tensor_tensor(out=MODD[:, 3], in0=MODR[:, 32:40, :], scalar=1.0, in1=nmlp,
                                                   op0=ALU.add, op1=ALU.mult),
             reads=(b_modr, b_pcv), writes=(b_modd,))
        K.op(dve, lambda h: h.tensor_copy(out=MODD[:, 4], in_=MODR[:, 24:32, :]), reads=(b_modr,), writes=(b_modd,))
        K.op(dve, lambda h: h.tensor_copy(out=MODD[:, 5], in_=MODR[:, 40:48, :]), reads=(b_modr,), writes=(b_modd,))
        if i % 2 == 0:
            j = i // 2
            psc = pcv("pools")[:, j * 8:(j + 1) * 8].unsqueeze(2).broadcast_to([P, 8, 2])
            pbb = pcv("poolb")[:, j * 8:(j + 1) * 8].unsqueeze(2).broadcast_to([P, 8, 2])
            K.op(dve, lambda h: h.tensor_tensor(out=POOLD[:, 0], in0=MODR[:, 16:24, :], in1=psc, op=ALU.mult),
                 reads=(b_modr, b_pcv), writes=(b_poold,))
            K.op(dve, lambda h: h.tensor_tensor(out=POOLD[:, 1], in0=POOLD[:, 0], in1=pbb, op=ALU.mult),
                 reads=(b_poold, b_pcv), writes=(b_poold,))

    def mcol(k, c, cond):
        return MODD[:, k, c, cond:cond + 1]

    def norm_group(g, ka, kb, out_fn):
        cond = 0 if g < 2 else 1
        bank = 6
        for c in range(8):
            si = rot["sq"] % 2
            rot["sq"] += 1
            K.op(act, lambda h, c=c, si=si: h.activation(out=SQ[si], in_=xview(c, g), func=AF.Square),
                 reads=(xbuf(c, g),), writes=(b_sq[si],))
            K.op(pe, lambda h, c=c, si=si: h.matmul(ps[:, bank, :], ONES, SQ[si], start=(c == 0), stop=(c == 7)),
                 reads=(b_sq[si], b_ones), writes=(pbuf[bank],), inc=True)
        ti = rot["tmp"] % 2
        rot["tmp"] += 1
        K.op(act, lambda h, ti=ti: h.activation(out=TMP[ti], in_=ps[:, bank, :], func=AF.Sqrt, bias=NORM_EPS, scale=1.0 / 1024.0),
             reads=(pbuf[bank],), writes=(b_tmp[ti],))
        K.op(dve, lambda h, ti=ti: h.reciprocal(out=RSTD, in_=TMP[ti]), reads=(b_tmp[ti],), writes=(b_rstd,))
        for c in range(8):
            ti = rot["tmp"] % 2
            rot["tmp"] += 1
            K.op(dve, lambda h, c=c, ti=ti: h.tensor_tensor(out=TMP[ti], in0=xview(c, g), in1=RSTD, op=ALU.mult),
                 reads=(xbuf(c, g), b_rstd), writes=(b_tmp[ti],))
            dst, wb = out_fn(c)
            K.op(act, lambda h, c=c, ti=ti, dst=dst: h.activation(out=dst, in_=TMP[ti], func=AF.Identity,
                                                                  scale=mcol(ka, c, cond), bias=mcol(kb, c, cond)),
                 reads=(b_tmp[ti], b_modd), writes=wb)

    def norm_to_ht(g, ka, kb):
        norm_group(g, ka, kb, lambda c: (htview(c, g), (b_ht[c][g],)))

    def mlp_layer(i, prefetch_mod=None):
        K.barrier()
        for g in range(3):
            norm_to_ht(g, 3, 4)
        o_h1 = o_phase
        H1 = bfv(o_h1, 8 * TT // 2).rearrange("p (f t) -> p f t", f=8)
        b_h1 = [[Buf(f"h1_{f}_{g}") for g in range(3)] for f in range(8)]
        o_sqf = o_h1 + 8 * TT // 2
        SQF = [f32v(o_sqf + k * 512, 512) for k in range(2)]
        b_sqf = [Buf("sqf0"), Buf("sqf1")]
        for q in range(4):
            w1s = []
            for hb in range(2):
                c0 = q * 1024 + hb * 512
                slot, sb = ring_load([(0, 8, 512, w1_d[i, :, c0:c0 + 512].rearrange("(c p) n -> p c n", p=P))])
                w1s.append((slot.rearrange("p (c n) -> p c n", c=8), sb))
            for fc in range(8):
                W, sb = w1s[fc // 4]
                fl = fc % 4
                for g in range(3):
                    bank = rot["pb"] % 4
                    rot["pb"] += 1
                    for k in range(8):
                        K.op(pe, lambda h, W=W, k=k, fl=fl, g=g, bank=bank: h.matmul(
                            ps[:, bank, :], W[:, k, fl * 128:(fl + 1) * 128], htview(k, g), start=(k == 0), stop=(k == 7)),
                            reads=(sb, b_ht[k][g]), writes=(pbuf[bank],), inc=(k == 7))
                    si = rot["sq"] % 2
                    rot["sq"] += 1
                    K.op(act, lambda h, bank=bank, si=si: h.activation(out=SQF[si], in_=ps[:, bank, :], func=AF.Square),
                         reads=(pbuf[bank],), writes=(b_sqf[si],))
                    K.op(dve, lambda h, bank=bank, si=si, fc=fc, g=g: h.scalar_tensor_tensor(
                        out=H1[:, fc, g * 512:(g + 1) * 512], in0=ps[:, bank, :], scalar=0.0, in1=SQF[si],
                        op0=ALU.is_gt, op1=ALU.mult),
                        reads=(pbuf[bank], b_sqf[si]), writes=(b_h1[fc][g],))
            w2s = []
            for hb in range(2):
                r0 = q * 1024 + hb * 512
                slot, sb = ring_load([(0, 4, 1024, w2_d[i, r0:r0 + 512, :].rearrange("(c p) n -> p c n", p=P))])
                w2s.append((slot.rearrange("p (c n) -> p c n", c=4), sb))
            for dm in range(8):
                for g in range(3):
                    cond = 0 if g < 2 else 1
                    bank = 4 + rot["pb"] % 2
                    rot["pb"] += 1
                    for fc in range(8):
                        W, sb = w2s[fc // 4]
                        K.op(pe, lambda h, W=W, fc=fc, dm=dm, g=g, bank=bank: h.matmul(
                            ps[:, bank, :], W[:, fc % 4, dm * 128:(dm + 1) * 128], H1[:, fc, g * 512:(g + 1) * 512],
                            start=(fc == 0), stop=(fc == 7)),
                            reads=(sb, b_h1[fc][g]), writes=(pbuf[bank],), inc=(fc == 7))
                    K.op(dve, lambda h, dm=dm, g=g, bank=bank, cond=cond: h.scalar_tensor_tensor(
                        out=xview(dm, g), in0=ps[:, bank, :], scalar=mcol(5, dm, cond), in1=xview(dm, g),
                        op0=ALU.mult, op1=ALU.add),
                        reads=(pbuf[bank], b_modd, xbuf(dm, g)), writes=(xbuf(dm, g),))
            if prefetch_mod is not None:
                mod_matmuls(prefetch_mod, range(3 * q, 3 * q + 3))

    def pool_layer(i):
        K.barrier()
        j = i // 2
        o = o_phase
        o_hf = o
        o += 8 * TS
        o_pa = o
        o += 2 * 272
        o_pb = o
        o += 2 * 272
        o_rowa = o
        o += 32 * 64
        o_rowb = o
        o += 32 * 64
        o_cola = o
        o += 8 * 80
        o_colb = o
        o += 8 * 80
        o_hal = o
        o += 4 * 8 * 64
        o_evt = o
        o += 512
        assert o <= MEMW, o
        HFS = f32v(o_hf, 8 * TS).rearrange("p (c r w) -> p c r w", c=8, r=16)
        HFPc = [HT[:, c, 0:1024].bitcast(F32).rearrange("p (s t) -> p s t", s=2) for c in range(8)]
        b_hfs = [Buf(f"hfs{c}") for c in range(8)]
        ROWA = f32v(o_rowa, 2048).rearrange("p (r w) -> p r w", w=64)
        ROWB = f32v(o_rowb, 2048).rearrange("p (r w) -> p r w", w=64)
        COLA = f32v(o_cola, 640).rearrange("p (r w) -> p r w", w=80)
        COLB = f32v(o_colb, 640).rearrange("p (r w) -> p r w", w=80)
        HAL = f32v(o_hal, 2048).rearrange("p (k r w) -> p k r w", k=4, w=64)
        PA = f32v(o_pa, 544).rearrange("p (s t) -> p s t", s=2)
        PB = f32v(o_pb, 544).rearrange("p (s t) -> p s t", s=2)
        EVT = f32v(o_evt, 512)
        b_rowa, b_rowb, b_cola, b_colb, b_hal, b_pa, b_pb, b_evt = (Buf(n) for n in
                                                                    ("rowa", "rowb", "cola", "colb", "hal", "pa", "pb", "evt"))
        b_exe, b_exg, b_exe2, b_exg2 = Buf("exe"), Buf("exg"), Buf("exe2"), Buf("exg2")
        s_ex = K.dsem()
        s_ex2 = K.dsem()
        s_hal = K.dsem()
        s_cc = K.dsem()
        STEPS = [(1, 0), (1, 1), (2, 2), (4, 4)]
        NLEV = {2: 1, 4: 2, 8: 3, 16: 4}
        icp = pcv("icp")
        icr = pcv("icr")
        icc = pcv("icc")
        pm = pcv("pm")

        def dbl_last(cur, cb, outs, w, length):
            lo, hi = 0, length
            for lev in range(NLEV[w]):
                sm, spp = STEPS[lev]
                nlo, nhi = lo + sm, hi - spp
                dst, db = outs[lev % 2]
                K.op(dve, lambda h, dst=dst, cur=cur, nlo=nlo, nhi=nhi, sm=sm, spp=spp: h.tensor_tensor(
                    out=dst[:, :, nlo:nhi], in0=cur[:, :, nlo - sm:nhi - sm], in1=cur[:, :, nlo + spp:nhi + spp], op=ALU.add),
                    reads=(cb,), writes=(db,))
                cur, cb, lo, hi = dst, db, nlo, nhi
            return cur, cb

        def dbl_mid(cur, cb, outs, w, length):
            lo, hi = 0, length
            for lev in range(NLEV[w]):
                sm, spp = STEPS[lev]
                nlo, nhi = lo + sm, hi - spp
                dst, db = outs[lev % 2]
                K.op(dve, lambda h, dst=dst, cur=cur, nlo=nlo, nhi=nhi, sm=sm, spp=spp: h.tensor_tensor(
                    out=dst[:, nlo:nhi, :], in0=cur[:, nlo - sm:nhi - sm, :], in1=cur[:, nlo + spp:nhi + spp, :], op=ALU.add),
                    reads=(cb,), writes=(db,))
                cur, cb, lo, hi = dst, db, nlo, nhi
            return cur, cb

        def prompt_chunk(c):
            grp = c // 2
            w = POOL_W[grp]
            K.op(dve, lambda h: h.memset(PA, 0.0), writes=(b_pa,))
            K.op(dve, lambda h: h.tensor_copy(out=PA[:, :, 8:264], in_=HFPc[c]), reads=(b_ht[c][0], b_ht[c][1]), writes=(b_pa,))
            cur, cb = dbl_last(PA, b_pa, [(PB, b_pb), (PA, b_pa)], w, 272)
            ev = EVT.rearrange("p (s t) -> p s t", s=2)
            K.op(dve, lambda h: h.tensor_tensor(
                out=ev, in0=cur[:, :, 8:264], in1=icp[:, grp * 256:(grp + 1) * 256].unsqueeze(1).broadcast_to([P, 2, 256]),
                op=ALU.mult), reads=(cb, b_pcv), writes=(b_evt,))
            K.op(dve, lambda h: h.tensor_tensor(
                out=HT[:, c, 1024:1536].rearrange("p (s t) -> p s t", s=2), in0=ev, in1=HFPc[c], op=ALU.subtract),
                reads=(b_evt, b_ht[c][0], b_ht[c][1]), writes=(b_ht[c][2],))

        def sample_chunk(c):
            grp = c // 2
            w = POOL_W[grp]
            ha, hb = w // 2, w // 2 - 1
            K.op(act, lambda h: h.activation(out=ROWA[:, 8:24, :], in_=HFS[:, c], func=AF.Copy),
                 reads=(b_hfs[c],), writes=(b_rowa,))
            K.dma(sp, HAL[:, :, 0:ha, :].rearrange("p k r w -> p k (r w)"), GA[:, :, A_OFF[c]:A_OFF[c] + ha * 64], s_hal,
                  reads=(b_exg,), writes=(b_hal,))
            K.op(dve, lambda h: h.tensor_scalar(out=ROWA[:, 8 - ha:8, :], in0=HAL[:, 0, 0:ha, :], scalar1=pm[:, 0:1], scalar2=None,
                                                op0=ALU.mult), reads=(b_hal, b_pcv), writes=(b_rowa,))
            for k in range(1, 4):
                K.op(dve, lambda h, k=k: h.scalar_tensor_tensor(
                    out=ROWA[:, 8 - ha:8, :], in0=HAL[:, k, 0:ha, :], scalar=pm[:, k:k + 1], in1=ROWA[:, 8 - ha:8, :],
                    op0=ALU.mult, op1=ALU.add), reads=(b_hal, b_pcv, b_rowa), writes=(b_rowa,))
            if hb > 0:
                K.dma(sp, HAL[:, :, 0:hb, :].rearrange("p k r w -> p k (r w)"), GB[:, :, B_OFF[c]:B_OFF[c] + hb * 64], s_hal,
                      reads=(b_exg2,), writes=(b_hal,))
                K.op(dve, lambda h: h.tensor_scalar(out=ROWA[:, 24:24 + hb, :], in0=HAL[:, 0, 0:hb, :], scalar1=pm[:, 4:5], scalar2=None,
                                                    op0=ALU.mult), reads=(b_hal, b_pcv), writes=(b_rowa,))
                for k in range(1, 4):
                    K.op(dve, lambda h, k=k: h.scalar_tensor_tensor(
                        out=ROWA[:, 24:24 + hb, :], in0=HAL[:, k, 0:hb, :], scalar=pm[:, 4 + k:5 + k], in1=ROWA[:, 24:24 + hb, :],
                        op0=ALU.mult, op1=ALU.add), reads=(b_hal, b_pcv, b_rowa), writes=(b_rowa,))
            cur, cb = dbl_mid(ROWA, b_rowa, [(ROWB, b_rowb), (ROWA, b_rowa)], w, 32)
            for g in range(2):
                K.op(dve, lambda h: h.memset(COLA, 0.0), writes=(b_cola,))
                K.op(dve, lambda h, g=g, cur=cur: h.tensor_tensor(
                    out=COLA[:, :, 8:72], in0=cur[:, 8 + g * 8:16 + g * 8, :],
                    in1=icr[:, grp * 16 + g * 8:grp * 16 + g * 8 + 8].unsqueeze(2).broadcast_to([P, 8, 64]), op=ALU.mult),
                    reads=(cb, b_pcv), writes=(b_cola,))
                c2, c2b = dbl_last(COLA, b_cola, [(COLB, b_colb), (COLA, b_cola)], w, 80)
                ev = EVT.rearrange("p (r w) -> p r w", w=64)
                K.op(dve, lambda h, c2=c2: h.tensor_tensor(
                    out=ev, in0=c2[:, :, 8:72],
                    in1=icc[:, grp * 64:(grp + 1) * 64].unsqueeze(1).broadcast_to([P, 8, 64]), op=ALU.mult),
                    reads=(c2b, b_pcv), writes=(b_evt,))
                K.op(dve, lambda h, g=g: h.tensor_tensor(
                    out=HT[:, c, g * 512:(g + 1) * 512].rearrange("p (r w) -> p r w", w=64), in0=ev,
                    in1=HFS[:, c, g * 8:(g + 1) * 8, :], op=ALU.subtract),
                    reads=(b_evt, b_hfs[c]), writes=(b_ht[c][g],))

        def linear_group(g):
            cond = 0 if g < 2 else 1
            for fo in range(8):
                grp = fo // 2
                bank = rot["pb"] % 4
                rot["pb"] += 1
                for kk in range(2):
                    fi = grp * 2 + kk
                    K.op(pe, lambda h, fi=fi, fo=fo, bank=bank, kk=kk: h.matmul(
                        ps[:, bank, :], PW[:, fi, (fo % 2) * 128:(fo % 2 + 1) * 128], htview(fi, g), start=(kk == 0), stop=(kk == 1)),
                        reads=(sbw, b_ht[fi][g]), writes=(pbuf[bank],), inc=(kk == 1))
                ti = rot["tmp"] % 2
                rot["tmp"] += 1
                K.op(act, lambda h, bank=bank, ti=ti, fo=fo: h.activation(
                    out=TMP[ti], in_=ps[:, bank, :], func=AF.Identity,
                    scale=POOLD[:, 0, fo, cond:cond + 1], bias=POOLD[:, 1, fo, cond:cond + 1]),
                    reads=(pbuf[bank], b_poold), writes=(b_tmp[ti],))
                K.op(dve, lambda h, ti=ti, fo=fo: h.tensor_tensor(out=xview(fo, g), in0=xview(fo, g), in1=TMP[ti], op=ALU.add),
                     reads=(b_tmp[ti], xbuf(fo, g)), writes=(xbuf(fo, g),))

        slot, sbw = ring_load([(0, 8, 256, poolw_d[j].rearrange("g (c p) n -> p (g c) n", p=P))])
        PW = slot[:, 0:2048].rearrange("p (c n) -> p c n", c=8)

        for g in range(2):
            norm_group(g, 0, 1, lambda c, g=g: (HFS[:, c, g * 8:(g + 1) * 8, :].rearrange("p r w -> p (r w)"), (b_hfs[c],)))
        EA, EB = ex_ea[i].ap(), ex_eb[i].ap()
        for c in range(8):
            ha, hb = HA[c], HB[c]
            K.dma(sp, EA[:, A_OFF[c]:A_OFF[c] + ha * 64], HFS[:, c, 16 - ha:16, :].rearrange("p r w -> p (r w)"), s_ex,
                  reads=(b_hfs[c],), writes=())
        b_exe.w = ("d", s_ex, s_ex.n)
        for c in range(8):
            ha, hb = HA[c], HB[c]
            if hb > 0:
                K.dma(sp, EB[:, B_OFF[c]:B_OFF[c] + hb * 64], HFS[:, c, 0:hb, :].rearrange("p r w -> p (r w)"), s_ex2,
                      reads=(b_hfs[c],), writes=())
        b_exe2.w = ("d", s_ex2, s_ex2.n)
        all_gather(ex_ea[i], ex_ga[i], b_exe, b_exg, s_cc)
        all_gather(ex_eb[i], ex_gb[i], b_exe2, b_exg2, s_cc)
        GA = ex_ga[i].ap().rearrange("(k p) n -> p k n", p=P)
        GB = ex_gb[i].ap().rearrange("(k p) n -> p k n", p=P)
        norm_group(2, 0, 1, lambda c: (HFPc[c].rearrange("p s t -> p (s t)"), (b_ht[c][0], b_ht[c][1])))
        for c in range(8):
            prompt_chunk(c)
        linear_group(2)
        K.op(dve, lambda h: h.memset(ROWA, 0.0), writes=(b_rowa,))
        K.op(dve, lambda h: h.memset(ROWB, 0.0), writes=(b_rowb,))
        for c in range(8):
            sample_chunk(c)
        for g in range(2):
            linear_group(g)

    def ret_layer(i):
        K.barrier()
        j = i // 2
        for g in range(3):
            norm_to_ht(g, 0, 1)
        oo = [o_phase]

        def al(n):
            r_ = oo[0]
            oo[0] += n
            assert oo[0] <= MEMW, oo[0]
            return r_

        o_rope = al(2 * 384)
        o_dec = al(384)
        o_cols = al(256)
        o_lg = al(16)
        o_idb = al(64)
        o_qt = al(1024)
        o_kt = al(1024)
        o_ktok = al(1024)
        o_v = al(2048)
        o_ra = al(256)
        o_rb = al(256)
        o_qtok = al(2 * 128)
        o_ks = al(2 * 128)
        o_qs = al(2 * 128)
        o_st = al(2 * 64)
        o_sball = al(4096)
        o_sfm = al(1024)
        o_sbm = al(1024)
        o_sfb = al(512)
        o_sin = al(2 * 1024)
        o_oh = al(512)
        o_sg = al(512)
        o_ogt = al(256)
        o_og = al(1024)
        o_gnw = al(512)
        o_stat = al(32)

        ROPE = [f32v(o_rope + r_ * 384, 384) for r_ in range(2)]
        b_rope = [Buf("rope0"), Buf("rope1")]
        s_rope = [K.dsem(), K.dsem()]
        DEC = f32v(o_dec, 384)
        Dm, XIF, XIB = DEC[:, 0:128], DEC[:, 128:256], DEC[:, 256:384]
        COLS4 = f32v(o_cols, 256).rearrange("p (h c) -> p h c", h=4)
        b_cols = Buf("cols")
        LG = f32v(o_lg, 16)
        IDB = bfv(o_idb, 64)
        QT = bfv(o_qt, 1024).rearrange("p (c t) -> p c t", c=2)
        KT = bfv(o_kt, 1024).rearrange("p (c t) -> p c t", c=2)
        KTOK = bfv(o_ktok, 1024).rearrange("p (t d) -> p t d", t=8)
        V = bfv(o_v, 2048).rearrange("p (t e) -> p t e", t=8)
        RA = f32v(o_ra, 256)
        RB = f32v(o_rb, 256)
        QTOK = [bfv(o_qtok + r_ * 128, 128) for r_ in range(2)]
        KS = [bfv(o_ks + r_ * 128, 128) for r_ in range(2)]
        QS = [bfv(o_qs + r_ * 128, 128).rearrange("p (c t) -> p c t", c=2) for r_ in range(2)]
        ST = [bfv(o_st + r_ * 64, 64) for r_ in range(2)]
        SBALL = bfv(o_sball, 4096).rearrange("p (t c e) -> p t c e", t=8, c=2)
        SFM = f32v(o_sfm, 1024).rearrange("p (c e) -> p c e", c=2)
        SBM = f32v(o_sbm, 1024).rearrange("p (c e) -> p c e", c=2)
        SFBS = [bfv(o_sfb, 512).rearrange("p (c e) -> p c e", c=2),
                bfv(o_tmp + 512, 512).rearrange("p (c e) -> p c e", c=2)]
        SIN = [f32v(o_sin + r_ * 1024, 1024).rearrange("p (c e) -> p c e", c=2) for r_ in range(2)]
        TSTG = f32v(o_sin, 2048).rearrange("p (a e) -> p a e", a=4)
        VT = [bfv(o_v + r_ * 256, 256) for r_ in range(2)]
        OH = f32v(o_oh, 512)
        SG = f32v(o_sg, 512)
        OGT = bfv(o_ogt, 256)
        OG = bfv(o_og, 1024).rearrange("p (e t) -> p e t", e=4)
        GNW = f32v(o_gnw, 512)
        STAT = f32v(o_stat, 32)
        TPB = ps[:, 3, :].bitcast(BF16)[:, 0:512].rearrange("p (a n) -> p a n", a=4)

        b_lg, b_dec, b_idb = Buf("lg"), Buf("dec"), Buf("idb")
        b_ra, b_rb = Buf("ra"), Buf("rb")
        b_qtok = [Buf("qtok0"), Buf("qtok1")]
        b_ks = [Buf("ks0"), Buf("ks1")]
        b_qs = [Buf("qs0"), Buf("qs1")]
        b_st = [Buf("st0"), Buf("st1")]
        b_sfm, b_sbm = Buf("sfm"), Buf("sbm")
        b_statc = Buf("statc")
        K.op(dve, lambda h: h.memset(STAT[:, 16:17], -0.5), writes=(b_statc,))
        b_sfbs = [Buf("sfb0"), b_tmp[1]]
        b_sin = [Buf("sin0"), Buf("sin1")]
        b_vt = [Buf("vt0"), Buf("vt1")]
        b_oh, b_sg, b_ogt, b_og, b_gnw, b_stat = Buf("oh"), Buf("sg"), Buf("ogt"), Buf("og"), Buf("gnw"), Buf("stat")
        b_exe = [Buf(f"exe{q_}") for q_ in range(4)]
        b_exg = [Buf(f"exg{q_}") for q_ in range(4)]
        s_sin = [K.dsem(), K.dsem()]
        s_ex, s_cc, s_gnw, s_ns = K.dsem(), K.dsem(), K.dsem(), K.dsem()
        LN16 = -2.772588722239781

        dec = pcv("decay")[:, j * 8:(j + 1) * 8]
        xw = pcv("xw")
        K.op(act, lambda h: h.activation(out=LG[:, 8:16], in_=dec, func=AF.Exp, scale=-0.6931471805599453),
             reads=(b_pcv,), writes=(b_lg,))
        K.op(act, lambda h: h.activation(out=LG[:, 0:8], in_=LG[:, 8:16], func=AF.Ln, scale=-1.0, bias=1.0),
             reads=(b_lg,), writes=(b_lg,))
        K.op(dve, lambda h: h.tensor_copy(out=IDB, in_=ident), reads=(b_cst,), writes=(b_idb,))

        def aexp(out, in_, lg, bias=0.0, rd=(), wr=()):
            K.op(act, lambda h: h.activation(out=out, in_=in_, func=AF.Exp, scale=lg, bias=bias),
                 reads=(b_lg, b_cst, b_pcv) + tuple(rd), writes=tuple(wr))

        def setup_cols(hh):
            COLS = COLS4[:, hh, :]
            lgf = LG[:, hh:hh + 1]
            lgb = LG[:, 4 + hh:5 + hh]
            aexp(COLS[:, 0:1], cst("colA"), lgf, bias=LN16, wr=(b_cols,))
            aexp(COLS[:, 1:2], cst("colB"), lgb, bias=LN16, wr=(b_cols,))
            aexp(COLS[:, 2:3], cst("xib")[:, 0:1], lgf, wr=(b_cols,))
            aexp(COLS[:, 3:4], cst("xib")[:, 0:1], lgb, wr=(b_cols,))
            aexp(COLS[:, 4:12], cst("colsF"), lgf, bias=LN16, wr=(b_cols,))
            aexp(COLS[:, 12:20], cst("colsB"), lgb, bias=LN16, wr=(b_cols,))
            aexp(COLS[:, 32:36], xw[:, 0:4], lgf, wr=(b_cols,))
            aexp(COLS[:, 36:40], xw[:, 8:12], lgb, wr=(b_cols,))
            aexp(COLS[:, 28:29], xw[:, 16:17], lgf, wr=(b_cols,))
            aexp(COLS[:, 29:30], xw[:, 17:18], lgb, wr=(b_cols,))
            K.op(dve, lambda h: h.tensor_tensor(out=COLS[:, 20:24], in0=COLS[:, 32:36], in1=xw[:, 4:8], op=ALU.mult),
                 reads=(b_cols, b_pcv), writes=(b_cols,))
            K.op(dve, lambda h: h.tensor_tensor(out=COLS[:, 24:28], in0=COLS[:, 36:40], in1=xw[:, 12:16], op=ALU.mult),
                 reads=(b_cols, b_pcv), writes=(b_cols,))

        def setup_head(hh):
            lgf = LG[:, hh:hh + 1]
            lgb = LG[:, 4 + hh:5 + hh]
            T1 = TMP[0][:, 0:128]
            T2 = TMP[1][:, 0:128]
            aexp(T1, cst("dpos"), lgf, wr=(b_tmp[0],))
            aexp(T2, cst("dneg"), lgb, wr=(b_tmp[1],))
            K.op(dve, lambda h: h.tensor_tensor(out=T1, in0=T1, in1=cst("mf"), op=ALU.mult), reads=(b_tmp[0], b_cst), writes=(b_tmp[0],))
            K.op(dve, lambda h: h.tensor_tensor(out=T2, in0=T2, in1=cst("mb"), op=ALU.mult), reads=(b_tmp[1], b_cst), writes=(b_tmp[1],))
            K.op(dve, lambda h: h.tensor_tensor(out=Dm, in0=T1, in1=T2, op=ALU.add), reads=(b_tmp[0], b_tmp[1]), writes=(b_dec,))
            aexp(XIF, cst("xif"), lgf, wr=(b_dec,))
            aexp(XIB, cst("xib"), lgb, wr=(b_dec,))

        def rope_apply(src_ps, pb_, r_, out_ap, out_bufs):
            rp = ROPE[r_]
            s4 = src_ps.rearrange("p (h x f) -> p h x f", h=2, x=2)
            cos4 = rp[:, 0:128].rearrange("p (h f) -> p h f", h=2).unsqueeze(2).broadcast_to([P, 2, 2, 64])
            sin3 = rp[:, 128:256].rearrange("p (h f) -> p h f", h=2)
            nsin3 = rp[:, 256:384].rearrange("p (h f) -> p h f", h=2)
            RA4 = RA.rearrange("p (h x f) -> p h x f", h=2, x=2)
            RB4 = RB.rearrange("p (h x f) -> p h x f", h=2, x=2)
            K.op(dve, lambda h: h.tensor_tensor(out=RA4, in0=s4, in1=cos4, op=ALU.mult),
                 reads=(pb_, b_rope[r_]), writes=(b_ra,))
            K.op(dve, lambda h: h.tensor_tensor(out=RB4[:, :, 0, :], in0=s4[:, :, 1, :], in1=nsin3, op=ALU.mult),
                 reads=(pb_, b_rope[r_]), writes=(b_rb,))
            K.op(dve, lambda h: h.tensor_tensor(out=RB4[:, :, 1, :], in0=s4[:, :, 0, :], in1=sin3, op=ALU.mult),
                 reads=(pb_, b_rope[r_]), writes=(b_rb,))
            K.op(dve, lambda h: h.tensor_tensor(out=out_ap, in0=RA, in1=RB, op=ALU.add),
                 reads=(b_ra, b_rb), writes=tuple(out_bufs))

        def proj(bank, ncol, W, sbw_, colsl, htb_, last_inc=True):
            for k in range(8):
                K.op(pe, lambda h, k=k: h.matmul(ps[:, bank, 0:ncol], HT[:, k, colsl], W[:, k, :], start=(k == 0), stop=(k == 7)),
                     reads=(sbw_, htb_[k]), writes=(pbuf[bank],), inc=(k == 7))

        def phase1_loads(hh):
            slotk, sbk = ring_load([(0, 8, 256, win_d[j, :, 1024 + hh * 256:1024 + (hh + 1) * 256].rearrange("(c p) n -> p c n", p=P))])
            WK = slotk[:, 0:2048].rearrange("p (c n) -> p c n", c=8)
            slotv, sbv = ring_load([(0, 8, 512, win_d[j, :, 2048 + hh * 512:2048 + (hh + 1) * 512].rearrange("(c p) n -> p c n", p=P))])
            WV = slotv.rearrange("p (c n) -> p c n", c=8)
            return WK, sbk, WV, sbv

        def phase1_head(hh, wts, dummy=False):
            COLS = COLS4[:, hh, :]
            WK, sbk, WV, sbv = wts

            def tile_proj(t):
                r_ = t % 2
                bk, bv = (0, 1) if t % 2 == 0 else (2, 3)
                colsl = slice(t * 128, (t + 1) * 128)
                htb_ = [b_ht[k][t // 4] for k in range(8)]
                K.dma(sp, ROPE[r_], rope_d[:, t, :], s_rope[r_], writes=(b_rope[r_],))
                proj(bk, 256, WK, sbk, colsl, htb_)
                proj(bv, 512, WV, sbv, colsl, htb_)

            def tile_post(t):
                r_ = t % 2
                bk, bv = (0, 1) if t % 2 == 0 else (2, 3)
                rope_apply(ps[:, bk, 0:256], pbuf[bk], r_, RA, (b_ra,))
                K.op(dve, lambda h: h.tensor_scalar(out=KS[0], in0=RA, scalar1=COLS[:, 4 + t:5 + t], scalar2=None, op0=ALU.mult),
                     reads=(b_ra, b_dec, b_cols), writes=(b_ks[0],))
                K.op(dve, lambda h: h.tensor_scalar(out=KS[1], in0=RA, scalar1=COLS[:, 12 + t:13 + t], scalar2=None, op0=ALU.mult),
                     reads=(b_ra, b_dec, b_cols), writes=(b_ks[1],))
                K.op(act, lambda h: h.activation(out=VT[r_], in_=ps[:, bv, :], func=AF.Copy), reads=(pbuf[bv],), writes=(b_vt[r_],))
                for d in range(2):
                    for dc in range(2):
                        K.op(pe, lambda h, d=d, dc=dc: h.matmul(ps[:, 4 + d * 2 + dc, :], KS[d][:, dc * 128:(dc + 1) * 128], VT[r_],
                                                              start=(t == 0), stop=(t == 7)),
                             reads=(b_ks[d], b_vt[r_]), writes=(pbuf[4 + d * 2 + dc],), inc=(d == 1 and dc == 1))

            tile_proj(0)
            for t in range(8):
                if t + 1 < 8:
                    tile_proj(t + 1)
                tile_post(t)
            for a in range(4):
                if a % 2 == 0:
                    K.op(act, lambda h, a=a: h.activation(out=TSTG[:, a, :], in_=ps[:, 4 + a, :], func=AF.Copy),
                         reads=(pbuf[4 + a],), writes=(b_sin[a // 2],))
                else:
                    K.op(dve, lambda h, a=a: h.tensor_copy(out=TSTG[:, a, :], in_=ps[:, 4 + a, :]),
                         reads=(pbuf[4 + a],), writes=(b_sin[a // 2],))
            K.dma(sp, ex_e[(i, hh)].ap(), TSTG.rearrange("p a e -> p (a e)"), s_ex,
                  reads=(b_sin[0], b_sin[1]), writes=(b_exe[hh],))

        RSTAGE = 99
        for hh in range(4):
            setup_cols(hh)
        wts = phase1_loads(0)
        for hh in range(4):
            phase1_head(hh, wts)
            if hh + 1 < 4:
                wts = phase1_loads(hh + 1)
                all_gather(ex_e[(i, hh)], ex_g[(i, hh)], b_exe[hh], b_exg[hh], s_cc)
        def ret_full(sample, hh, WQK, sbqk, WV, sbv, WG, sbg, WO, sbo):
            COLS = COLS4[:, hh, :]
            nt = 8 if sample else 4
            tok0 = 0 if sample else 1024
            cond = 0 if sample else 1
            seqs = [list(range(8))] if sample else [[0, 1], [2, 3]]
            b_qt = [Buf(f"qt{t}") for t in range(nt)]
            b_kt = [Buf(f"kt{t}") for t in range(nt)]
            b_ktok = [Buf(f"ktok{t}") for t in range(nt)]
            b_v = [Buf(f"v{t}") for t in range(nt)]
            b_sball = [Buf(f"sball{t}") for t in range(nt)]

            def htbs(t):
                return [b_ht[k][(tok0 + t * 128) // 512] for k in range(8)]

            SBP = [SBM, f32v(o_oh, 1024).rearrange("p (c e) -> p c e", c=2)]
            b_sbp = [(b_sbm,), (b_oh, b_sg)]
            SFP = [SFM, SBM]
            b_sfp = [(b_sfm,), (b_sbm,)]

            def stepA_proj(t):
                r_ = t % 2
                bq, bv = (0, 1) if t % 2 == 0 else (6, 7)
                colsl = slice(tok0 + t * 128, tok0 + (t + 1) * 128)
                if sample:
                    K.dma(sp, ROPE[r_], rope_d[:, t, :], s_rope[r_], writes=(b_rope[r_],))
                proj(bq, 512, WQK, sbqk, colsl, htbs(t))
                proj(bv, 512, WV, sbv, colsl, htbs(t))

            def stepA_post(t):
                r_ = t % 2
                bq, bv = (0, 1) if t % 2 == 0 else (6, 7)
                tc = slice(t * 128, (t + 1) * 128)
                if sample:
                    rope_apply(ps[:, bq, 0:256], pbuf[bq], r_, QTOK[r_], (b_qtok[r_],))
                    rope_apply(ps[:, bq, 256:512], pbuf[bq], r_, KTOK[:, t, :], (b_ktok[t],))
                else:
                    K.op(act, lambda h: h.activation(out=QTOK[r_], in_=ps[:, bq, 0:256], func=AF.Copy),
                         reads=(pbuf[bq],), writes=(b_qtok[r_],))
                    K.op(dve, lambda h: h.tensor_copy(out=KTOK[:, t, :], in_=ps[:, bq, 256:512]),
                         reads=(pbuf[bq],), writes=(b_ktok[t],))
                K.op(act, lambda h: h.activation(out=V[:, t, :], in_=ps[:, bv, :], func=AF.Copy), reads=(pbuf[bv],), writes=(b_v[t],))
                for dc in range(2):
                    K.op(pe, lambda h, dc=dc: h.transpose(out=TPB[:, dc, :], in_=QTOK[r_][:, dc * 128:(dc + 1) * 128], identity=IDB),
                         reads=(b_qtok[r_], b_idb), writes=(pbuf[3],), inc=False)
                for dc in range(2):
                    K.op(pe, lambda h, dc=dc: h.transpose(out=TPB[:, 2 + dc, :], in_=KTOK[:, t, dc * 128:(dc + 1) * 128], identity=IDB),
                         reads=(b_ktok[t], b_idb), writes=(pbuf[3],), inc=(dc == 1))
                K.op(act, lambda h: h.activation(out=QT[:, :, tc], in_=TPB[:, 0:2, :], func=AF.Copy), reads=(pbuf[3],), writes=(b_qt[t],))
                K.op(dve, lambda h: h.tensor_copy(out=KT[:, :, tc], in_=TPB[:, 2:4, :]), reads=(pbuf[3],), writes=(b_kt[t],))

            def s_in(d, SM, b_sm):
                K.dma(sp, SIN[0], s0_d[j, d, hh].rearrange("(c p) e -> p c e", p=P), s_sin[0], writes=(b_sin[0],))
                K.op(dve, lambda h: h.tensor_scalar(out=SM, in0=SIN[0], scalar1=COLS[:, 28 + d:29 + d], scalar2=None, op0=ALU.mult),
                     reads=(b_sin[0], b_dec, b_cols), writes=(b_sm,))
                yield
                for r4 in (range(0, 3) if d == 0 else range(1, 4)):
                    bi = (r4 + 1) % 2
                    srcg = ex_g[(i, hh)].ap()[r4 * 128:(r4 + 1) * 128, d * 1024:(d + 1) * 1024].rearrange(
                        "p (c e) -> p c e", c=2)
                    K.dma(sp, SIN[bi], srcg, s_sin[bi], reads=(b_exg[hh],), writes=(b_sin[bi],))
                    if os.environ.get("DEBUG_SB") and hh == 0 and d == 1:
                        K.dma(sp, ns_d[0, 1, 0, r4].rearrange("(c p) e -> p c e", p=P), SIN[bi], s_ns, reads=(b_sin[bi],))
                    K.op(dve, lambda h, bi=bi, r4=r4: h.scalar_tensor_tensor(
                        out=SM, in0=SIN[bi], scalar=COLS[:, 20 + d * 4 + r4:21 + d * 4 + r4], in1=SM, op0=ALU.mult, op1=ALU.add),
                        reads=(b_sin[bi], b_dec, b_cols, b_sm), writes=(b_sm,))
                    yield

            def wout(g):
                for dm in range(8):
                    bank = 7 if dm % 2 == 0 else 0
                    for e in range(4):
                        K.op(pe, lambda h, dm=dm, e=e, bank=bank: h.matmul(ps[:, bank, :], WO[:, e, dm * 128:(dm + 1) * 128], OG[:, e, :],
                                                                            start=(e == 0), stop=(e == 3)),
                             reads=(sbo, b_og), writes=(pbuf[bank],), inc=(e == 3))
                    K.op(dve, lambda h, dm=dm, bank=bank: h.scalar_tensor_tensor(
                        out=xview(dm, g), in0=ps[:, bank, :], scalar=mcol(2, dm, cond), in1=xview(dm, g), op0=ALU.mult, op1=ALU.add),
                        reads=(pbuf[bank], b_modd, xbuf(dm, g)), writes=(xbuf(dm, g),))

            def bwd(seq, si):
                if not sample:
                    K.op(dve, lambda h: h.memset(SBP[0], 0.0), writes=b_sbp[0])
                order = list(reversed(seq))

                def ksu(c, ub):
                    K.op(dve, lambda h: h.tensor_scalar(out=KS[1], in0=KTOK[:, c, :], scalar1=COLS[:, 1:2], scalar2=None, op0=ALU.mult),
                         reads=(b_ktok[c], b_dec, b_cols), writes=(b_ks[1],))
                    for dc in range(2):
                        K.op(pe, lambda h, dc=dc: h.matmul(ps[:, ub + dc, :], KS[1][:, dc * 128:(dc + 1) * 128], V[:, c, :],
                                                           start=True, stop=True),
                             reads=(b_ks[1], b_v[c]), writes=(pbuf[ub + dc],), inc=(dc == 1))

                def upd(c, ub, p):
                    K.op(act, lambda h: h.activation(out=SBALL[:, c], in_=SBP[p], func=AF.Copy), reads=b_sbp[p], writes=(b_sball[c],))
                    K.op(dve, lambda h: h.scalar_tensor_tensor(out=SBP[1 - p], in0=SBP[p], scalar=COLS[:, 3:4], in1=ps[:, ub:ub + 2, :],
                                                               op0=ALU.mult, op1=ALU.add),
                         reads=b_sbp[p] + (b_dec, b_cols, pbuf[ub], pbuf[ub + 1]), writes=b_sbp[1 - p])

                ksu(order[0], 4)
                for k_, c in enumerate(order):
                    if k_ + 1 < len(order):
                        ksu(order[k_ + 1], 4 if (k_ + 1) % 2 == 0 else 6)
                    upd(c, 4 if k_ % 2 == 0 else 6, k_ % 2)
                pf = len(order) % 2
                if not sample:
                    K.dma(sp, ns_d[si, j, 1, hh].rearrange("(c p) e -> p c e", p=P), SBP[pf], s_ns, reads=b_sbp[pf])

            def fwd(seq, si):
                if not sample:
                    K.op(dve, lambda h: h.memset(SFP[0], 0.0), writes=b_sfp[0])
                K.op(act, lambda h: h.activation(out=SFBS[0], in_=SFP[0], func=AF.Copy), reads=b_sfp[0], writes=(b_sfbs[0],))
                prev = None
                for k_, c in enumerate(seq):
                    fwd_head(c, k_)
                    if prev is not None:
                        fwd_tail(*prev)
                    prev = (c, k_)
                fwd_tail(*prev)
                pf = len(seq) % 2
                if not sample:
                    K.dma(sp, ns_d[si, j, 0, hh].rearrange("(c p) e -> p c e", p=P), SFP[pf], s_ns, reads=b_sfp[pf])

            def fwd_head(c, k_):
                r_ = c % 2
                ob = 1 if k_ % 2 == 0 else 6
                sfb_r, sfb_w = SFBS[k_ % 2], SFBS[(k_ + 1) % 2]
                bs_r, bs_w = b_sfbs[k_ % 2], b_sfbs[(k_ + 1) % 2]
                tc = slice(c * 128, (c + 1) * 128)
                K.op(dve, lambda h: h.tensor_scalar(out=KS[0], in0=KTOK[:, c, :], scalar1=COLS[:, 0:1], scalar2=None, op0=ALU.mult),
                     reads=(b_ktok[c], b_dec, b_cols), writes=(b_ks[0],))
                for dc in range(2):
                    K.op(pe, lambda h, dc=dc: h.matmul(ps[:, 2, 0:128], KT[:, dc, tc], QT[:, dc, tc], start=(dc == 0), stop=(dc == 1)),
                         reads=(b_kt[c], b_qt[c]), writes=(pbuf[2],), inc=(dc == 1))
                for dc in range(2):
                    K.op(pe, lambda h, dc=dc: h.matmul(ps[:, 4 + dc, :], KS[0][:, dc * 128:(dc + 1) * 128], V[:, c, :], start=True, stop=True),
                         reads=(b_ks[0], b_v[c]), writes=(pbuf[4 + dc],), inc=(dc == 1))
                K.op(dve, lambda h: h.tensor_tensor(out=QS[0], in0=QT[:, :, tc], in1=XIF.unsqueeze(1).broadcast_to([P, 2, 128]), op=ALU.mult),
                     reads=(b_qt[c], b_dec), writes=(b_qs[0],))
                K.op(dve, lambda h: h.tensor_tensor(out=QS[1], in0=QT[:, :, tc], in1=XIB.unsqueeze(1).broadcast_to([P, 2, 128]), op=ALU.mult),
                     reads=(b_qt[c], b_dec), writes=(b_qs[1],))
                K.op(dve, lambda h: h.tensor_tensor(out=ST[r_], in0=ps[:, 2, 0:128], in1=Dm, op=ALU.mult),
                     reads=(pbuf[2], b_dec), writes=(b_st[r_],))
                K.op(pe, lambda h: h.matmul(ps[:, ob, :], ST[r_], V[:, c, :], start=True, stop=False),
                     reads=(b_st[r_], b_v[c]), writes=(pbuf[ob],), inc=False)
                for dc in range(2):
                    K.op(pe, lambda h, dc=dc: h.matmul(ps[:, ob, :], QS[1][:, dc, :], SBALL[:, c, dc, :], start=False, stop=False),
                         reads=(b_qs[1], b_sball[c]), writes=(pbuf[ob],), inc=False)
                for dc in range(2):
                    K.op(pe, lambda h, dc=dc: h.matmul(ps[:, ob, :], QS[0][:, dc, :], sfb_r[:, dc, :], start=False, stop=(dc == 1)),
                         reads=(b_qs[0], bs_r), writes=(pbuf[ob],), inc=(dc == 1))
                p_ = k_ % 2
                K.op(dve, lambda h: h.scalar_tensor_tensor(out=SFP[1 - p_], in0=SFP[p_], scalar=COLS[:, 2:3], in1=ps[:, 4:6, :],
                                                           op0=ALU.mult, op1=ALU.add),
                     reads=b_sfp[p_] + (b_dec, b_cols, pbuf[4], pbuf[5]), writes=b_sfp[1 - p_])
                K.op(act, lambda h: h.activation(out=sfb_w, in_=SFP[1 - p_], func=AF.Copy), reads=b_sfp[1 - p_], writes=(bs_w,))

            def fwd_tail(c, k_):
                ob = 1 if k_ % 2 == 0 else 6
                colsl = slice(tok0 + c * 128, tok0 + (c + 1) * 128)
                K.op(dve, lambda h: h.bn_stats(out=STAT[:, 0:6], in_=ps[:, ob, :]), reads=(pbuf[ob],), writes=(b_stat,))
                K.op(dve, lambda h: h.bn_aggr(out=STAT[:, 8:10], in_=STAT[:, 0:6]), reads=(b_stat,), writes=(b_stat,))
                K.op(dve, lambda h: h.tensor_scalar(out=STAT[:, 10:11], in0=STAT[:, 9:10], scalar1=GN_EPS, scalar2=None, op0=ALU.add),
                     reads=(b_stat,), writes=(b_stat,))
                K.op(pool, lambda h: h.tensor_tensor(out=STAT[:, 11:12], in0=STAT[:, 10:11], in1=STAT[:, 16:17], op=ALU.pow),
                     reads=(b_stat, b_statc), writes=(b_stat,))
                K.op(dve, lambda h: h.tensor_scalar(out=STAT[:, 12:13], in0=STAT[:, 8:9], scalar1=STAT[:, 11:12], scalar2=-1.0,
                                                    op0=ALU.mult, op1=ALU.mult), reads=(b_stat,), writes=(b_stat,))
                K.op(act, lambda h: h.activation(out=OH, in_=ps[:, ob, :], func=AF.Identity, scale=STAT[:, 11:12], bias=STAT[:, 12:13]),
                     reads=(pbuf[ob], b_stat), writes=(b_oh,))
                proj(0, 512, WG, sbg, colsl, htbs(c))
                K.op(act, lambda h: h.activation(out=SG, in_=ps[:, 0, :], func=AF.Silu), reads=(pbuf[0],), writes=(b_sg,))
                K.op(dve, lambda h: h.tensor_tensor(out=SG, in0=SG, in1=GNW, op=ALU.mult), reads=(b_sg, b_gnw), writes=(b_sg,))
                K.op(dve, lambda h: h.tensor_tensor(out=OGT, in0=OH, in1=SG, op=ALU.mult), reads=(b_oh, b_sg), writes=(b_ogt,))
                for e in range(4):
                    K.op(pe, lambda h, e=e: h.transpose(out=TPB[:, e, :], in_=OGT[:, e * 128:(e + 1) * 128], identity=IDB),
                         reads=(b_ogt, b_idb), writes=(pbuf[3],), inc=(e == 3))
                oc = slice((c % 4) * 128, (c % 4 + 1) * 128)
                K.op(act, lambda h: h.activation(out=OG[:, :, oc], in_=TPB, func=AF.Copy), reads=(pbuf[3],), writes=(b_og,))
                if c % 4 == 3:
                    wout((c // 4) if sample else 2)

            import itertools
            sin_steps = itertools.chain(s_in(1, SBM, b_sbm), s_in(0, SFM, b_sfm)) if sample else iter(())
            stepA_proj(0)
            for t in range(nt):
                if t + 1 < nt:
                    stepA_proj(t + 1)
                stepA_post(t)
                next(sin_steps, None)
            for _ in sin_steps:
                pass
            for si, seq in enumerate(seqs):
                bwd(seq, si)
            for si, seq in enumerate(seqs):
                fwd(seq, si)

        def p2_load_qk(hh):
            i_slot = ring_state["next"]
            slotqk = bfv(o_ring + i_slot * 2048, 2048)
            WQK = slotqk.rearrange("p (c n) -> p c n", c=8)
            ring_state["next"] = (i_slot + 1) % NSLOT
            K.dma(pool, WQK[:, :, 0:256], win_d[j, :, hh * 256:(hh + 1) * 256].rearrange("(c p) n -> p c n", p=P),
                  ring_sems[i_slot], writes=(ring_bufs[i_slot],))
            K.dma(pool, WQK[:, :, 256:512], win_d[j, :, 1024 + hh * 256:1024 + (hh + 1) * 256].rearrange("(c p) n -> p c n", p=P),
                  ring_sems[i_slot], writes=())
            ring_bufs[i_slot].w = ("d", ring_sems[i_slot], ring_sems[i_slot].n)
            return WQK, ring_bufs[i_slot]

        def p2_load_v(hh):
            slotv, sbv = ring_load([(0, 8, 512, win_d[j, :, 2048 + hh * 512:2048 + (hh + 1) * 512].rearrange("(c p) n -> p c n", p=P))])
            return slotv.rearrange("p (c n) -> p c n", c=8), sbv

        def p2_load_g(hh):
            slotg, sbg = ring_load([(0, 8, 512, win_d[j, :, 4096 + hh * 512:4096 + (hh + 1) * 512].rearrange("(c p) n -> p c n", p=P))])
            return slotg.rearrange("p (c n) -> p c n", c=8), sbg

        def p2_load_o(hh):
            sloto, sbo = ring_load([(0, 4, 1024, wout_d[j, hh * 512:(hh + 1) * 512, :].rearrange("(c p) n -> p c n", p=P))])
            return sloto.rearrange("p (c n) -> p c n", c=4), sbo

        WQK, sbqk = p2_load_qk(0)
        WV, sbv = p2_load_v(0)
        WG, sbg = p2_load_g(0)
        all_gather(ex_e[(i, 3)], ex_g[(i, 3)], b_exe[3], b_exg[3], s_cc)
        WO, sbo = p2_load_o(0)
        for hh in range(4):
            if hh > 0:
                WQK, sbqk = p2_load_qk(hh)
                WV, sbv = p2_load_v(hh)
                WG, sbg = p2_load_g(hh)
                WO, sbo = p2_load_o(hh)
            setup_head(hh)
            K.dma(sp, GNW, gnw_d[:, j, hh, :], s_gnw, writes=(b_gnw,))
            ret_full(False, hh, WQK, sbqk, WV, sbv, WG, sbg, WO, sbo)
            ret_full(True, hh, WQK, sbqk, WV, sbv, WG, sbg, WO, sbo)

    sub = 0
    mod_prefetched = False
    for i in range(DEPTH):
        if sub < nsub or sub + 1 < nsub:
            compute_mod(i, do_matmuls=not mod_prefetched)
        mod_prefetched = False
        if sub < nsub:
            if i % 2 == 0:
                pool_layer(i)
            else:
                ret_layer(i)
        sub += 1
        if sub < nsub:
            nxt = i + 1 if (i + 1 < DEPTH and sub + 1 < nsub) else None
            mlp_layer(i, prefetch_mod=nxt)
            mod_prefetched = nxt is not None
        sub += 1

    K.barrier()
    fnw = pcv("fnw")
    o_yt = o_phase + 2048
    s_out = [K.dsem(), K.dsem()]
    out_toks = []
    YT = f32v(o_yt, 8 * 512).rearrange("p (c t) -> p c t", c=8)
    b_yt = [Buf(f"yt{c}") for c in range(8)]
    def final_group(g):
        bank = 6
        for c in range(8):
            si = rot["sq"] % 2
            rot["sq"] += 1
            K.op(act, lambda h, c=c, si=si: h.activation(out=SQ[si], in_=xview(c, g), func=AF.Square),
                 reads=(xbuf(c, g),), writes=(b_sq[si],))
            K.op(pe, lambda h, c=c, si=si: h.matmul(ps[:, bank, :], ONES, SQ[si], start=(c == 0), stop=(c == 7)),
                 reads=(b_sq[si], b_ones), writes=(pbuf[bank],), inc=True)
        ti = rot["tmp"] % 2
        rot["tmp"] += 1
        K.op(act, lambda h, ti=ti: h.activation(out=TMP[ti], in_=ps[:, bank, :], func=AF.Sqrt, bias=NORM_EPS, scale=1.0 / 1024.0),
             reads=(pbuf[bank],), writes=(b_tmp[ti],))
        K.op(dve, lambda h, ti=ti: h.reciprocal(out=RSTD, in_=TMP[ti]), reads=(b_tmp[ti],), writes=(b_rstd,))
        for c in range(8):
            K.op(dve, lambda h, c=c: h.scalar_tensor_tensor(out=YT[:, c, :], in0=xview(c, g), scalar=fnw[:, c:c + 1], in1=RSTD,
                                                             op0=ALU.mult, op1=ALU.mult),
                 reads=(xbuf(c, g), b_rstd, b_pcv), writes=(b_yt[c],))
        for tt in range(4):
            t = g * 4 + tt
            i2 = t % 2
            for half in range(2):
                bank2 = 2 * i2 + half
                for cc in range(4):
                    c = half * 4 + cc
                    K.op(pe, lambda h, c=c, cc=cc, bank2=bank2, tt=tt: h.transpose(
                        out=ps[:, bank2, cc * 128:(cc + 1) * 128], in_=YT[:, c, tt * 128:(tt + 1) * 128], identity=ident),
                        reads=(b_yt[c], b_cst), writes=(pbuf[bank2],), inc=(cc == 3))
                if half == 0:
                    K.op(act, lambda h, i2=i2, bank2=bank2: h.activation(out=IO[i2][:, 0:512], in_=ps[:, bank2, :], func=AF.Copy),
                         reads=(pbuf[bank2],), writes=(b_io[i2],))
                else:
                    K.op(dve, lambda h, i2=i2, bank2=bank2: h.tensor_copy(out=IO[i2][:, 512:1024], in_=ps[:, bank2, :]),
                         reads=(pbuf[bank2],), writes=(b_io[i2],))
            dst = ys_d[t * 128:(t + 1) * 128, :] if t < 8 else yp_d[(t - 8) * 128:(t - 7) * 128, :]
            out_toks.append(K.dma(sp, dst, IO[i2], s_out[i2], reads=(b_io[i2],)))
    for g in range(3):
        final_group(g)
    K.wait_all(sp, out_toks + [("d", ds, ds.n) for ds in K.live_dsems.values()])

    print("dry run:", K.dry_run(), file=sys.stderr)
    with nc.Block() as block:
        @block.tensor
        def _(h):
            for f in pe.prog:
                f(h)

        @block.scalar
        def _(h):
            for f in act.prog:
                f(h)

        @block.vector
        def _(h):
            for f in dve.prog:
                f(h)

        @block.gpsimd
        def _(h):
            for f in pool.prog:
                f(h)

        @block.sync
        def _(h):
            for f in sp.prog:
                f(h)
    es.close()
    return nc


_NC_CACHE = {}


def kernel(nsub=None, **inputs):
    if nsub is None:
        nsub = int(os.environ.get("KNSUB", str(2 * DEPTH)))
    inp = {k: np.asarray(v) for k, v in inputs.items()}
    if nsub not in _NC_CACHE:
        _NC_CACHE[nsub] = build_program(nsub)
    nc = _NC_CACHE[nsub]
    cst = _make_cst()
    gnw = np.ascontiguousarray(np.broadcast_to(np.asarray(inp["ret_gn_w"], np.float32)[None], (P, 2, 4, 512)))
    in_maps = []
    for core in range(8):
        b, q = core // 4, core % 4
        m = {
            "xs": np.ascontiguousarray(inp["x_sample"][b, q * TS:(q + 1) * TS]),
            "xp": np.ascontiguousarray(inp["x_prompt"][2 * core:2 * core + 2].reshape(TP, 1024)),
            "s0": np.ascontiguousarray(inp["state_ret"][b]),
            "cst": cst,
            "pcv": _make_pcv(core, inp),
            "rope": _make_rope(core),
            "gnw": gnw,
            "w_ada": inp["w_ada"], "pool_w": inp["pool_w"], "ret_w_in": inp["ret_w_in"],
            "ret_w_out": inp["ret_w_out"], "mlp_w1": inp["mlp_w1"], "mlp_w2": inp["mlp_w2"],
        }
        in_maps.append(m)
    res = run_bass_kernel_spmd(nc, in_maps, core_ids=list(range(8)))
    y_prompt = np.zeros((16, 256, 1024), np.float32)
    y_sample = np.zeros((2, 4096, 1024), np.float32)
    new_state = np.zeros((16, 2, 2, 4, 256, 512), np.float32)
    for core in range(8):
        b, q = core // 4, core % 4
        r = res.results[core]
        y_sample[b, q * TS:(q + 1) * TS] = r["ys"]
        y_prompt[2 * core:2 * core + 2] = np.asarray(r["yp"]).reshape(2, 256, 1024)
        new_state[2 * core:2 * core + 2] = r["ns"]
    return (y_prompt, y_sample, new_state)
```
